# Optimizing a Trainium2 kernel written in Bass

```python
import math
import jax, jax.numpy as jnp
from jax import lax
import numpy as np

D_MODEL = 1024
BATCH = 16
SEQ = 2048
DEPTH = 1

GRID_W = 64
MEM_LEN = 256
BRANCH_W = D_MODEL // 2
N_BRANCH = 3
DN_HEADS = 4
DN_HEAD_DIM = BRANCH_W // DN_HEADS
DN_CHUNK = 64
CONV_W = 5
CONV_PAD = CONV_W // 2
ATT_HEADS = 8
ATT_KV_HEADS = 2
ATT_GROUP = ATT_HEADS // ATT_KV_HEADS
ATT_HEAD_DIM = BRANCH_W // ATT_HEADS
ROPE_AXIS_DIM = ATT_HEAD_DIM // 2
ROPE_PAIRS = ROPE_AXIS_DIM // 2
ROPE_THETA = 10000.0
Q_BLOCK = 128
X_HEADS = 4
X_HEAD_DIM = BRANCH_W // X_HEADS
D_FF = 4 * D_MODEL
EPS = 1e-6

DN_QKV_COLS = 3 * BRANCH_W
DN_Z_COLS = BRANCH_W
DN_BETA_COLS = 2 * DN_HEADS
DN_ALPHA_COLS = 2 * DN_HEADS
ATT_Q_COLS = ATT_HEADS * ATT_HEAD_DIM
ATT_KV_COLS = ATT_KV_HEADS * ATT_HEAD_DIM
X_Q_COLS = X_HEADS * X_HEAD_DIM
GATE_COLS = N_BRANCH * D_MODEL
IN_SPLITS = (DN_QKV_COLS, DN_Z_COLS, DN_BETA_COLS, DN_ALPHA_COLS, ATT_Q_COLS, ATT_KV_COLS, ATT_KV_COLS, X_Q_COLS, GATE_COLS)
IN_COLS = sum(IN_SPLITS)
IN_OFFSETS = tuple(int(o) for o in np.cumsum(IN_SPLITS)[:-1])

kernel_name = "hybrid_gdn_axial_gqa_memxattn_block"


def rms_norm(x, w):
    xf = x.astype(jnp.float32)
    y = xf * lax.rsqrt(jnp.mean(xf * xf, axis=-1, keepdims=True) + EPS)
    return (y * w.astype(jnp.float32)).astype(x.dtype)


def l2_normalize(x):
    xf = x.astype(jnp.float32)
    return xf * lax.rsqrt(jnp.sum(xf * xf, axis=-1, keepdims=True) + EPS)


def centred_depthwise_conv(x, w):
    c = x.shape[-1]
    return lax.conv_general_dilated(
        x, w[:, None, :].astype(x.dtype), window_strides=(1,), padding=[(CONV_PAD, CONV_PAD)],
        dimension_numbers=("NWC", "WIO", "NWC"), feature_group_count=c)


def gated_delta_rule(q, k, v, g, beta):
    out_dtype = v.dtype
    q, k, v, g, beta = (t.astype(jnp.float32) for t in (q, k, v, g, beta))
    b_, s_, h_, dk = q.shape
    dv = v.shape[-1]
    n_ = s_ // DN_CHUNK
    q = q * (dk ** -0.5)

    def chunks(t):
        return t.reshape(b_, n_, DN_CHUNK, h_, t.shape[-1]).transpose(0, 3, 1, 2, 4)

    qc, kc, vc = chunks(q), chunks(k), chunks(v)
    bc = beta.reshape(b_, n_, DN_CHUNK, h_).transpose(0, 3, 1, 2)
    gc = jnp.cumsum(g.reshape(b_, n_, DN_CHUNK, h_).transpose(0, 3, 1, 2), axis=-1)
    idx = jnp.arange(DN_CHUNK)
    lower_incl = idx[:, None] >= idx[None, :]
    strict = idx[:, None] > idx[None, :]
    decay = jnp.exp(jnp.where(lower_incl, gc[..., :, None] - gc[..., None, :], -jnp.inf))
    kb = kc * bc[..., None]
    vb = vc * bc[..., None]
    lmat = jnp.where(strict, jnp.einsum("bhncd,bhnsd->bhncs", kb, kc) * decay, 0.0)
    rhs = jnp.concatenate([vb, kb * jnp.exp(gc)[..., None]], axis=-1)
    sol = lax.linalg.triangular_solve(lmat, rhs, left_side=True, lower=True, unit_diagonal=True)
    u, w = sol[..., :dv], sol[..., dv:]
    attn_intra = jnp.einsum("bhncd,bhnsd->bhncs", qc, kc) * decay
    q_dec = qc * jnp.exp(gc)[..., None]
    g_last = gc[..., -1]
    k_dec = kc * jnp.exp(g_last[..., None] - gc)[..., None]

    def step(state, inp):
        u_i, w_i, q_i, k_i, a_i, gl = inp
        v_new = u_i - jnp.einsum("bhcd,bhdv->bhcv", w_i, state)
        o = jnp.einsum("bhcd,bhdv->bhcv", q_i, state) + jnp.einsum("bhcs,bhsv->bhcv", a_i, v_new)
        state = state * jnp.exp(gl)[..., None, None] + jnp.einsum("bhcd,bhcv->bhdv", k_i, v_new)
        return state, o

    xs = (jnp.moveaxis(u, 2, 0), jnp.moveaxis(w, 2, 0), jnp.moveaxis(q_dec, 2, 0),
          jnp.moveaxis(k_dec, 2, 0), jnp.moveaxis(attn_intra, 2, 0), jnp.moveaxis(g_last, 2, 0))
    state0 = jnp.zeros((b_, h_, dk, dv), jnp.float32)
    _, o = lax.scan(step, state0, xs)
    return o.transpose(1, 0, 3, 2, 4).reshape(b_, s_, h_, dv).astype(out_dtype)


def axial_rope_tables(row_ids, col_ids):
    inv_freq = ROPE_THETA ** (-jnp.arange(ROPE_PAIRS, dtype=jnp.float32) / ROPE_PAIRS)
    ang = jnp.stack([row_ids.astype(jnp.float32)[:, None] * inv_freq,
                     col_ids.astype(jnp.float32)[:, None] * inv_freq], axis=1)
    return jnp.cos(ang), jnp.sin(ang)


def apply_axial_rope(x, cos, sin):
    xs = x.astype(jnp.float32).reshape(*x.shape[:-1], 2, 2, ROPE_PAIRS)
    x1, x2 = xs[..., 0, :], xs[..., 1, :]
    c = cos[None, :, None]
    s = sin[None, :, None]
    out = jnp.stack([x1 * c - x2 * s, x2 * c + x1 * s], axis=-2)
    return out.reshape(x.shape).astype(x.dtype)


def block_gqa(q, k, v):
    b_, s_ = q.shape[:2]
    nb = s_ // Q_BLOCK
    qb = q.reshape(b_, nb, Q_BLOCK, ATT_KV_HEADS, ATT_GROUP, ATT_HEAD_DIM).transpose(1, 0, 2, 3, 4, 5)
    scale = ATT_HEAD_DIM ** -0.5

    def one_block(qi):
        sc = jnp.einsum("bqkgd,bskd->bkgqs", qi, k).astype(jnp.float32) * scale
        p = jax.nn.softmax(sc, axis=-1).astype(v.dtype)
        return jnp.einsum("bkgqs,bskd->bqkgd", p, v)

    o = lax.map(one_block, qb)
    return o.transpose(1, 0, 2, 3, 4, 5).reshape(b_, s_, ATT_HEADS * ATT_HEAD_DIM)


def cross_attention(q, mk, mv):
    sc = jnp.einsum("bshd,bmhd->bhsm", q, mk).astype(jnp.float32) * (X_HEAD_DIM ** -0.5)
    p = jax.nn.softmax(sc, axis=-1).astype(mv.dtype)
    o = jnp.einsum("bhsm,bmhd->bshd", p, mv)
    return o.reshape(q.shape[0], q.shape[1], X_HEADS * X_HEAD_DIM)


def setup_inputs(seed: int = 0) -> dict:
    key = jax.random.key(seed)
    ks = jax.random.split(key, 20)
    f32 = jnp.float32

    def normal(k, shape, scale):
        return jax.random.normal(k, shape, f32) * scale

    def gain(k, shape):
        return 1.0 + 0.02 * jax.random.normal(k, shape, f32)

    x = normal(ks[0], (BATCH, SEQ, D_MODEL), 1.0)
    mem = normal(ks[1], (BATCH, MEM_LEN, D_MODEL), 1.0)
    mix_norm_w = gain(ks[2], (DEPTH, D_MODEL))
    w_in = normal(ks[3], (DEPTH, D_MODEL, IN_COLS), D_MODEL ** -0.5)
    dn_conv_w = normal(ks[4], (DEPTH, CONV_W, DN_QKV_COLS), CONV_W ** -0.5)
    dn_a_log = jnp.log(jax.random.uniform(ks[5], (DEPTH, 2, DN_HEADS), f32, 1.0, 16.0))
    dt = jnp.exp(jax.random.uniform(ks[6], (DEPTH, 2, DN_HEADS), f32, math.log(1e-3), math.log(1e-1)))
    dn_dt_bias = dt + jnp.log(-jnp.expm1(-dt))
    dn_norm_w = gain(ks[7], (DEPTH, DN_HEAD_DIM))
    q_norm_w = gain(ks[8], (DEPTH, ATT_HEAD_DIM))
    k_norm_w = gain(ks[9], (DEPTH, ATT_HEAD_DIM))
    mem_norm_w = gain(ks[10], (DEPTH, D_MODEL))
    w_mem_kv = normal(ks[11], (DEPTH, D_MODEL, 2 * X_HEADS * X_HEAD_DIM), D_MODEL ** -0.5)
    w_branch = normal(ks[12], (DEPTH, N_BRANCH, BRANCH_W, D_MODEL), BRANCH_W ** -0.5)
    w_out = normal(ks[13], (DEPTH, D_MODEL, D_MODEL), D_MODEL ** -0.5)
    ffn_norm_w = gain(ks[14], (DEPTH, D_MODEL))
    w_up = normal(ks[15], (DEPTH, D_MODEL, D_FF), D_MODEL ** -0.5)
    w_down = normal(ks[16], (DEPTH, D_FF, D_MODEL), D_FF ** -0.5)
    final_norm_w = gain(ks[17], (D_MODEL,))
    return {"x": x, "mem": mem, "mix_norm_w": mix_norm_w, "w_in": w_in, "dn_conv_w": dn_conv_w,
            "dn_a_log": dn_a_log, "dn_dt_bias": dn_dt_bias, "dn_norm_w": dn_norm_w,
            "q_norm_w": q_norm_w, "k_norm_w": k_norm_w, "mem_norm_w": mem_norm_w, "w_mem_kv": w_mem_kv,
            "w_branch": w_branch, "w_out": w_out, "ffn_norm_w": ffn_norm_w, "w_up": w_up,
            "w_down": w_down, "final_norm_w": final_norm_w}


def reference(x, mem, mix_norm_w, w_in, dn_conv_w, dn_a_log, dn_dt_bias, dn_norm_w, q_norm_w, k_norm_w,
              mem_norm_w, w_mem_kv, w_branch, w_out, ffn_norm_w, w_up, w_down, final_norm_w):
    b_, s_, _ = x.shape
    m_ = mem.shape[1]
    rows = s_ // GRID_W
    row_ids = jnp.repeat(jnp.arange(rows), GRID_W)
    col_ids = jnp.tile(jnp.arange(GRID_W), rows)
    cos, sin = axial_rope_tables(row_ids, col_ids)

    def flip(t):
        return jnp.flip(t, axis=1)

    for l in range(DEPTH):
        h = rms_norm(x, mix_norm_w[l])
        proj = h @ w_in[l]
        dn_qkv, dn_z, dn_b, dn_a, a_q, a_k, a_v, x_q, gate_logits = jnp.split(proj, IN_OFFSETS, axis=-1)

        dn_qkv = jax.nn.silu(centred_depthwise_conv(dn_qkv, dn_conv_w[l]))
        dq, dk, dv = jnp.split(dn_qkv, 3, axis=-1)
        dq = l2_normalize(dq.reshape(b_, s_, DN_HEADS, DN_HEAD_DIM))
        dk = l2_normalize(dk.reshape(b_, s_, DN_HEADS, DN_HEAD_DIM))
        dv = dv.reshape(b_, s_, DN_HEADS, DN_HEAD_DIM)
        beta = jax.nn.sigmoid(dn_b.astype(jnp.float32)).reshape(b_, s_, 2, DN_HEADS)
        g = -jnp.exp(dn_a_log[l].astype(jnp.float32)) * jax.nn.softplus(
            dn_a.astype(jnp.float32).reshape(b_, s_, 2, DN_HEADS) + dn_dt_bias[l].astype(jnp.float32))
        o_fwd = gated_delta_rule(dq, dk, dv, g[:, :, 0], beta[:, :, 0])
        o_bwd = flip(gated_delta_rule(flip(dq), flip(dk), flip(dv), flip(g[:, :, 1]), flip(beta[:, :, 1])))
        o_dn = rms_norm(o_fwd + o_bwd, dn_norm_w[l]) * jax.nn.silu(dn_z.reshape(b_, s_, DN_HEADS, DN_HEAD_DIM))
        y_dn = o_dn.reshape(b_, s_, BRANCH_W)

        aq = rms_norm(a_q.reshape(b_, s_, ATT_HEADS, ATT_HEAD_DIM), q_norm_w[l])
        ak = rms_norm(a_k.reshape(b_, s_, ATT_KV_HEADS, ATT_HEAD_DIM), k_norm_w[l])
        av = a_v.reshape(b_, s_, ATT_KV_HEADS, ATT_HEAD_DIM)
        aq = apply_axial_rope(aq, cos, sin)
        ak = apply_axial_rope(ak, cos, sin)
        y_att = block_gqa(aq, ak, av)

        mkv = rms_norm(mem, mem_norm_w[l]) @ w_mem_kv[l]
        mk, mv = jnp.split(mkv, 2, axis=-1)
        y_x = cross_attention(x_q.reshape(b_, s_, X_HEADS, X_HEAD_DIM),
                              mk.reshape(b_, m_, X_HEADS, X_HEAD_DIM),
                              mv.reshape(b_, m_, X_HEADS, X_HEAD_DIM))

        ys = jnp.stack([y_dn, y_att, y_x], axis=2)
        yb = jnp.einsum("bsnc,ncd->bsnd", ys, w_branch[l])
        gates = jax.nn.sigmoid(gate_logits.reshape(b_, s_, N_BRANCH, D_MODEL))
        merged = jnp.sum(gates * yb, axis=2)
        x = x + merged @ w_out[l]

        hf = rms_norm(x, ffn_norm_w[l])
        x = x + jnp.square(jax.nn.relu(hf @ w_up[l])) @ w_down[l]

    return rms_norm(x, final_norm_w)
```

```python
import os
import numpy as np
from contextlib import ExitStack
from collections import defaultdict
import concourse.bass as bass
import concourse.mybir as mybir
from concourse.bass_utils import run_bass_kernel_spmd

F32 = mybir.dt.float32
BF16 = mybir.dt.bfloat16
AF = mybir.ActivationFunctionType
ALU = mybir.AluOpType

NSEQ = 2
S = 2048
D = 1024
NT = 16
EPS = 1e-6
DEBUG = bool(os.environ.get("KDEBUG", ""))
STOP = os.environ.get("KSTOP", "")
STRICT = not os.environ.get("KLOOSE")


class _Stop(Exception):
    pass


_ST = {"stopped": False}


def _chk(tag):
    if STOP == tag:
        _ST["stopped"] = True


class Trk:
    __slots__ = ("last_w", "readers")

    def __init__(self):
        self.last_w = None
        self.readers = {}


class Op:
    __slots__ = ("eng", "fn", "deps", "signal", "dma_sem", "dma_val", "idx", "sigval")

    def __init__(self, eng, fn):
        self.eng = eng
        self.fn = fn
        self.deps = {}
        self.signal = False
        self.dma_sem = None
        self.dma_val = 0
        self.idx = -1
        self.sigval = 0


class Sched:
    ENGS = ("pe", "act", "dve", "pool", "sp")

    def __init__(self, nc):
        self.nc = nc
        self.ops = []
        self.dma_sems = {}
        self.eng_sems = {}
        self.last_on = {}

    def op(self, eng, fn, reads=(), writes=(), dma=None):
        o = Op(eng, fn)
        o.idx = len(self.ops)
        for t in reads:
            if t.last_w is not None:
                o.deps[t.last_w] = True
        for t in writes:
            if t.last_w is not None:
                o.deps.setdefault(t.last_w, False)
            for r in t.readers.values():
                o.deps.setdefault(r, False)
        for t in reads:
            t.readers[eng if dma is None else ("dma", dma)] = o.idx
        for t in writes:
            t.last_w = o.idx
            t.readers = {}
        o.deps.pop(o.idx, None)
        if dma is not None:
            ent = self.dma_sems[dma]
            ent[1] += 16
            o.dma_sem = dma
            o.dma_val = ent[1]
            self.last_on[("dma", dma)] = o.idx
        else:
            self.last_on[eng] = o.idx
        self.ops.append(o)
        return o

    def barrier(self, scratch_ap):
        if _ST["stopped"]:
            return None
        o = Op("dve", lambda e: e.memset(scratch_ap, 0.0))
        o.idx = len(self.ops)
        for k, v in self.last_on.items():
            o.deps[v] = True
        self.last_on["dve"] = o.idx
        self.ops.append(o)
        self.bar = o.idx
        return o.idx

    def fresh(self):
        t = Trk()
        t.last_w = getattr(self, "bar", None)
        return t

    def new_dma_sem(self, name, handle):
        self.dma_sems[name] = [handle, 0]

    def _skip(self, p, o):
        return p.dma_sem is None and o.dma_sem is None and p.eng == o.eng

    def finalize(self):
        ops = self.ops
        for o in ops:
            for d, raw in o.deps.items():
                p = ops[d]
                if p.dma_sem is None:
                    if self._skip(p, o) and (o.eng == "pe" or (not raw and not STRICT)):
                        continue
                    p.signal = True
        cnt = {e: 0 for e in self.ENGS}
        for o in ops:
            if o.dma_sem is None and o.signal:
                cnt[o.eng] += 1
                o.sigval = cnt[o.eng]

    def emit(self, nc, final_waits=(), max_pe=int(os.environ.get("KMAXPE", "250"))):
        ops = self.ops
        sems = self.eng_sems
        dma_sems = self.dma_sems
        segments = []
        cur = []
        npe = 0
        for o in ops:
            cur.append(o)
            if o.eng == "pe":
                npe += 1
            if npe >= max_pe or len(cur) >= 4 * max_pe:
                segments.append(cur)
                cur = []
                npe = 0
        if cur:
            segments.append(cur)
        waited = {e: {} for e in self.ENGS}
        self._skipfn = self._skip

        def run(engname, eng, seg_ops, last):
            wd = waited[engname]
            for o in seg_ops:
                need = {}
                for d, raw in o.deps.items():
                    p = ops[d]
                    if p.dma_sem is not None:
                        key = ("d", p.dma_sem)
                        val = p.dma_val
                    else:
                        if self._skip(p, o) and (engname == "pe" or (not raw and not STRICT)):
                            continue
                        key = ("e", p.eng)
                        val = p.sigval
                    if need.get(key, 0) < val:
                        need[key] = val
                for key, val in need.items():
                    if wd.get(key, 0) >= val:
                        continue
                    wd[key] = val
                    h = dma_sems[key[1]][0] if key[0] == "d" else sems[key[1]]
                    eng.wait_ge(h, val)
                if o.fn is None:
                    continue
                ins = o.fn(eng)
                if o.dma_sem is not None:
                    ins.then_inc(dma_sems[o.dma_sem][0], 16)
                elif o.signal:
                    ins.then_inc(sems[engname], 1)
            if engname == "sp" and last:
                for name in final_waits:
                    h, c = dma_sems[name]
                    if c > 0:
                        eng.wait_ge(h, c)

        for si, seg in enumerate(segments):
            last = (si == len(segments) - 1)
            per_eng = {e: [o for o in seg if o.eng == e] for e in self.ENGS}
            with nc.Block() as block:
                if per_eng["pe"]:
                    block.tensor(lambda e, l=per_eng["pe"]: run("pe", e, l, last))
                if per_eng["act"]:
                    block.scalar(lambda e, l=per_eng["act"]: run("act", e, l, last))
                if per_eng["dve"]:
                    block.vector(lambda e, l=per_eng["dve"]: run("dve", e, l, last))
                if per_eng["pool"]:
                    block.gpsimd(lambda e, l=per_eng["pool"]: run("pool", e, l, last))
                if per_eng["sp"] or last:
                    block.sync(lambda e, l=per_eng["sp"]: run("sp", e, l, last))
        print("kernel: blocks", len(segments), flush=True)


C_IDENT, C_UF, C_UB, C_ONES, C_MASKF, C_MASKB, C_STRF, C_STRB, C_PERM, C_BD64, C_SWLO, C_SWHI, C_NEG1 = range(13)
NCST = 13


def make_consts():
    i = np.arange(128)
    c = np.zeros((NCST, 128, 128), np.float32)
    c[C_IDENT] = np.eye(128)
    c[C_UF] = (i[:, None] <= i[None, :])
    c[C_UB] = (i[:, None] >= i[None, :])
    c[C_ONES] = 1.0
    c[C_MASKF] = np.where(i[None, :] <= i[:, None], 0.0, -1e30)
    c[C_MASKB] = np.where(i[None, :] >= i[:, None], 0.0, -1e30)
    c[C_STRF] = (i[None, :] < i[:, None])
    c[C_STRB] = (i[None, :] > i[:, None])
    d = i % 64
    partner = np.where((d % 32) < 16, i + 16, i - 16)
    pm = np.zeros((128, 128), np.float32)
    pm[partner, i] = 1.0
    c[C_PERM] = pm
    c[C_BD64] = ((i[:, None] // 64) == (i[None, :] // 64))
    c[C_SWLO] = (i[:, None] == i[None, :] + 64)
    c[C_SWHI] = (i[:, None] + 64 == i[None, :])
    c[C_NEG1] = -1.0
    cst = np.ascontiguousarray(c.transpose(1, 0, 2).reshape(128, NCST * 128))
    t = np.arange(S)
    inv = (10000.0 ** (-np.arange(16, dtype=np.float32) / 16)).astype(np.float32)
    row = (t // 64).astype(np.float32)
    col = (t % 64).astype(np.float32)
    ang = np.stack([row[:, None] * inv[None, :], col[:, None] * inv[None, :]], 0)
    cos = np.cos(ang).astype(np.float32)
    sin = np.sin(ang).astype(np.float32)
    rope = np.zeros((128, 2, S), np.float32)
    for p in range(128):
        dd = p % 64
        ax = dd // 32
        half = (dd % 32) // 16
        pr = dd % 16
        rope[p, 0] = cos[ax, :, pr]
        rope[p, 1] = (-sin[ax, :, pr]) if half == 0 else sin[ax, :, pr]
    return cst, rope


V_NW_MIX, V_NW_MEM, V_NW_FFN, V_NW_FIN, V_CONV, V_DNW, V_QNW, V_KNW, V_ALOG, V_DTB = 0, 8, 16, 24, 32, 92, 93, 94, 95, 96
NVEC = 100


def make_vec(inp):
    v = np.zeros((128, NVEC), np.float32)
    v[:, V_NW_MIX:V_NW_MIX + 8] = inp["mix_norm_w"][0].reshape(8, 128).T
    v[:, V_NW_MEM:V_NW_MEM + 8] = inp["mem_norm_w"][0].reshape(8, 128).T
    v[:, V_NW_FFN:V_NW_FFN + 8] = inp["ffn_norm_w"][0].reshape(8, 128).T
    v[:, V_NW_FIN:V_NW_FIN + 8] = inp["final_norm_w"].reshape(8, 128).T
    cw = inp["dn_conv_w"][0]
    v[:, V_CONV:V_CONV + 60] = cw.reshape(5, 12, 128).transpose(2, 1, 0).reshape(128, 60)
    v[:, V_DNW] = inp["dn_norm_w"][0]
    v[:, V_QNW] = np.tile(inp["q_norm_w"][0], 2)
    v[:, V_KNW] = np.tile(inp["k_norm_w"][0], 2)
    v[0:8, V_ALOG] = inp["dn_a_log"][0].reshape(8)
    v[0:8, V_DTB] = inp["dn_dt_bias"][0].reshape(8)
    return v


def build_program():
    _ST["stopped"] = False
    nc = bass.Bass("TRN2", target_bir_lowering=False)

    def dram(name, shape, kind="ExternalInput"):
        return nc.dram_tensor(name, shape, F32, kind=kind).ap()

    x_d = dram("x", [NSEQ, S, D])
    mem_d = dram("mem", [NSEQ, 256, D])
    w_in_d = dram("w_in", [D, 6416])
    w_kv_d = dram("w_mem_kv", [D, 1024])
    w_br_d = dram("w_branch", [1536, D])
    w_out_d = dram("w_out", [D, D])
    w_up_d = dram("w_up", [D, 4096])
    w_dn_d = dram("w_down", [4096, D])
    cst_d = dram("cst", [128, NCST * 128])
    rope_d = dram("rope", [128, 2, S])
    vec_d = dram("vec", [128, NVEC])
    out_d = dram("out", [NSEQ, S, D], kind="ExternalOutput")
    dbg_d = nc.dram_tensor("dbg", [128, 8, 4, S], BF16, kind="ExternalOutput").ap() if DEBUG else None
    dbgf_d = nc.dram_tensor("dbgf", [128, 4, S], F32, kind="ExternalOutput").ap() if DEBUG else None
    wmap = {"w_in": w_in_d, "w_kv": w_kv_d, "w_br": w_br_d, "w_out": w_out_d, "w_up": w_up_d, "w_dn": w_dn_d}

    es = ExitStack()
    with es:
        def sb(name, shape, dt=F32):
            return es.enter_context(nc.sbuf_tensor(name, shape, dt))

        sch = Sched(nc)
        for e in Sched.ENGS:
            sch.eng_sems[e] = es.enter_context(nc.semaphore("sem_" + e))

        def dsem(name):
            sch.new_dma_sem(name, es.enter_context(nc.semaphore("ds_" + name)))

        TK = defaultdict(Trk)

        def T(*key):
            return TK[key]

        def OP(eng, fn, r=(), w=(), dma=None):
            if _ST["stopped"]:
                return None
            return sch.op(eng, fn, reads=r, writes=w, dma=dma)

        cst = sb("cst_sb", [128, NCST, 128])
        cbf = sb("cbf", [128, NCST, 128], BF16)
        vec = sb("vec_sb", [128, NVEC])
        nA = sb("nA", [8, 1])
        scr = sb("scr", [128, 4])
        rtmp = sb("rtmp", [128, 512])
        wst = [sb("wst%d" % i, [128, 8, 256]) for i in range(2)]
        wbf = [sb("wbf%d" % i, [128, 8, 256], BF16) for i in range(3)]
        hT = sb("hT", [128, 8, S], BF16)
        banks = [es.enter_context(nc.psum_tensor("bank%d" % i, [128, 512], F32)) for i in range(7)]
        pTb = es.enter_context(nc.psum_tensor("pTb", [128, 1024], BF16))
        for n_ in ["cst", "vec", "w0", "w1", "x0", "x1", "out0", "out1", "misc", "dbg"]:
            dsem(n_)

        def cf(i):
            return cst[:, i, :]

        def cb(i):
            return cbf[:, i, :]

        OP("sp", lambda e: e.dma_start(out=cst[:].rearrange("p a b -> p (a b)"), in_=cst_d), w=[T("cst")], dma="cst")
        OP("sp", lambda e: e.dma_start(out=vec[:], in_=vec_d), w=[T("vec")], dma="vec")
        OP("dve", lambda e: e.tensor_copy(out=cbf[:], in_=cst[:]), r=[T("cst")], w=[T("cbf")])
        OP("act", lambda e: e.activation(out=nA[:], in_=vec[0:8, V_ALOG:V_ALOG + 1], func=AF.Exp), r=[T("vec")], w=[T("nA")])
        OP("dve", lambda e: e.tensor_scalar_mul(out=nA[:], in0=nA[:], scalar1=-1.0), r=[T("nA")], w=[T("nA")])
        CONSTS = [T("cst"), T("cbf"), T("vec"), T("nA")]

        bank_rr = defaultdict(int)

        def bank(group, ids):
            i = ids[bank_rr[group] % len(ids)]
            bank_rr[group] += 1
            return banks[i], T("bank", i)

        wplan = []
        for s_ in range(NSEQ):
            for c0 in range(0, 1536, 256):
                wplan.append(("w_in", 0, 8, c0, 256))
            wplan.append(("w_in", 0, 8, 2048, 256))
            for c0 in range(1536, 2048, 256):
                wplan.append(("w_in", 0, 8, c0, 256))
            for dt in range(4):
                wplan.append(("w_in", 0, 8, 3344 + 0 * 1024 + dt * 256, 256))
                wplan.append(("w_br", 0 * 512, 4, dt * 256, 256))
            for c0 in range(2064, 2576, 256):
                wplan.append(("w_in", 0, 8, c0, 256))
            wplan.append(("kdup", 0, 8, 2576, 256))
            wplan.append(("w_in", 0, 8, 2704, 128))
            for dt in range(4):
                wplan.append(("w_in", 0, 8, 3344 + 1 * 1024 + dt * 256, 256))
                wplan.append(("w_br", 1 * 512, 4, dt * 256, 256))
            for c0 in range(0, 1024, 256):
                wplan.append(("w_kv", 0, 8, c0, 256))
            for c0 in range(2832, 3344, 256):
                wplan.append(("w_in", 0, 8, c0, 256))
            for dt in range(4):
                wplan.append(("w_in", 0, 8, 3344 + 2 * 1024 + dt * 256, 256))
                wplan.append(("w_br", 2 * 512, 4, dt * 256, 256))
            for hs in range(2):
                for dt in range(4):
                    wplan.append(("w_out", 0, 8, dt * 256, 256))
                for fg in range(4):
                    for ut in range(4):
                        wplan.append(("w_up", 0, 8, fg * 1024 + ut * 256, 256))
                    for dt in range(4):
                        wplan.append(("w_dn", fg * 1024, 8, dt * 256, 256))
        wstate = {"dma": 0, "cast": 0, "use": 0}

        def w_issue_dma(i):
            name, k0, KC, c0, ncols = wplan[i]
            st = wst[i % 2]
            sem = "w%d" % (i % 2)
            tr = T("wst", i % 2)
            if name == "kdup":
                for j in range(4):
                    src = w_in_d[0:D, 2576 + (j // 2) * 64: 2576 + (j // 2) * 64 + 64].rearrange("(kc p) n -> p kc n", p=128)
                    OP("sp", lambda e, st=st, src=src, j=j: e.dma_start(out=st[:, 0:8, j * 64:(j + 1) * 64], in_=src),
                       w=[tr], dma=sem)
            else:
                src = wmap[name][k0:k0 + KC * 128, c0:c0 + ncols].rearrange("(kc p) n -> p kc n", p=128)
                OP("sp", lambda e, st=st, src=src, KC=KC, ncols=ncols: e.dma_start(out=st[:, 0:KC, 0:ncols], in_=src),
                   w=[tr], dma=sem)

        def w_issue_cast(i):
            name, k0, KC, c0, ncols = wplan[i]
            st = wst[i % 2]
            wb = wbf[i % 3]
            if i % 2 == 0:
                OP("act", lambda e, st=st, wb=wb, KC=KC, ncols=ncols: e.activation(out=wb[:, 0:KC, 0:ncols], in_=st[:, 0:KC, 0:ncols], func=AF.Copy),
                   r=[T("wst", i % 2)], w=[T("wbf", i % 3)])
            else:
                OP("dve", lambda e, st=st, wb=wb, KC=KC, ncols=ncols: e.tensor_copy(out=wb[:, 0:KC, 0:ncols], in_=st[:, 0:KC, 0:ncols]),
                   r=[T("wst", i % 2)], w=[T("wbf", i % 3)])

        def wnext(expect):
            i = wstate["use"]
            if _ST["stopped"]:
                wstate["use"] += 1
                return wbf[i % 3], T("wbf", i % 3)
            assert wplan[i][0] == expect[0] and wplan[i][3] == expect[1], (wplan[i], expect)
            while wstate["dma"] < min(len(wplan), i + 2):
                w_issue_dma(wstate["dma"])
                wstate["dma"] += 1
            while wstate["cast"] < min(len(wplan), i + 2):
                if wstate["dma"] <= wstate["cast"]:
                    w_issue_dma(wstate["dma"])
                    wstate["dma"] += 1
                w_issue_cast(wstate["cast"])
                wstate["cast"] += 1
                while wstate["dma"] < min(len(wplan), wstate["cast"] + 2):
                    w_issue_dma(wstate["dma"])
                    wstate["dma"] += 1
            wstate["use"] += 1
            return wbf[i % 3], T("wbf", i % 3)

        def proj_fm(wb, wtr, KC, col_chunks, rhs_fn, rhs_trk_fn, evac_fn, ntb=4, bgroup=("proj", (0, 1, 2, 3))):
            for ci, (co, m) in enumerate(col_chunks):
                for tb in range(ntb):
                    bk, btr = bank(*bgroup)
                    for kc in range(KC):
                        OP("pe", lambda e, bk=bk, kc=kc, co=co, m=m, tb=tb: e.matmul(
                            bk[0:m, 0:512], lhsT=wb[:, kc, co:co + m], rhs=rhs_fn(kc, tb), start=(kc == 0), stop=(kc == KC - 1)),
                           r=[wtr, rhs_trk_fn(kc, tb)], w=[btr])
                    evac_fn(ci, tb, bk, btr)

        def rmsnorm_tokmajor(src_ap_fn, ntiles, dstT, dst_trk_fn, nw_off, tag, sb, xs, F):
            junk = sb("junk_" + tag, [128, D], BF16)
            xn = [sb("xn%d_" % i + tag, [128, D], BF16) for i in range(2)]
            st = sb("st_" + tag, [128, 4])
            for t in range(ntiles):
                xt = xs[t % 2]
                xtr = F("xs", t % 2)
                OP("sp", lambda e, xt=xt, t=t: e.dma_start(out=xt[:], in_=src_ap_fn(t)), w=[xtr], dma="x%d" % (t % 2))
                OP("act", lambda e, xt=xt: e.activation(out=junk[:], in_=xt[:], func=AF.Square, scale=1.0 / 32.0, accum_out=st[:, 0:1]),
                   r=[xtr], w=[F("junk", tag), F("st", tag)])
                OP("act", lambda e: e.activation(out=st[:, 1:2], in_=st[:, 0:1], func=AF.Sqrt, bias=EPS, scale=1.0),
                   r=[F("st", tag)], w=[F("st", tag)])
                OP("dve", lambda e: e.reciprocal(out=st[:, 2:3], in_=st[:, 1:2]), r=[F("st", tag)], w=[F("st", tag)])
                xnt = xn[t % 2]
                xntr = F("xn", tag, t % 2)
                OP("dve", lambda e, xt=xt, xnt=xnt: e.tensor_scalar_mul(out=xnt[:], in0=xt[:], scalar1=st[:, 2:3]),
                   r=[xtr, F("st", tag)], w=[xntr])
                for c in range(8):
                    OP("pe", lambda e, c=c, xnt=xnt: e.transpose(out=pTb[:, c * 128:(c + 1) * 128], in_=xnt[:, c * 128:(c + 1) * 128], identity=cb(C_IDENT)),
                       r=[xntr, T("cbf")], w=[T("pTb", 0)])
                OP("dve", lambda e, t=t: e.tensor_tensor(
                    out=dstT[:, :, t * 128:(t + 1) * 128], in0=pTb[:].rearrange("p (c j) -> p c j", c=8),
                    in1=vec[:, nw_off:nw_off + 8].unsqueeze(2).broadcast_to([128, 8, 128]), op=ALU.mult),
                   r=[T("pTb", 0), T("pTb", 0), T("vec")], w=[dst_trk_fn(t)])

        def sumsq_rn(src_ap, src_trks, nparts_scale, lhsT_const, dst_rn, dst_trk, tmp_sq, tmp_trk, bgroup):
            OP("act", lambda e: e.activation(out=tmp_sq, in_=src_ap, func=AF.Square), r=src_trks, w=[tmp_trk])
            bk, btr = bank(*bgroup)
            OP("pe", lambda e: e.matmul(bk[:, 0:512], lhsT=lhsT_const, rhs=tmp_sq, start=True, stop=True), r=[tmp_trk, T("cbf")], w=[btr])
            OP("act", lambda e: e.activation(out=rtmp[:], in_=bk[:, 0:512], func=AF.Sqrt, bias=EPS, scale=nparts_scale), r=[btr], w=[T("rtmp")])
            OP("dve", lambda e: e.reciprocal(out=dst_rn, in_=rtmp[:]), r=[T("rtmp")], w=[dst_trk])

        def body():
          for s_ in range(NSEQ):
            body_seq(s_)

        def body_seq(s_):
            nonlocal_dummy = None
            seqscope = ExitStack()
            SEQSC.append(seqscope)
            ysb = seqscope.enter_context(nc.sbuf_tensor("ysb_s%d" % s_, [128, 4, S], BF16))
            with ExitStack() as ph:
                def psb(name, shape, dt=F32, ph=ph):
                    return ph.enter_context(nc.sbuf_tensor(name + "_s%d" % s_, shape, dt))
                FR = {}

                def F(*key):
                    if key not in FR:
                        FR[key] = sch.fresh()
                    return FR[key]
                xs = [psb("xsA%d" % i, [128, D]) for i in range(2)]
                rmsnorm_tokmajor(lambda t: x_d[s_, t * 128:(t + 1) * 128, :], NT, hT, lambda t: T("hT", t // 4), V_NW_MIX, "a", psb, xs, F)
            sch.barrier(scr[:, 0:1])
            _chk("A")

            with ExitStack() as ph:
                def psb(name, shape, dt=F32, ph=ph):
                    return ph.enter_context(nc.sbuf_tensor(name + "_s%d" % s_, shape, dt))
                qT = psb("qT", [128, 4, S], BF16)
                kT = psb("kT", [128, 4, S], BF16)
                vtok = psb("vtok", [128, NT, 4, 128], BF16)
                ydn = ysb
                FR = {}

                def F(*key):
                    if key not in FR:
                        FR[key] = sch.fresh()
                    return FR[key]

                with ExitStack() as ph2:
                    def psb2(name, shape, dt=F32, ph2=ph2):
                        return ph2.enter_context(nc.sbuf_tensor(name + "_s%d" % s_, shape, dt))
                    pre = [psb2("pre%d" % i, [128, S + 128]) for i in range(2)]
                    cacc = [psb2("cacc%d" % i, [128, S]) for i in range(2)]
                    sqb = psb2("sqb", [128, 512], BF16)
                    rn = psb2("rn", [128, 512])
                    vTt = psb2("vTt", [128, S], BF16)
                    ctmp = psb2("ctmp", [128, S])
                    for i in range(2):
                        OP("dve", lambda e, i=i: e.memset(pre[i][:, 0:64], 0.0), w=[F("prepad", i)])
                        OP("dve", lambda e, i=i: e.memset(pre[i][:, S + 64:S + 128], 0.0), w=[F("prepad", i)])
                    for c in range(12):
                        if c % 2 == 0:
                            wb, wtr = wnext(("w_in", c * 128))
                        pb = pre[c % 2]
                        ca = cacc[c % 2]
                        if c < int(os.environ.get("KSKIP", "0")):
                            continue

                        def ev(ci, tb, bk, btr, pb=pb, c=c):
                            OP("act", lambda e: e.activation(out=pb[:, 64 + tb * 512: 64 + (tb + 1) * 512], in_=bk[:, 0:512], func=AF.Copy),
                               r=[btr], w=[F("pre", c % 2, tb)])
                        proj_fm(wb, wtr, 8, [((c % 2) * 128, 128)], lambda kc, tb: hT[:, kc, tb * 512:(tb + 1) * 512],
                                lambda kc, tb: T("hT", tb), ev)
                        _chk("c%dproj" % c)
                        ce = "dve"
                        pre_tr = [F("pre", c % 2, tb) for tb in range(4)] + [F("prepad", c % 2)]
                        OP(ce, lambda e, ca=ca, pb=pb, c=c: e.tensor_scalar_mul(out=ca[:], in0=pb[:, 62:62 + S], scalar1=vec[:, V_CONV + c * 5:V_CONV + c * 5 + 1]),
                           r=pre_tr + [T("vec")], w=[F("cacc", c % 2)])
                        for j in range(1, 5):
                            if ce == "dve":
                                OP(ce, lambda e, ca=ca, pb=pb, c=c, j=j: e.scalar_tensor_tensor(
                                    out=ca[:], in0=pb[:, 62 + j:62 + j + S], scalar=vec[:, V_CONV + c * 5 + j:V_CONV + c * 5 + j + 1], in1=ca[:],
                                    op0=ALU.mult, op1=ALU.add), r=pre_tr + [F("cacc", c % 2), T("vec")], w=[F("cacc", c % 2)])
                            else:
                                OP(ce, lambda e, pb=pb, c=c, j=j: e.tensor_scalar_mul(out=ctmp[:], in0=pb[:, j:j + S], scalar1=vec[:, V_CONV + c * 5 + j:V_CONV + c * 5 + j + 1]),
                                   r=pre_tr + [T("vec")], w=[F("ctmp")])
                                OP(ce, lambda e, ca=ca: e.tensor_tensor(out=ca[:], in0=ca[:], in1=ctmp[:], op=ALU.add), r=[F("ctmp"), F("cacc", c % 2)], w=[F("cacc", c % 2)])
                        _chk("c%dconv" % c)
                        h = c % 4
                        if c >= 8:
                            OP("act", lambda e, ca=ca: e.activation(out=vTt[:], in_=ca[:], func=AF.Silu), r=[F("cacc", c % 2)], w=[F("vTt")])
                            src, strk, dst, dtrk = vTt, F("vTt"), vtok, F("vtok")
                        else:
                            OP("act", lambda e, ca=ca: e.activation(out=ca[:], in_=ca[:], func=AF.Silu), r=[F("cacc", c % 2)], w=[F("cacc", c % 2)])
                            dstT = qT if c < 4 else kT
                            dtr = F("qT", h) if c < 4 else F("kT", h)
                            for tb in range(4):
                                sl = slice(tb * 512, (tb + 1) * 512)
                                sumsq_rn(ca[:, sl], [F("cacc", c % 2)], 1.0, cb(C_ONES), rn[:], F("rn"), sqb[:], F("sqb"), ("aux", (4, 5)))
                                OP("dve", lambda e, ca=ca, sl=sl, dstT=dstT, h=h, c=c, rn=rn: e.scalar_tensor_tensor(
                                    out=dstT[:, h, sl], in0=ca[:, sl], scalar=(128.0 ** -0.5 if c < 4 else 1.0), in1=rn[:],
                                    op0=ALU.mult, op1=ALU.mult), r=[F("cacc", c % 2), F("rn")], w=[dtr])
                            src, strk, dst, dtrk = (None, None, None, None)
                        _chk("c%dnorm" % c)
                        if src is not None and not os.environ.get("KNOVT"):
                            for n0 in range(0, NT, 4):
                                hf = 0 if os.environ.get("KHF0") else (n0 // 4) % 2
                                for j in range(4):
                                    n = n0 + j
                                    sap = src[:, n * 128:(n + 1) * 128]
                                    OP("pe", lambda e, sap=sap, hf=hf, j=j: e.transpose(out=pTb[:, hf * 512 + j * 128: hf * 512 + (j + 1) * 128], in_=sap, identity=cb(C_IDENT)),
                                       r=[strk, T("cbf")], w=[T("pTb", 0)])
                                OP("dve", lambda e, hf=hf, n0=n0, dst=dst, h=h: e.tensor_scalar_mul(out=dst[:, n0:n0 + 4, h, :], in0=pTb[:, hf * 512:(hf + 1) * 512].rearrange("p (a b) -> p a b", a=4), scalar1=1.0),
                                   r=[T("pTb", 0)], w=[dtrk])
                sch.barrier(scr[:, 1:2])
                _chk("conv")
                FR.clear()
                oacc = psb("oacc", [128, 4, S])
                with ExitStack() as ph2:
                    def psb2(name, shape, dt=F32, ph2=ph2):
                        return ph2.enter_context(nc.sbuf_tensor(name + "_s%d" % s_, shape, dt))
                    btok = psb2("btok", [128, NT, 8])
                    gtk = psb2("gtk", [128, 2, NT, 4])
                    gcs = psb2("gcs", [128, NT, 8])
                    gts = psb2("gts", [128, NT, 8])
                    egc = psb2("egc", [128, NT, 8])
                    nbe = psb2("nbe", [128, NT, 8])
                    nbt = psb2("nbt", [128, NT, 8])
                    ekd = psb2("ekd", [128, NT, 8])
                    egt = psb2("egt", [128, NT, 8])
                    ph2b = ph2.enter_context(ExitStack())
                    bT = ph2b.enter_context(nc.sbuf_tensor("bT_s%d" % s_, [8, S], F32))
                    gT = ph2b.enter_context(nc.sbuf_tensor("gT_s%d" % s_, [8, S], F32))
                    t1 = ph2b.enter_context(nc.sbuf_tensor("t1_s%d" % s_, [8, S], F32))
                    t2 = ph2b.enter_context(nc.sbuf_tensor("t2_s%d" % s_, [8, S], F32))
                    wb, wtr = wnext(("w_in", 2048))
                    for tb in range(4):
                        sl = slice(tb * 512, (tb + 1) * 512)
                        for which in range(2):
                            bk, btr = bank("proj", (0, 1, 2, 3))
                            for kc in range(8):
                                OP("pe", lambda e, bk=bk, kc=kc, which=which, sl=sl, wb=wb: e.matmul(bk[0:8, 0:512], lhsT=wb[:, kc, which * 8:which * 8 + 8], rhs=hT[:, kc, sl], start=(kc == 0), stop=(kc == 7)),
                                   r=[wtr, T("hT", tb)], w=[btr])
                            if which == 0:
                                OP("act", lambda e, bk=bk, sl=sl: e.activation(out=bT[:, sl], in_=bk[0:8, 0:512], func=AF.Sigmoid), r=[btr], w=[F("bT")])
                            else:
                                OP("act", lambda e, bk=bk, sl=sl: e.activation(out=t1[:, sl], in_=bk[0:8, 0:512], func=AF.Identity, bias=vec[0:8, V_DTB:V_DTB + 1], scale=1.0),
                                   r=[btr, T("vec")], w=[F("t1")])
                    OP("act", lambda e: e.activation(out=t2[:], in_=t1[:], func=AF.Abs), r=[F("t1")], w=[F("t2")])
                    OP("act", lambda e: e.activation(out=t2[:], in_=t2[:], func=AF.Exp, scale=-1.0), r=[F("t2")], w=[F("t2")])
                    OP("act", lambda e: e.activation(out=t2[:], in_=t2[:], func=AF.Ln, bias=1.0, scale=1.0), r=[F("t2")], w=[F("t2")])
                    OP("dve", lambda e: e.tensor_scalar_max(out=t1[:], in0=t1[:], scalar1=0.0), r=[F("t1")], w=[F("t1")])
                    OP("dve", lambda e: e.tensor_tensor(out=t1[:], in0=t1[:], in1=t2[:], op=ALU.add), r=[F("t1"), F("t2")], w=[F("t1")])
                    OP("dve", lambda e: e.tensor_scalar_mul(out=gT[:], in0=t1[:], scalar1=nA[:, 0:1]), r=[F("t1"), T("nA")], w=[F("gT")])
                    bk, btr = bank("aux", (4, 5))
                    for n in range(NT):
                        OP("pe", lambda e, n=n, bk=bk: e.transpose(out=bk[:, n * 8:(n + 1) * 8], in_=bT[0:8, n * 128:(n + 1) * 128], identity=cst[0:8, C_IDENT, 0:8]),
                           r=[F("bT"), T("cst")], w=[btr])
                        OP("pe", lambda e, n=n, bk=bk: e.transpose(out=bk[:, 128 + n * 8:128 + (n + 1) * 8], in_=gT[0:8, n * 128:(n + 1) * 128], identity=cst[0:8, C_IDENT, 0:8]),
                           r=[F("gT"), T("cst")], w=[btr])
                    OP("dve", lambda e, bk=bk: e.tensor_copy(out=btok[:], in_=bk[:, 0:128].rearrange("p (n r) -> p n r", r=8)), r=[btr], w=[F("btok")])
                    for dr in range(2):
                        OP("dve", lambda e, bk=bk, dr=dr: e.tensor_copy(out=gtk[:, dr], in_=bk[:, 128:256].rearrange("p (n r) -> p n r", r=8)[:, :, dr * 4:dr * 4 + 4]),
                           r=[btr], w=[F("gtk")])
                    bk2, btr2 = bank("aux", (4, 5))
                    for dr in range(2):
                        OP("pe", lambda e, dr=dr, bk2=bk2: e.matmul(bk2[:, dr * 64:(dr + 1) * 64], lhsT=cf(C_UF if dr == 0 else C_UB), rhs=gtk[:, dr].rearrange("p n r -> p (n r)"), start=True, stop=True),
                           r=[F("gtk"), T("cst")], w=[btr2])
                        OP("pe", lambda e, dr=dr, bk2=bk2: e.matmul(bk2[:, 128 + dr * 64:128 + (dr + 1) * 64], lhsT=cf(C_ONES), rhs=gtk[:, dr].rearrange("p n r -> p (n r)"), start=True, stop=True),
                           r=[F("gtk"), T("cst")], w=[btr2])
                    for dr in range(2):
                        OP("dve", lambda e, dr=dr, bk2=bk2: e.tensor_copy(out=gcs[:, :, dr * 4:dr * 4 + 4], in_=bk2[:, dr * 64:(dr + 1) * 64].rearrange("p (n r) -> p n r", r=4)), r=[btr2], w=[F("gcs")])
                        OP("dve", lambda e, dr=dr, bk2=bk2: e.tensor_copy(out=gts[:, :, dr * 4:dr * 4 + 4], in_=bk2[:, 128 + dr * 64:128 + (dr + 1) * 64].rearrange("p (n r) -> p n r", r=4)), r=[btr2], w=[F("gts")])
                    OP("act", lambda e: e.activation(out=egc[:], in_=gcs[:], func=AF.Exp), r=[F("gcs")], w=[F("sc")])
                    OP("act", lambda e: e.activation(out=egt[:], in_=gts[:], func=AF.Exp), r=[F("gts")], w=[F("sc")])
                    OP("dve", lambda e: e.tensor_tensor(out=ekd[:], in0=gts[:], in1=gcs[:], op=ALU.subtract), r=[F("gts"), F("gcs")], w=[F("sc")])
                    OP("act", lambda e: e.activation(out=ekd[:], in_=ekd[:], func=AF.Exp), r=[F("sc")], w=[F("sc")])
                    OP("dve", lambda e: e.tensor_scalar_mul(out=nbt[:], in0=btok[:], scalar1=-1.0), r=[F("btok")], w=[F("sc")])
                    OP("dve", lambda e: e.tensor_tensor(out=nbe[:], in0=nbt[:], in1=egc[:], op=ALU.mult), r=[F("sc")], w=[F("sc")])
                    SC = F("sc")
                    _chk("tables")
                    ph2b.close()
                    sch.barrier(scr[:, 3:4])
                    for k_ in ("sc", "btok", "gtk"):
                        FR[(k_,)] = sch.fresh()
                    for h_ in range(4):
                        FR[("kT", h_)] = sch.fresh()
                        FR[("qT", h_)] = sch.fresh()
                    FR[("vtok",)] = sch.fresh()
                    SC = F("sc")

                    GU = psb2("GU", [128, 4, 128])
                    Dm = psb2("Dm", [128, 4, 128])
                    Dsn = psb2("Dsn", [128, 4, 128])
                    P0 = [psb2("Pk%d" % i, [128, 4, 128], BF16) for i in range(2)]
                    M0 = [psb2("Mk%d" % i, [128, 4, 128], BF16) for i in range(2)]
                    Y0 = [psb2("Yk%d" % i, [128, 4, 128], BF16) for i in range(2)]
                    Abf = psb2("Abf", [128, 4, 128], BF16)
                    ATb = psb2("ATb", [128, 4, 128], BF16)
                    Sst = [psb2("Sst%d" % i, [128, 4, 128]) for i in range(2)]
                    Sbf = [psb2("Sbf%d" % i, [128, 4, 128], BF16) for i in range(2)]
                    VBt = psb2("VBt", [128, 4, 128])
                    R1 = psb2("R1", [128, 4, 128])
                    Rb = psb2("Rb", [128, 4, 128], BF16)
                    vn = psb2("vn", [128, 4, 128], BF16)
                    O1 = psb2("O1", [128, 4, 128], BF16)
                    kdec = psb2("kdec", [128, 4, 128], BF16)
                    for dr in range(2):
                        OP("dve", lambda e, dr=dr: e.memset(Sst[dr][:], 0.0), w=[F("S", dr)])
                        OP("dve", lambda e, dr=dr: e.memset(Sbf[dr][:], 0.0), w=[F("Sbf", dr)])

                    def bc4(ap3):
                        return ap3.unsqueeze(2).broadcast_to([128, 4, 128])

                    def c4(ci, bf=False):
                        a = cb(ci) if bf else cf(ci)
                        return a.unsqueeze(1).broadcast_to([128, 4, 128])

                    def flat(t):
                        return t[:].rearrange("p h j -> p (h j)")

                    for step in range(NT):
                        for dr in range(2):
                            n = step if dr == 0 else NT - 1 - step
                            r0 = dr * 4
                            tok = slice(n * 128, (n + 1) * 128)
                            OP("dve", lambda e, dr=dr, n=n: e.tensor_tensor(out=GU[:], in0=c4(C_UF if dr == 0 else C_UB), in1=bc4(gtk[:, dr, n, :]), op=ALU.mult),
                               r=[F("gtk"), T("cst")], w=[F("GU")])
                            bG, tG = bank("pre", (0, 1, 2))
                            OP("pe", lambda e, bG=bG: e.matmul(bG[:, 0:512], lhsT=cf(C_NEG1), rhs=flat(GU), start=True, stop=False), r=[F("GU"), T("cst")], w=[tG])
                            for h in range(4):
                                OP("pe", lambda e, bG=bG, h=h: e.matmul(bG[:, h * 128:(h + 1) * 128], lhsT=GU[:, h, :], rhs=cf(C_ONES), start=False, stop=True), r=[F("GU"), T("cst")], w=[tG])
                            OP("dve", lambda e, bG=bG, dr=dr: e.tensor_tensor(out=Dm[:], in0=bG[:, 0:512].rearrange("p (h j) -> p h j", h=4), in1=c4(C_MASKF if dr == 0 else C_MASKB), op=ALU.add),
                               r=[tG, T("cst")], w=[F("Dm")])
                            OP("act", lambda e: e.activation(out=Dm[:], in_=Dm[:], func=AF.Exp), r=[F("Dm")], w=[F("Dm")])
                            bKK, tKK = bank("pre", (0, 1, 2))
                            bQK, tQK = bank("pre", (0, 1, 2))
                            for h in range(4):
                                OP("pe", lambda e, h=h, bKK=bKK, tok=tok: e.matmul(bKK[:, h * 128:(h + 1) * 128], lhsT=kT[:, h, tok], rhs=kT[:, h, tok], start=True, stop=True), r=[F("kT", h)], w=[tKK])
                                OP("pe", lambda e, h=h, bQK=bQK, tok=tok: e.matmul(bQK[:, h * 128:(h + 1) * 128], lhsT=qT[:, h, tok], rhs=kT[:, h, tok], start=True, stop=True), r=[F("kT", h), F("qT", h)], w=[tQK])
                            OP("dve", lambda e, dr=dr: e.tensor_tensor(out=Dsn[:], in0=Dm[:], in1=c4(C_STRF if dr == 0 else C_STRB), op=ALU.mult), r=[F("Dm"), T("cst")], w=[F("Dsn")])
                            OP("dve", lambda e, n=n, r0=r0: e.tensor_tensor(out=Dsn[:], in0=Dsn[:], in1=bc4(nbt[:, n, r0:r0 + 4]), op=ALU.mult), r=[F("Dsn"), SC], w=[F("Dsn")])
                            OP("dve", lambda e, bKK=bKK: e.tensor_tensor(out=P0[0][:], in0=bKK[:, 0:512].rearrange("p (h j) -> p h j", h=4), in1=Dsn[:], op=ALU.mult), r=[tKK, F("Dsn")], w=[F("P", 0)])
                            OP("dve", lambda e, bQK=bQK: e.tensor_tensor(out=Abf[:], in0=bQK[:, 0:512].rearrange("p (h j) -> p h j", h=4), in1=Dm[:], op=ALU.mult), r=[tQK, F("Dm")], w=[F("Abf")])
                            for h in range(4):
                                OP("pe", lambda e, h=h: e.transpose(out=pTb[:, h * 128:(h + 1) * 128], in_=P0[0][:, h, :], identity=cb(C_IDENT)), r=[F("P", 0), T("cbf")], w=[T("pTb", 0)])
                            for h in range(4):
                                OP("pe", lambda e, h=h: e.transpose(out=pTb[:, 512 + h * 128:512 + (h + 1) * 128], in_=Abf[:, h, :], identity=cb(C_IDENT)), r=[F("Abf"), T("cbf")], w=[T("pTb", 0)])
                            OP("dve", lambda e: e.tensor_scalar_mul(out=flat(M0[0]), in0=pTb[:, 0:512], scalar1=1.0), r=[T("pTb", 0)], w=[F("M", 0)])
                            OP("dve", lambda e: e.tensor_scalar_mul(out=flat(ATb), in0=pTb[:, 512:1024], scalar1=1.0), r=[T("pTb", 0)], w=[F("AT")])
                            OP("dve", lambda e: e.tensor_tensor(out=Y0[0][:], in0=M0[0][:], in1=c4(C_IDENT, True), op=ALU.add), r=[F("M", 0), T("cbf")], w=[F("Y", 0)])
                            cur = 0
                            for lvl in range(7):
                                nxt = 1 - cur
                                Pc, Mc, Yc = P0[cur], M0[cur], Y0[cur]
                                Pn, Mn, Yn = P0[nxt], M0[nxt], Y0[nxt]
                                tP, tM, tY = F("P", cur), F("M", cur), F("Y", cur)
                                if lvl <= 4:
                                    bM, tbM = bank("pre", (0, 1, 2))
                                    for h in range(4):
                                        OP("pe", lambda e, h=h, bM=bM, Pc=Pc, Mc=Mc: e.matmul(bM[:, h * 128:(h + 1) * 128], lhsT=Pc[:, h, :], rhs=Mc[:, h, :], start=True, stop=True), r=[tP, tM], w=[tbM])
                                if lvl <= 5:
                                    bP, tbP = bank("pre", (0, 1, 2))
                                    for h in range(4):
                                        OP("pe", lambda e, h=h, bP=bP, Pc=Pc, Mc=Mc: e.matmul(bP[:, h * 128:(h + 1) * 128], lhsT=Mc[:, h, :], rhs=Pc[:, h, :], start=True, stop=True), r=[tP, tM], w=[tbP])
                                if lvl >= 1:
                                    bY, tbY = bank("pre", (0, 1, 2))
                                    for h in range(4):
                                        OP("pe", lambda e, h=h, bY=bY, Pc=Pc, Yc=Yc: e.matmul(bY[:, h * 128:(h + 1) * 128], lhsT=Pc[:, h, :], rhs=Yc[:, h, :], start=True, stop=True), r=[tP, tY], w=[tbY])
                                if lvl <= 4:
                                    OP("act", lambda e, bM=bM, Mn=Mn: e.activation(out=flat(Mn), in_=bM[:, 0:512], func=AF.Copy), r=[tbM], w=[F("M", nxt)])
                                if lvl <= 5:
                                    OP("act", lambda e, bP=bP, Pn=Pn: e.activation(out=flat(Pn), in_=bP[:, 0:512], func=AF.Copy), r=[tbP], w=[F("P", nxt)])
                                if lvl >= 1:
                                    OP("dve", lambda e, bY=bY, Yn=Yn, Yc=Yc: e.tensor_tensor(out=flat(Yn), in0=bY[:, 0:512], in1=flat(Yc), op=ALU.add), r=[tbY, tY], w=[F("Y", nxt)])
                                else:
                                    OP("act", lambda e, Yn=Yn, Yc=Yc: e.activation(out=Yn[:], in_=Yc[:], func=AF.Copy), r=[tY], w=[F("Y", nxt)])
                                cur = nxt
                            TTb = Y0[cur]
                            tTT = F("Y", cur)
                            OP("dve", lambda e, n=n, r0=r0: e.tensor_tensor(out=VBt[:], in0=vtok[:, n], in1=bc4(btok[:, n, r0:r0 + 4]), op=ALU.mult), r=[F("vtok"), F("btok")], w=[F("VBt")])
                            for h in range(4):
                                OP("pe", lambda e, h=h, tok=tok: e.transpose(out=pTb[:, h * 128:(h + 1) * 128], in_=kT[:, h, tok], identity=cb(C_IDENT)), r=[F("kT", h), T("cbf")], w=[T("pTb", 0)])
                            OP("dve", lambda e, n=n, r0=r0: e.tensor_tensor(out=kdec[:], in0=pTb[:, 0:512].rearrange("p (h j) -> p h j", h=4), in1=bc4(ekd[:, n, r0:r0 + 4]), op=ALU.mult), r=[T("pTb", 0), SC], w=[F("kdec")])
                            bKS, tKS = bank("scan", (3, 4, 5, 6))
                            bW1, tW1 = bank("scan", (3, 4, 5, 6))
                            for h in range(4):
                                OP("pe", lambda e, h=h, bKS=bKS, tok=tok, dr=dr: e.matmul(bKS[:, h * 128:(h + 1) * 128], lhsT=kT[:, h, tok], rhs=Sbf[dr][:, h, :], start=True, stop=True), r=[F("kT", h), F("Sbf", dr)], w=[tKS])
                            for h in range(4):
                                OP("pe", lambda e, h=h, bW1=bW1, tok=tok, dr=dr: e.matmul(bW1[:, h * 128:(h + 1) * 128], lhsT=qT[:, h, tok], rhs=Sbf[dr][:, h, :], start=True, stop=True), r=[F("qT", h), F("Sbf", dr)], w=[tW1])
                            OP("dve", lambda e, bKS=bKS, n=n, r0=r0: e.tensor_tensor(out=R1[:], in0=bKS[:, 0:512].rearrange("p (h j) -> p h j", h=4), in1=bc4(nbe[:, n, r0:r0 + 4]), op=ALU.mult), r=[tKS, SC], w=[F("R1")])
                            OP("dve", lambda e: e.tensor_tensor(out=Rb[:], in0=R1[:], in1=VBt[:], op=ALU.add), r=[F("R1"), F("VBt")], w=[F("Rb")])
                            bV, tV = bank("scan", (3, 4, 5, 6))
                            for h in range(4):
                                OP("pe", lambda e, h=h, bV=bV, TTb=TTb: e.matmul(bV[:, h * 128:(h + 1) * 128], lhsT=TTb[:, h, :], rhs=Rb[:, h, :], start=True, stop=True), r=[tTT, F("Rb")], w=[tV])
                            OP("act", lambda e, bV=bV: e.activation(out=flat(vn), in_=bV[:, 0:512], func=AF.Copy), r=[tV], w=[F("vn")])
                            OP("dve", lambda e, bW1=bW1, n=n, r0=r0: e.tensor_tensor(out=O1[:], in0=bW1[:, 0:512].rearrange("p (h j) -> p h j", h=4), in1=bc4(egc[:, n, r0:r0 + 4]), op=ALU.mult), r=[tW1, SC], w=[F("O1")])
                            bO, tO = bank("scan", (3, 4, 5, 6))
                            for h in range(4):
                                OP("pe", lambda e, h=h, bO=bO: e.matmul(bO[:, h * 128:(h + 1) * 128], lhsT=O1[:, h, :], rhs=cb(C_IDENT), start=True, stop=False), r=[F("O1"), T("cbf")], w=[tO])
                                OP("pe", lambda e, h=h, bO=bO: e.matmul(bO[:, h * 128:(h + 1) * 128], lhsT=vn[:, h, :], rhs=ATb[:, h, :], start=False, stop=True), r=[F("vn"), F("AT")], w=[tO])
                            first = (step < NT // 2)
                            if first:
                                OP("act", lambda e, bO=bO, tok=tok: e.activation(out=oacc[:, :, tok], in_=bO[:, 0:512].rearrange("p (h j) -> p h j", h=4), func=AF.Copy), r=[tO], w=[F("oacc", n)])
                            else:
                                OP("dve", lambda e, bO=bO, tok=tok: e.tensor_tensor(out=oacc[:, :, tok], in0=bO[:, 0:512].rearrange("p (h j) -> p h j", h=4), in1=oacc[:, :, tok], op=ALU.add), r=[tO, F("oacc", n)], w=[F("oacc", n)])
                            bS, tS = bank("scan", (3, 4, 5, 6))
                            for h in range(4):
                                OP("pe", lambda e, h=h, bS=bS: e.matmul(bS[:, h * 128:(h + 1) * 128], lhsT=kdec[:, h, :], rhs=vn[:, h, :], start=True, stop=True), r=[F("kdec"), F("vn")], w=[tS])
                            OP("dve", lambda e, dr=dr, n=n, r0=r0: e.tensor_tensor(out=Sst[dr][:], in0=Sst[dr][:], in1=bc4(egt[:, n, r0:r0 + 4]), op=ALU.mult), r=[F("S", dr), SC], w=[F("S", dr)])
                            OP("dve", lambda e, dr=dr, bS=bS: e.tensor_tensor(out=flat(Sst[dr]), in0=bS[:, 0:512], in1=flat(Sst[dr]), op=ALU.add), r=[tS, F("S", dr)], w=[F("S", dr)])
                            OP("act", lambda e, dr=dr: e.activation(out=Sbf[dr][:], in_=Sst[dr][:], func=AF.Copy), r=[F("S", dr)], w=[F("Sbf", dr)])
                sch.barrier(scr[:, 2:3])
                _chk("loop")
                FR.clear()
                if DEBUG and s_ == 0:
                    OP("sp", lambda e: e.dma_start(out=dbg_d[:, 5], in_=qT[:]), r=[F("qT", 0)], dma="dbg")
                    OP("sp", lambda e: e.dma_start(out=dbg_d[:, 6], in_=kT[:]), r=[F("kT", 0)], dma="dbg")
                    OP("sp", lambda e: e.dma_start(out=dbgf_d, in_=oacc[:]), r=[F("oacc", 0)], dma="dbg")
                with ExitStack() as ph2:
                    def psb2(name, shape, dt=F32, ph2=ph2):
                        return ph2.enter_context(nc.sbuf_tensor(name + "_s%d" % s_, shape, dt))
                    zs = psb2("zs", [128, 4, S], BF16)
                    sqb = psb2("sqb2", [128, 512], BF16)
                    rn = psb2("rn2", [128, 512])
                    yt = psb2("yt", [128, 512])
                    for zt in range(2):
                        wb, wtr = wnext(("w_in", 1536 + zt * 256))

                        def ev(ci, tb, bk, btr, zt=zt):
                            OP("act", lambda e: e.activation(out=zs[:, zt * 2 + ci, tb * 512:(tb + 1) * 512], in_=bk[:, 0:512], func=AF.Silu), r=[btr], w=[F("zs", zt * 2 + ci)])
                        proj_fm(wb, wtr, 8, [(0, 128), (128, 128)], lambda kc, tb: hT[:, kc, tb * 512:(tb + 1) * 512], lambda kc, tb: T("hT", tb), ev)
                    oall = [F("oacc", n) for n in range(NT)]
                    for h in range(4):
                        for tb in range(4):
                            sl = slice(tb * 512, (tb + 1) * 512)
                            sumsq_rn(oacc[:, h, sl], oall, 1.0 / 128.0, cb(C_ONES), rn[:], F("rn"), sqb[:], F("sqb"), ("aux", (4, 5)))
                            OP("dve", lambda e, h=h, sl=sl, rn=rn: e.tensor_tensor(out=yt[:], in0=oacc[:, h, sl], in1=rn[:], op=ALU.mult), r=oall + [F("rn")], w=[F("yt")])
                            OP("dve", lambda e, h=h, sl=sl: e.scalar_tensor_tensor(out=ydn[:, h, sl], in0=yt[:], scalar=vec[:, V_DNW:V_DNW + 1], in1=zs[:, h, sl], op0=ALU.mult, op1=ALU.mult),
                               r=[F("yt"), F("zs", h), T("vec")], w=[F("ydn")])
                    if DEBUG and s_ == 0:
                        OP("sp", lambda e: e.dma_start(out=dbg_d[:, 0], in_=ydn[:]), r=[F("ydn")], dma="dbg")
            sch.barrier(scr[:, 0:1])
            TYS = sch.fresh()
            mrg = seqscope.enter_context(nc.sbuf_tensor("mrg_s%d" % s_, [128, 8, S], BF16))
            for dc_ in range(8):
                for tb_ in range(4):
                    TK[("mrg", dc_, tb_)] = sch.fresh()
            if True:

                def merge(nb, ysT, ystr, first):
                    with ExitStack() as ph3:
                        sg = ph3.enter_context(nc.sbuf_tensor("sg_%d_%d" % (s_, nb), [128, 512], F32))
                        ct = ph3.enter_context(nc.sbuf_tensor("ct_%d_%d" % (s_, nb), [128, 512], BF16))
                        tsg, tct = sch.fresh(), sch.fresh()
                        for dt in range(4):
                            wg, wgtr = wnext(("w_in", 3344 + nb * 1024 + dt * 256))
                            wbr, wbtr = wnext(("w_br", dt * 256))
                            for cc in range(2):
                                dc = dt * 2 + cc
                                for tb in range(4):
                                    sl = slice(tb * 512, (tb + 1) * 512)
                                    bA, tA = bank("proj", (0, 1, 2, 3))
                                    for kc in range(8):
                                        OP("pe", lambda e, bA=bA, kc=kc, cc=cc, sl=sl, wg=wg: e.matmul(bA[:, 0:512], lhsT=wg[:, kc, cc * 128:(cc + 1) * 128], rhs=hT[:, kc, sl], start=(kc == 0), stop=(kc == 7)), r=[wgtr, T("hT", tb)], w=[tA])
                                    bB, tB = bank("proj", (0, 1, 2, 3))
                                    for kc in range(4):
                                        OP("pe", lambda e, bB=bB, kc=kc, cc=cc, sl=sl, wbr=wbr: e.matmul(bB[:, 0:512], lhsT=wbr[:, kc, cc * 128:(cc + 1) * 128], rhs=ysT[:, kc, sl], start=(kc == 0), stop=(kc == 3)), r=[wbtr, ystr], w=[tB])
                                    OP("act", lambda e, bA=bA: e.activation(out=sg[:], in_=bA[:, 0:512], func=AF.Sigmoid), r=[tA], w=[tsg])
                                    if first:
                                        OP("dve", lambda e, bB=bB, dc=dc, sl=sl: e.tensor_tensor(out=mrg[:, dc, sl], in0=bB[:, 0:512], in1=sg[:], op=ALU.mult), r=[tB, tsg], w=[T("mrg", dc, tb)])
                                    else:
                                        OP("dve", lambda e, bB=bB: e.tensor_tensor(out=ct[:], in0=bB[:, 0:512], in1=sg[:], op=ALU.mult), r=[tB, tsg], w=[tct])
                                        OP("dve", lambda e, dc=dc, sl=sl: e.tensor_tensor(out=mrg[:, dc, sl], in0=mrg[:, dc, sl], in1=ct[:], op=ALU.add), r=[tct, T("mrg", dc, tb)], w=[T("mrg", dc, tb)])
                _chk("dn")
                merge(0, ysb, TYS, True)
            sch.barrier(scr[:, 3:4])
            _chk("m0")

            with ExitStack() as ph:
                def psb(name, shape, dt=F32, ph=ph):
                    return ph.enter_context(nc.sbuf_tensor(name + "_s%d" % s_, shape, dt))
                FR = {}

                def F(*key):
                    if key not in FR:
                        FR[key] = sch.fresh()
                    return FR[key]
                AQ = psb("AQ", [128, 4, S], BF16)
                AK = psb("AK", [128, 2, S], BF16)
                VX = psb("VX", [128, 4, NT, 128], BF16)
                ropes = psb("ropes", [128, 2, S])
                yat = ysb
                sqb = psb("sqb3", [128, 512], BF16)
                rn = psb("rn3", [128, 512])
                aqn = psb("aqn", [128, 512], BF16)
                r1 = psb("r1", [128, 512])
                r2 = psb("r2", [128, 512])
                PT = [psb("PT%d" % i, [128, 512], BF16) for i in range(4)]
                UA = psb("UA", [128, 512])
                UB = psb("UB", [128, 512])
                rd = psb("rd", [128, 512])
                OP("sp", lambda e: e.dma_start(out=ropes[:], in_=rope_d), w=[F("ropes")], dma="misc")
                OP("dve", lambda e: e.memset(VX[:], 1.0), w=[F("VX")])

                def qk_evac(dst_ap_fn, dst_trk, nwcol):
                    def ev(ci, tb, bk, btr):
                        sl = slice(tb * 512, (tb + 1) * 512)
                        sumsq_rn(bk[:, 0:512], [btr], 1.0 / 64.0, cb(C_BD64), rn[:], F("rn"), sqb[:], F("sqb"), ("aux", (4, 5)))
                        OP("dve", lambda e, rn=rn: e.scalar_tensor_tensor(out=aqn[:], in0=bk[:, 0:512], scalar=vec[:, nwcol:nwcol + 1], in1=rn[:], op0=ALU.mult, op1=ALU.mult),
                           r=[btr, F("rn"), T("vec")], w=[F("aqn")])
                        bR, tR = bank("aux", (4, 5))
                        OP("pe", lambda e: e.matmul(bR[:, 0:512], lhsT=cb(C_PERM), rhs=aqn[:], start=True, stop=True), r=[F("aqn"), T("cbf")], w=[tR])
                        OP("dve", lambda e: e.tensor_tensor(out=r1[:], in0=aqn[:], in1=ropes[:, 0, sl], op=ALU.mult), r=[F("aqn"), F("ropes")], w=[F("r1")])
                        OP("dve", lambda e: e.tensor_tensor(out=r2[:], in0=bR[:, 0:512], in1=ropes[:, 1, sl], op=ALU.mult), r=[tR, F("ropes")], w=[F("r2")])
                        OP("dve", lambda e: e.tensor_tensor(out=dst_ap_fn(ci, sl), in0=r1[:], in1=r2[:], op=ALU.add), r=[F("r1"), F("r2")], w=[dst_trk])
                    return ev
                for qt in range(2):
                    wb, wtr = wnext(("w_in", 2064 + qt * 256))
                    proj_fm(wb, wtr, 8, [(0, 128), (128, 128)], lambda kc, tb: hT[:, kc, tb * 512:(tb + 1) * 512], lambda kc, tb: T("hT", tb),
                            qk_evac(lambda ci, sl, qt=qt: AQ[:, qt * 2 + ci, sl], F("AQ"), V_QNW))
                wb, wtr = wnext(("kdup", 2576))
                proj_fm(wb, wtr, 8, [(0, 128), (128, 128)], lambda kc, tb: hT[:, kc, tb * 512:(tb + 1) * 512], lambda kc, tb: T("hT", tb),
                        qk_evac(lambda ci, sl: AK[:, ci, sl], F("AK"), V_KNW))
                wb, wtr = wnext(("w_in", 2704))
                for n0 in range(0, NT, 4):
                    bk, btr = bank("proj", (0, 1, 2, 3))
                    for j in range(4):
                        n = n0 + j
                        for kc in range(8):
                            OP("pe", lambda e, bk=bk, j=j, n=n, kc=kc, wb=wb: e.matmul(bk[:, j * 128:(j + 1) * 128], lhsT=hT[:, kc, n * 128:(n + 1) * 128], rhs=wb[:, kc, 0:128], start=(kc == 0), stop=(kc == 7)),
                               r=[wtr, T("hT", n // 4)], w=[btr])
                    for g in range(2):
                        src = lambda bk=bk, g=g: bk[:, 0:512].rearrange("p (a b) -> p a b", a=4)[:, :, g * 64:(g + 1) * 64]
                        OP("act", lambda e, g=g, n0=n0, src=src: e.activation(out=VX[:, g * 2 + 0, n0:n0 + 4, 0:64], in_=src(), func=AF.Copy), r=[btr], w=[F("VX")])
                        OP("dve", lambda e, g=g, n0=n0, src=src: e.tensor_copy(out=VX[:, g * 2 + 1, n0:n0 + 4, 64:128], in_=src()), r=[btr], w=[F("VX")])
                for c in range(4):
                    g = c // 2
                    for qb in range(4):
                        sl = slice(qb * 512, (qb + 1) * 512)
                        bOA, tOA = banks[4], T("bank", 4)
                        bOB, tOB = banks[5], T("bank", 5)
                        for n in range(NT):
                            tok = slice(n * 128, (n + 1) * 128)
                            b1, t1_ = bank("sc", (0, 1, 2, 3))
                            b2, t2_ = bank("sc", (0, 1, 2, 3))
                            OP("pe", lambda e, b1=b1, g=g, tok=tok, c=c, sl=sl: e.matmul(b1[:, 0:512], lhsT=AK[0:64, g, tok], rhs=AQ[0:64, c, sl], start=True, stop=True), r=[F("AK"), F("AQ")], w=[t1_])
                            OP("pe", lambda e, b2=b2, g=g, tok=tok, c=c, sl=sl: e.matmul(b2[:, 0:512], lhsT=AK[64:128, g, tok], rhs=AQ[64:128, c, sl], start=True, stop=True), r=[F("AK"), F("AQ")], w=[t2_])
                            p1, p2 = PT[(n % 2) * 2], PT[(n % 2) * 2 + 1]
                            tp1, tp2 = F("PT", (n % 2) * 2), F("PT", (n % 2) * 2 + 1)
                            OP("act", lambda e, b1=b1, p1=p1: e.activation(out=p1[:], in_=b1[:, 0:512], func=AF.Exp, scale=0.125), r=[t1_], w=[tp1])
                            OP("act", lambda e, b2=b2, p2=p2: e.activation(out=p2[:], in_=b2[:, 0:512], func=AF.Exp, scale=0.125), r=[t2_], w=[tp2])
                            OP("pe", lambda e, g=g, n=n, p1=p1: e.matmul(bOA[:, 0:512], lhsT=VX[:, g * 2 + 0, n, :], rhs=p1[:], start=(n == 0), stop=(n == NT - 1)), r=[F("VX"), tp1], w=[tOA])
                            OP("pe", lambda e, g=g, n=n, p2=p2: e.matmul(bOB[:, 0:512], lhsT=VX[:, g * 2 + 1, n, :], rhs=p2[:], start=(n == 0), stop=(n == NT - 1)), r=[F("VX"), tp2], w=[tOB])
                        OP("act", lambda e: e.activation(out=UA[:], in_=bOA[:, 0:512], func=AF.Copy), r=[tOA], w=[F("UA")])
                        OP("dve", lambda e: e.tensor_copy(out=UB[:], in_=bOB[:, 0:512]), r=[tOB], w=[F("UB")])
                        bD, tD = banks[6], T("bank", 6)
                        OP("pe", lambda e: e.matmul(bD[:, 0:512], lhsT=cf(C_SWLO), rhs=UA[:], start=True, stop=False), r=[F("UA"), T("cst")], w=[tD])
                        OP("pe", lambda e: e.matmul(bD[:, 0:512], lhsT=cf(C_SWHI), rhs=UB[:], start=False, stop=True), r=[F("UB"), T("cst")], w=[tD])
                        OP("dve", lambda e, rd=rd: e.reciprocal(out=rd[:], in_=bD[:, 0:512]), r=[tD], w=[F("rd")])
                        OP("dve", lambda e, c=c, sl=sl, rd=rd: e.tensor_tensor(out=yat[0:64, c, sl], in0=UA[0:64, :], in1=rd[0:64, :], op=ALU.mult), r=[F("UA"), F("rd")], w=[F("yat")])
                        OP("dve", lambda e, c=c, sl=sl, rd=rd: e.tensor_tensor(out=yat[64:128, c, sl], in0=UB[64:128, :], in1=rd[64:128, :], op=ALU.mult), r=[F("UB"), F("rd")], w=[F("yat")])
                if DEBUG and s_ == 0:
                    OP("sp", lambda e: e.dma_start(out=dbg_d[:, 1], in_=yat[:]), r=[F("yat")], dma="dbg")
                    OP("sp", lambda e: e.dma_start(out=dbg_d[:, 3], in_=AQ[:]), r=[F("AQ")], dma="dbg")
                    OP("sp", lambda e: e.dma_start(out=dbg_d[:, 4, 0:2], in_=AK[:]), r=[F("AK")], dma="dbg")
                _chk("gqa")
                merge(1, yat, F("yat"), False)
            sch.barrier(scr[:, 0:1])

            with ExitStack() as ph:
                def psb(name, shape, dt=F32, ph=ph):
                    return ph.enter_context(nc.sbuf_tensor(name + "_s%d" % s_, shape, dt))
                FR = {}

                def F(*key):
                    if key not in FR:
                        FR[key] = sch.fresh()
                    return FR[key]
                memT = psb("memT", [128, 8, 256], BF16)
                mkT = psb("mkT", [128, 4, 256], BF16)
                mv = psb("mv", [128, 2, 512], BF16)
                xqT = psb("xqT", [128, 4, S], BF16)
                yx = ysb
                PTx = [psb("PTx%d" % i, [128, 512], BF16) for i in range(2)]
                rd = psb("rdx", [128, 512])
                xs = [psb("xsM%d" % i, [128, D]) for i in range(2)]
                rmsnorm_tokmajor(lambda t: mem_d[s_, t * 128:(t + 1) * 128, :], 2, memT, lambda t: F("memT"), V_NW_MEM, "m", psb, xs, F)
                for kt in range(2):
                    wb, wtr = wnext(("w_kv", kt * 256))
                    for ci in range(2):
                        hh = kt * 2 + ci
                        bk, btr = bank("proj", (0, 1, 2, 3))
                        for kc in range(8):
                            OP("pe", lambda e, bk=bk, kc=kc, ci=ci, wb=wb: e.matmul(bk[:, 0:256], lhsT=wb[:, kc, ci * 128:(ci + 1) * 128], rhs=memT[:, kc, :], start=(kc == 0), stop=(kc == 7)), r=[wtr, F("memT")], w=[btr])
                        OP("act", lambda e, bk=bk, hh=hh: e.activation(out=mkT[:, hh, :], in_=bk[:, 0:256], func=AF.Copy), r=[btr], w=[F("mkT")])
                for vt in range(2):
                    wb, wtr = wnext(("w_kv", 512 + vt * 256))
                    for mt in range(2):
                        bk, btr = bank("proj", (0, 1, 2, 3))
                        for kc in range(8):
                            OP("pe", lambda e, bk=bk, kc=kc, mt=mt, wb=wb: e.matmul(bk[:, 0:256], lhsT=memT[:, kc, mt * 128:(mt + 1) * 128], rhs=wb[:, kc, 0:256], start=(kc == 0), stop=(kc == 7)), r=[wtr, F("memT")], w=[btr])
                        OP("act", lambda e, bk=bk, mt=mt, vt=vt: e.activation(out=mv[:, mt, vt * 256:(vt + 1) * 256], in_=bk[:, 0:256], func=AF.Copy), r=[btr], w=[F("mv")])
                for qt in range(2):
                    wb, wtr = wnext(("w_in", 2832 + qt * 256))

                    def ev(ci, tb, bk, btr, qt=qt):
                        OP("act", lambda e: e.activation(out=xqT[:, qt * 2 + ci, tb * 512:(tb + 1) * 512], in_=bk[:, 0:512], func=AF.Copy), r=[btr], w=[F("xqT")])
                    proj_fm(wb, wtr, 8, [(0, 128), (128, 128)], lambda kc, tb: hT[:, kc, tb * 512:(tb + 1) * 512], lambda kc, tb: T("hT", tb), ev)
                for h in range(4):
                    for qb in range(4):
                        sl = slice(qb * 512, (qb + 1) * 512)
                        bO, tO = banks[4], T("bank", 4)
                        bDn, tDn = banks[5], T("bank", 5)
                        for mt in range(2):
                            b1, t1_ = bank("sc", (0, 1, 2, 3))
                            OP("pe", lambda e, b1=b1, h=h, mt=mt, sl=sl: e.matmul(b1[:, 0:512], lhsT=mkT[:, h, mt * 128:(mt + 1) * 128], rhs=xqT[:, h, sl], start=True, stop=True), r=[F("mkT"), F("xqT")], w=[t1_])
                            OP("act", lambda e, b1=b1, mt=mt: e.activation(out=PTx[mt][:], in_=b1[:, 0:512], func=AF.Exp, scale=128.0 ** -0.5), r=[t1_], w=[F("PTx", mt)])
                            OP("pe", lambda e, h=h, mt=mt, bO=bO: e.matmul(bO[:, 0:512], lhsT=mv[:, mt, h * 128:(h + 1) * 128], rhs=PTx[mt][:], start=(mt == 0), stop=(mt == 1)), r=[F("mv"), F("PTx", mt)], w=[tO])
                            OP("pe", lambda e, mt=mt: e.matmul(bDn[:, 0:512], lhsT=cb(C_ONES), rhs=PTx[mt][:], start=(mt == 0), stop=(mt == 1)), r=[T("cbf"), F("PTx", mt)], w=[tDn])
                        OP("dve", lambda e, rd=rd: e.reciprocal(out=rd[:], in_=bDn[:, 0:512]), r=[tDn], w=[F("rd")])
                        OP("dve", lambda e, h=h, sl=sl, bO=bO, rd=rd: e.tensor_tensor(out=yx[:, h, sl], in0=bO[:, 0:512], in1=rd[:], op=ALU.mult), r=[tO, F("rd")], w=[F("yx")])
                if DEBUG and s_ == 0:
                    OP("sp", lambda e: e.dma_start(out=dbg_d[:, 2], in_=yx[:]), r=[F("yx")], dma="dbg")
                _chk("xat")
                merge(2, yx, F("yx"), False)
            sch.barrier(scr[:, 1:2])

            with ExitStack() as ph:
                def psb(name, shape, dt=F32, ph=ph):
                    return ph.enter_context(nc.sbuf_tensor(name + "_s%d" % s_, shape, dt))
                FR = {}

                def F(*key):
                    if key not in FR:
                        FR[key] = sch.fresh()
                    return FR[key]
                xT = psb("xT", [128, 8, 1024])
                xs = [psb("xsC%d" % i, [128, D]) for i in range(2)]
                hfT = hT[:, :, 0:1024]
                uT = hT[:, :, 1024:2048]
                sq8 = psb("sq8", [128, 8, 512], BF16)
                rn = psb("rn4", [128, 512])
                rl = [psb("rl%d" % i, [128, 512]) for i in range(2)]
                yo = psb("yo", [128, 8, 128])
                ot = [psb("ot%d" % i, [128, D]) for i in range(2)]
                for hs in range(2):
                    for t8 in range(8):
                        t = hs * 8 + t8
                        xt = xs[t % 2]
                        xtr = F("xs", t % 2)
                        OP("sp", lambda e, xt=xt, t=t: e.dma_start(out=xt[:], in_=x_d[s_, t * 128:(t + 1) * 128, :]), w=[xtr], dma="x%d" % (t % 2))
                        for half in range(2):
                            bk, btr = bank("aux", (4, 5, 6))
                            for j in range(4):
                                c = half * 4 + j
                                OP("pe", lambda e, bk=bk, j=j, c=c, xt=xt: e.transpose(out=bk[:, j * 128:(j + 1) * 128], in_=xt[:, c * 128:(c + 1) * 128], identity=cf(C_IDENT)), r=[xtr, T("cst")], w=[btr])
                            OP("act" if half == 0 else "dve", lambda e, bk=bk, half=half, t8=t8: (e.activation(out=xT[:, half * 4:half * 4 + 4, t8 * 128:(t8 + 1) * 128], in_=bk[:, 0:512].rearrange("p (a b) -> p a b", a=4), func=AF.Copy) if half == 0 else e.tensor_copy(out=xT[:, half * 4:half * 4 + 4, t8 * 128:(t8 + 1) * 128], in_=bk[:, 0:512].rearrange("p (a b) -> p a b", a=4))),
                               r=[btr], w=[F("xT", hs, t8 // 4)])
                    for dt in range(4):
                        wb, wtr = wnext(("w_out", dt * 256))
                        for cc in range(2):
                            dc = dt * 2 + cc
                            for tbh in range(2):
                                tb = hs * 2 + tbh
                                bk, btr = bank("proj", (0, 1, 2, 3))
                                for kc in range(8):
                                    OP("pe", lambda e, bk=bk, kc=kc, cc=cc, tb=tb, wb=wb: e.matmul(bk[:, 0:512], lhsT=wb[:, kc, cc * 128:(cc + 1) * 128], rhs=mrg[:, kc, tb * 512:(tb + 1) * 512], start=(kc == 0), stop=(kc == 7)), r=[wtr, T("mrg", kc, tb)], w=[btr])
                                OP("dve", lambda e, bk=bk, dc=dc, tbh=tbh: e.tensor_tensor(out=xT[:, dc, tbh * 512:(tbh + 1) * 512], in0=bk[:, 0:512], in1=xT[:, dc, tbh * 512:(tbh + 1) * 512], op=ALU.add), r=[btr, F("xT", hs, tbh)], w=[F("xT", hs, tbh)])

                    def fm_norm(tbh, nwoff, dst_fn, dst_trk, eng2):
                        sl = slice(tbh * 512, (tbh + 1) * 512)
                        OP("act", lambda e: e.activation(out=sq8[:], in_=xT[:, :, sl], func=AF.Square), r=[F("xT", hs, tbh)], w=[F("sq8")])
                        bk, btr = bank("aux", (4, 5, 6))
                        for c in range(8):
                            OP("pe", lambda e, bk=bk, c=c: e.matmul(bk[:, 0:512], lhsT=cb(C_ONES), rhs=sq8[:, c, :], start=(c == 0), stop=(c == 7)), r=[F("sq8"), T("cbf")], w=[btr])
                        OP("act", lambda e, bk=bk: e.activation(out=rtmp[:], in_=bk[:, 0:512], func=AF.Sqrt, bias=EPS, scale=1.0 / 1024.0), r=[btr], w=[T("rtmp")])
                        OP("dve", lambda e, rn=rn: e.reciprocal(out=rn[:], in_=rtmp[:]), r=[T("rtmp")], w=[F("rn")])
                        for c in range(8):
                            OP("dve", lambda e, c=c, rn=rn: e.scalar_tensor_tensor(out=dst_fn(c, sl), in0=xT[:, c, sl], scalar=vec[:, nwoff + c:nwoff + c + 1], in1=rn[:], op0=ALU.mult, op1=ALU.mult),
                               r=[F("xT", hs, tbh), F("rn"), T("vec")], w=[dst_trk])
                    for tbh in range(2):
                        fm_norm(tbh, V_NW_FFN, lambda c, sl: hfT[:, c, sl], F("hfT", tbh), "dve")
                    for fg in range(4):
                        for ut in range(4):
                            wb, wtr = wnext(("w_up", fg * 1024 + ut * 256))
                            for cc in range(2):
                                fc = ut * 2 + cc
                                for tbh in range(2):
                                    sl = slice(tbh * 512, (tbh + 1) * 512)
                                    bk, btr = bank("proj", (0, 1, 2, 3))
                                    for kc in range(8):
                                        OP("pe", lambda e, bk=bk, kc=kc, cc=cc, sl=sl, wb=wb: e.matmul(bk[:, 0:512], lhsT=wb[:, kc, cc * 128:(cc + 1) * 128], rhs=hfT[:, kc, sl], start=(kc == 0), stop=(kc == 7)), r=[wtr, F("hfT", tbh)], w=[btr])
                                    rr = rl[(fc * 2 + tbh) % 2]
                                    trr = F("rl", (fc * 2 + tbh) % 2)
                                    OP("act", lambda e, bk=bk, rr=rr: e.activation(out=rr[:], in_=bk[:, 0:512], func=AF.Relu), r=[btr], w=[trr])
                                    OP("dve", lambda e, rr=rr, fc=fc, sl=sl: e.tensor_tensor(out=uT[:, fc, sl], in0=rr[:], in1=rr[:], op=ALU.mult), r=[trr], w=[F("uT", fc, tbh)])
                        for dt in range(4):
                            wb, wtr = wnext(("w_dn", dt * 256))
                            for cc in range(2):
                                dc = dt * 2 + cc
                                for tbh in range(2):
                                    sl = slice(tbh * 512, (tbh + 1) * 512)
                                    bk, btr = bank("proj", (0, 1, 2, 3))
                                    for kc in range(8):
                                        OP("pe", lambda e, bk=bk, kc=kc, cc=cc, sl=sl, wb=wb: e.matmul(bk[:, 0:512], lhsT=wb[:, kc, cc * 128:(cc + 1) * 128], rhs=uT[:, kc, sl], start=(kc == 0), stop=(kc == 7)), r=[wtr, F("uT", kc, tbh)], w=[btr])
                                    OP("dve", lambda e, bk=bk, dc=dc, sl=sl: e.tensor_tensor(out=xT[:, dc, sl], in0=bk[:, 0:512], in1=xT[:, dc, sl], op=ALU.add), r=[btr, F("xT", hs, tbh)], w=[F("xT", hs, tbh)])
                    for tbh in range(2):
                        sl = slice(tbh * 512, (tbh + 1) * 512)
                        OP("act", lambda e, sl=sl: e.activation(out=sq8[:], in_=xT[:, :, sl], func=AF.Square), r=[F("xT", hs, tbh)], w=[F("sq8")])
                        bk, btr = bank("aux", (4, 5, 6))
                        for c in range(8):
                            OP("pe", lambda e, bk=bk, c=c: e.matmul(bk[:, 0:512], lhsT=cb(C_ONES), rhs=sq8[:, c, :], start=(c == 0), stop=(c == 7)), r=[F("sq8"), T("cbf")], w=[btr])
                        OP("act", lambda e, bk=bk: e.activation(out=rtmp[:], in_=bk[:, 0:512], func=AF.Sqrt, bias=EPS, scale=1.0 / 1024.0), r=[btr], w=[T("rtmp")])
                        OP("dve", lambda e, rn=rn: e.reciprocal(out=rn[:], in_=rtmp[:]), r=[T("rtmp")], w=[F("rn")])
                        for j in range(4):
                            t = hs * 8 + tbh * 4 + j
                            tsl = slice(tbh * 512 + j * 128, tbh * 512 + (j + 1) * 128)
                            for c in range(8):
                                OP("dve", lambda e, c=c, tsl=tsl, j=j, rn=rn: e.scalar_tensor_tensor(out=yo[:, c, :], in0=xT[:, c, tsl], scalar=vec[:, V_NW_FIN + c:V_NW_FIN + c + 1], in1=rn[:, j * 128:(j + 1) * 128], op0=ALU.mult, op1=ALU.mult),
                                   r=[F("xT", hs, tbh), F("rn"), T("vec")], w=[F("yo")])
                            o_t = ot[t % 2]
                            to_t = F("ot", t % 2)
                            for half in range(2):
                                bk2, btr2 = bank("aux", (4, 5, 6))
                                for jj in range(4):
                                    c = half * 4 + jj
                                    OP("pe", lambda e, bk2=bk2, jj=jj, c=c: e.transpose(out=bk2[:, jj * 128:(jj + 1) * 128], in_=yo[:, c, :], identity=cf(C_IDENT)), r=[F("yo"), T("cst")], w=[btr2])
                                OP("act", lambda e, bk2=bk2, half=half, o_t=o_t: e.activation(out=o_t[:, half * 512:(half + 1) * 512], in_=bk2[:, 0:512], func=AF.Copy), r=[btr2], w=[to_t])
                            OP("sp", lambda e, o_t=o_t, t=t: e.dma_start(out=out_d[s_, t * 128:(t + 1) * 128, :], in_=o_t[:]), r=[to_t], dma="out%d" % (t % 2))
            sch.barrier(scr[:, 2:3])
            for tb_ in range(4):
                TK[("hT", tb_)] = sch.fresh()
            seqscope.close()

        SEQSC = []
        try:
            body()
        except _Stop:
            for sc_ in reversed(SEQSC):
                sc_.close()
        assert STOP or wstate["use"] == len(wplan), (wstate, len(wplan))
        sch.finalize()
        print("kernel: ops", len(sch.ops), {e: sum(1 for o in sch.ops if o.eng == e) for e in Sched.ENGS}, flush=True)
        sch.emit(nc, final_waits=["out0", "out1", "dbg"])
    return nc


_CACHE = {}


def kernel(**inputs):
    inp = {k: np.asarray(v) for k, v in inputs.items()}
    if "nc" not in _CACHE:
        _CACHE["nc"] = build_program()
        _CACHE["consts"] = make_consts()
    nc = _CACHE["nc"]
    cst, rope = _CACHE["consts"]
    vec = make_vec(inp)
    shared = {
        "w_in": np.ascontiguousarray(inp["w_in"][0]),
        "w_mem_kv": np.ascontiguousarray(inp["w_mem_kv"][0]),
        "w_branch": np.ascontiguousarray(inp["w_branch"][0].reshape(1536, D)),
        "w_out": np.ascontiguousarray(inp["w_out"][0]),
        "w_up": np.ascontiguousarray(inp["w_up"][0]),
        "w_down": np.ascontiguousarray(inp["w_down"][0]),
        "cst": cst, "rope": rope, "vec": vec,
    }
    in_maps = []
    ncores = int(os.environ.get("KCORES", "8"))
    for c in range(ncores):
        m = dict(shared)
        m["x"] = np.ascontiguousarray(inp["x"][c * NSEQ:(c + 1) * NSEQ])
        m["mem"] = np.ascontiguousarray(inp["mem"][c * NSEQ:(c + 1) * NSEQ])
        in_maps.append(m)
    res = run_bass_kernel_spmd(nc, in_maps, core_ids=list(range(ncores)))
    _CACHE["last"] = res
    out = np.concatenate([np.asarray(r["out"]) for r in res.results], axis=0)
    if ncores < 8:
        out = np.concatenate([out, np.zeros((16 - out.shape[0], S, D), np.float32)], axis=0)
    return out.astype(np.float32)
```

```python
import os
import numpy as np
from contextlib import ExitStack
from collections import defaultdict
import concourse.bass as bass
import concourse.mybir as mybir
from concourse.bass_utils import run_bass_kernel_spmd

F32 = mybir.dt.float32
BF16 = mybir.dt.bfloat16
AF = mybir.ActivationFunctionType
ALU = mybir.AluOpType

NSEQ = 2
S = 2048
D = 1024
NT = 16
EPS = 1e-6
DEBUG = bool(os.environ.get("KDEBUG", ""))
STOP = os.environ.get("KSTOP", "")
STRICT = not os.environ.get("KLOOSE")


class _Stop(Exception):
    pass


_ST = {"stopped": False}


def _chk(tag):
    if STOP == tag:
        _ST["stopped"] = True


class Trk:
    __slots__ = ("last_w", "readers")

    def __init__(self):
        self.last_w = None
        self.readers = {}


class Op:
    __slots__ = ("eng", "fn", "deps", "signal", "dma_sem", "dma_val", "idx", "sigval")

    def __init__(self, eng, fn):
        self.eng = eng
        self.fn = fn
        self.deps = {}
        self.signal = False
        self.dma_sem = None
        self.dma_val = 0
        self.idx = -1
        self.sigval = 0


class Sched:
    ENGS = ("pe", "act", "dve", "pool", "sp")

    def __init__(self, nc):
        self.nc = nc
        self.ops = []
        self.dma_sems = {}
        self.eng_sems = {}
        self.last_on = {}

    def op(self, eng, fn, reads=(), writes=(), dma=None):
        o = Op(eng, fn)
        o.idx = len(self.ops)
        for t in reads:
            if t.last_w is not None:
                o.deps[t.last_w] = True
        for t in writes:
            if t.last_w is not None:
                o.deps.setdefault(t.last_w, False)
            for r in t.readers.values():
                o.deps.setdefault(r, False)
        for t in reads:
            t.readers[eng if dma is None else ("dma", dma)] = o.idx
        for t in writes:
            t.last_w = o.idx
            t.readers = {}
        o.deps.pop(o.idx, None)
        if dma is not None:
            ent = self.dma_sems[dma]
            ent[1] += 16
            o.dma_sem = dma
            o.dma_val = ent[1]
            self.last_on[("dma", dma)] = o.idx
        else:
            self.last_on[eng] = o.idx
        self.ops.append(o)
        return o

    def barrier(self, scratch_ap):
        if _ST["stopped"]:
            return None
        o = Op("dve", lambda e: e.memset(scratch_ap, 0.0))
        o.idx = len(self.ops)
        for k, v in self.last_on.items():
            o.deps[v] = True
        self.last_on["dve"] = o.idx
        self.ops.append(o)
        self.bar = o.idx
        return o.idx

    def fresh(self):
        t = Trk()
        t.last_w = getattr(self, "bar", None)
        return t

    def new_dma_sem(self, name, handle):
        self.dma_sems[name] = [handle, 0]

    def _skip(self, p, o):
        return p.dma_sem is None and o.dma_sem is None and p.eng == o.eng

    def finalize(self):
        ops = self.ops
        for o in ops:
            for d, raw in o.deps.items():
                p = ops[d]
                if p.dma_sem is None:
                    if self._skip(p, o) and (o.eng == "pe" or (not raw and not STRICT)):
                        continue
                    p.signal = True
        cnt = {e: 0 for e in self.ENGS}
        for o in ops:
            if o.dma_sem is None and o.signal:
                cnt[o.eng] += 1
                o.sigval = cnt[o.eng]

    def emit(self, nc, final_waits=(), max_pe=int(os.environ.get("KMAXPE", "4000"))):
        ops = self.ops
        sems = self.eng_sems
        dma_sems = self.dma_sems
        segments = []
        cur = []
        npe = 0
        for o in ops:
            cur.append(o)
            if o.eng == "pe":
                npe += 1
            if npe >= max_pe or len(cur) >= 4 * max_pe:
                segments.append(cur)
                cur = []
                npe = 0
        if cur:
            segments.append(cur)
        waited = {e: {} for e in self.ENGS}
        self._skipfn = self._skip

        def run(engname, eng, seg_ops, last):
            wd = waited[engname]
            for o in seg_ops:
                need = {}
                for d, raw in o.deps.items():
                    p = ops[d]
                    if p.dma_sem is not None:
                        key = ("d", p.dma_sem)
                        val = p.dma_val
                    else:
                        if self._skip(p, o) and (engname == "pe" or (not raw and not STRICT)):
                            continue
                        key = ("e", p.eng)
                        val = p.sigval
                    if need.get(key, 0) < val:
                        need[key] = val
                for key, val in need.items():
                    if wd.get(key, 0) >= val:
                        continue
                    wd[key] = val
                    h = dma_sems[key[1]][0] if key[0] == "d" else sems[key[1]]
                    eng.wait_ge(h, val)
                if o.fn is None:
                    continue
                ins = o.fn(eng)
                if o.dma_sem is not None:
                    ins.then_inc(dma_sems[o.dma_sem][0], 16)
                elif o.signal:
                    ins.then_inc(sems[engname], 1)
            if engname == "sp" and last:
                for name in final_waits:
                    h, c = dma_sems[name]
                    if c > 0:
                        eng.wait_ge(h, c)

        for si, seg in enumerate(segments):
            last = (si == len(segments) - 1)
            per_eng = {e: [o for o in seg if o.eng == e] for e in self.ENGS}
            with nc.Block() as block:
                if per_eng["pe"]:
                    block.tensor(lambda e, l=per_eng["pe"]: run("pe", e, l, last))
                if per_eng["act"]:
                    block.scalar(lambda e, l=per_eng["act"]: run("act", e, l, last))
                if per_eng["dve"]:
                    block.vector(lambda e, l=per_eng["dve"]: run("dve", e, l, last))
                if per_eng["pool"]:
                    block.gpsimd(lambda e, l=per_eng["pool"]: run("pool", e, l, last))
                if per_eng["sp"] or last:
                    block.sync(lambda e, l=per_eng["sp"]: run("sp", e, l, last))
        print("kernel: blocks", len(segments), flush=True)


C_IDENT, C_UF, C_UB, C_ONES, C_MASKF, C_MASKB, C_STRF, C_STRB, C_PERM, C_BD64, C_SWLO, C_SWHI, C_NEG1 = range(13)
NCST = 13


def make_consts():
    i = np.arange(128)
    c = np.zeros((NCST, 128, 128), np.float32)
    c[C_IDENT] = np.eye(128)
    c[C_UF] = (i[:, None] <= i[None, :])
    c[C_UB] = (i[:, None] >= i[None, :])
    c[C_ONES] = 1.0
    c[C_MASKF] = np.where(i[None, :] <= i[:, None], 0.0, -1e30)
    c[C_MASKB] = np.where(i[None, :] >= i[:, None], 0.0, -1e30)
    c[C_STRF] = (i[None, :] < i[:, None])
    c[C_STRB] = (i[None, :] > i[:, None])
    d = i % 64
    partner = np.where((d % 32) < 16, i + 16, i - 16)
    pm = np.zeros((128, 128), np.float32)
    pm[partner, i] = 1.0
    c[C_PERM] = pm
    c[C_BD64] = ((i[:, None] // 64) == (i[None, :] // 64))
    c[C_SWLO] = (i[:, None] == i[None, :] + 64)
    c[C_SWHI] = (i[:, None] + 64 == i[None, :])
    c[C_NEG1] = -1.0
    cst = np.ascontiguousarray(c.transpose(1, 0, 2).reshape(128, NCST * 128))
    t = np.arange(S)
    inv = (10000.0 ** (-np.arange(16, dtype=np.float32) / 16)).astype(np.float32)
    row = (t // 64).astype(np.float32)
    col = (t % 64).astype(np.float32)
    ang = np.stack([row[:, None] * inv[None, :], col[:, None] * inv[None, :]], 0)
    cos = np.cos(ang).astype(np.float32)
    sin = np.sin(ang).astype(np.float32)
    rope = np.zeros((128, 2, S), np.float32)
    for p in range(128):
        dd = p % 64
        ax = dd // 32
        half = (dd % 32) // 16
        pr = dd % 16
        rope[p, 0] = cos[ax, :, pr]
        rope[p, 1] = (-sin[ax, :, pr]) if half == 0 else sin[ax, :, pr]
    return cst, rope


V_NW_MIX, V_NW_MEM, V_NW_FFN, V_NW_FIN, V_CONV, V_DNW, V_QNW, V_KNW, V_ALOG, V_DTB = 0, 8, 16, 24, 32, 92, 93, 94, 95, 96
NVEC = 100


def make_vec(inp):
    v = np.zeros((128, NVEC), np.float32)
    v[:, V_NW_MIX:V_NW_MIX + 8] = inp["mix_norm_w"][0].reshape(8, 128).T
    v[:, V_NW_MEM:V_NW_MEM + 8] = inp["mem_norm_w"][0].reshape(8, 128).T
    v[:, V_NW_FFN:V_NW_FFN + 8] = inp["ffn_norm_w"][0].reshape(8, 128).T
    v[:, V_NW_FIN:V_NW_FIN + 8] = inp["final_norm_w"].reshape(8, 128).T
    cw = inp["dn_conv_w"][0]
    v[:, V_CONV:V_CONV + 60] = cw.reshape(5, 12, 128).transpose(2, 1, 0).reshape(128, 60)
    v[:, V_DNW] = inp["dn_norm_w"][0]
    v[:, V_QNW] = np.tile(inp["q_norm_w"][0], 2)
    v[:, V_KNW] = np.tile(inp["k_norm_w"][0], 2)
    v[0:8, V_ALOG] = inp["dn_a_log"][0].reshape(8)
    v[0:8, V_DTB] = inp["dn_dt_bias"][0].reshape(8)
    return v


def build_program():
    _ST["stopped"] = False
    nc = bass.Bass("TRN2", target_bir_lowering=False)

    def dram(name, shape, kind="ExternalInput"):
        return nc.dram_tensor(name, shape, F32, kind=kind).ap()

    x_d = dram("x", [NSEQ, S, D])
    mem_d = dram("mem", [NSEQ, 256, D])
    w_in_d = dram("w_in", [D, 6416])
    w_kv_d = dram("w_mem_kv", [D, 1024])
    w_br_d = dram("w_branch", [1536, D])
    w_out_d = dram("w_out", [D, D])
    w_up_d = dram("w_up", [D, 4096])
    w_dn_d = dram("w_down", [4096, D])
    cst_d = dram("cst", [128, NCST * 128])
    rope_d = dram("rope", [128, 2, S])
    vec_d = dram("vec", [128, NVEC])
    out_d = dram("out", [NSEQ, S, D], kind="ExternalOutput")
    dbg_d = nc.dram_tensor("dbg", [128, 8, 4, S], BF16, kind="ExternalOutput").ap() if DEBUG else None
    dbgf_d = nc.dram_tensor("dbgf", [128, 4, S], F32, kind="ExternalOutput").ap() if DEBUG else None
    wmap = {"w_in": w_in_d, "w_kv": w_kv_d, "w_br": w_br_d, "w_out": w_out_d, "w_up": w_up_d, "w_dn": w_dn_d}

    es = ExitStack()
    with es:
        def sb(name, shape, dt=F32):
            return es.enter_context(nc.sbuf_tensor(name, shape, dt))

        sch = Sched(nc)
        for e in Sched.ENGS:
            sch.eng_sems[e] = es.enter_context(nc.semaphore("sem_" + e))

        def dsem(name):
            sch.new_dma_sem(name, es.enter_context(nc.semaphore("ds_" + name)))

        TK = defaultdict(Trk)

        def T(*key):
            return TK[key]

        def OP(eng, fn, r=(), w=(), dma=None):
            if _ST["stopped"]:
                return None
            return sch.op(eng, fn, reads=r, writes=w, dma=dma)

        cst = sb("cst_sb", [128, NCST, 128])
        cbf = sb("cbf", [128, NCST, 128], BF16)
        vec = sb("vec_sb", [128, NVEC])
        nA = sb("nA", [8, 1])
        scr = sb("scr", [128, 4])
        rtmp = sb("rtmp", [128, 512])
        wst = [sb("wst%d" % i, [128, 8, 256]) for i in range(2)]
        wbf = [sb("wbf%d" % i, [128, 8, 256], BF16) for i in range(3)]
        hT = sb("hT", [128, 8, S], BF16)
        banks = [es.enter_context(nc.psum_tensor("bank%d" % i, [128, 512], F32)) for i in range(7)]
        pTb = es.enter_context(nc.psum_tensor("pTb", [128, 1024], BF16))
        for n_ in ["cst", "vec", "w0", "w1", "x0", "x1", "out0", "out1", "misc", "dbg"]:
            dsem(n_)

        def cf(i):
            return cst[:, i, :]

        def cb(i):
            return cbf[:, i, :]

        OP("sp", lambda e: e.dma_start(out=cst[:].rearrange("p a b -> p (a b)"), in_=cst_d), w=[T("cst")], dma="cst")
        OP("sp", lambda e: e.dma_start(out=vec[:], in_=vec_d), w=[T("vec")], dma="vec")
        OP("dve", lambda e: e.tensor_copy(out=cbf[:], in_=cst[:]), r=[T("cst")], w=[T("cbf")])
        OP("act", lambda e: e.activation(out=nA[:], in_=vec[0:8, V_ALOG:V_ALOG + 1], func=AF.Exp), r=[T("vec")], w=[T("nA")])
        OP("dve", lambda e: e.tensor_scalar_mul(out=nA[:], in0=nA[:], scalar1=-1.0), r=[T("nA")], w=[T("nA")])
        CONSTS = [T("cst"), T("cbf"), T("vec"), T("nA")]

        bank_rr = defaultdict(int)

        def bank(group, ids):
            i = ids[bank_rr[group] % len(ids)]
            bank_rr[group] += 1
            return banks[i], T("bank", i)

        wplan = []
        for s_ in range(NSEQ):
            for c0 in range(0, 1536, 256):
                wplan.append(("w_in", 0, 8, c0, 256))
            wplan.append(("w_in", 0, 8, 2048, 256))
            for c0 in range(1536, 2048, 256):
                wplan.append(("w_in", 0, 8, c0, 256))
            for dt in range(4):
                wplan.append(("w_in", 0, 8, 3344 + 0 * 1024 + dt * 256, 256))
                wplan.append(("w_br", 0 * 512, 4, dt * 256, 256))
            for c0 in range(2064, 2576, 256):
                wplan.append(("w_in", 0, 8, c0, 256))
            wplan.append(("kdup", 0, 8, 2576, 256))
            wplan.append(("w_in", 0, 8, 2704, 128))
            for dt in range(4):
                wplan.append(("w_in", 0, 8, 3344 + 1 * 1024 + dt * 256, 256))
                wplan.append(("w_br", 1 * 512, 4, dt * 256, 256))
            for c0 in range(0, 1024, 256):
                wplan.append(("w_kv", 0, 8, c0, 256))
            for c0 in range(2832, 3344, 256):
                wplan.append(("w_in", 0, 8, c0, 256))
            for dt in range(4):
                wplan.append(("w_in", 0, 8, 3344 + 2 * 1024 + dt * 256, 256))
                wplan.append(("w_br", 2 * 512, 4, dt * 256, 256))
            for hs in range(2):
                for dt in range(4):
                    wplan.append(("w_out", 0, 8, dt * 256, 256))
                for fg in range(4):
                    for ut in range(4):
                        wplan.append(("w_up", 0, 8, fg * 1024 + ut * 256, 256))
                    for dt in range(4):
                        wplan.append(("w_dn", fg * 1024, 8, dt * 256, 256))
        wstate = {"dma": 0, "cast": 0, "use": 0}

        def w_issue_dma(i):
            name, k0, KC, c0, ncols = wplan[i]
            st = wst[i % 2]
            sem = "w%d" % (i % 2)
            tr = T("wst", i % 2)
            if name == "kdup":
                for j in range(4):
                    src = w_in_d[0:D, 2576 + (j // 2) * 64: 2576 + (j // 2) * 64 + 64].rearrange("(kc p) n -> p kc n", p=128)
                    OP("sp", lambda e, st=st, src=src, j=j: e.dma_start(out=st[:, 0:8, j * 64:(j + 1) * 64], in_=src),
                       w=[tr], dma=sem)
            else:
                src = wmap[name][k0:k0 + KC * 128, c0:c0 + ncols].rearrange("(kc p) n -> p kc n", p=128)
                OP("sp", lambda e, st=st, src=src, KC=KC, ncols=ncols: e.dma_start(out=st[:, 0:KC, 0:ncols], in_=src),
                   w=[tr], dma=sem)

        def w_issue_cast(i):
            name, k0, KC, c0, ncols = wplan[i]
            st = wst[i % 2]
            wb = wbf[i % 3]
            if i % 2 == 0:
                OP("act", lambda e, st=st, wb=wb, KC=KC, ncols=ncols: e.activation(out=wb[:, 0:KC, 0:ncols], in_=st[:, 0:KC, 0:ncols], func=AF.Copy),
                   r=[T("wst", i % 2)], w=[T("wbf", i % 3)])
            else:
                OP("dve", lambda e, st=st, wb=wb, KC=KC, ncols=ncols: e.tensor_copy(out=wb[:, 0:KC, 0:ncols], in_=st[:, 0:KC, 0:ncols]),
                   r=[T("wst", i % 2)], w=[T("wbf", i % 3)])

        def wnext(expect):
            i = wstate["use"]
            if _ST["stopped"]:
                wstate["use"] += 1
                return wbf[i % 3], T("wbf", i % 3)
            assert wplan[i][0] == expect[0] and wplan[i][3] == expect[1], (wplan[i], expect)
            while wstate["dma"] < min(len(wplan), i + 2):
                w_issue_dma(wstate["dma"])
                wstate["dma"] += 1
            while wstate["cast"] < min(len(wplan), i + 2):
                if wstate["dma"] <= wstate["cast"]:
                    w_issue_dma(wstate["dma"])
                    wstate["dma"] += 1
                w_issue_cast(wstate["cast"])
                wstate["cast"] += 1
                while wstate["dma"] < min(len(wplan), wstate["cast"] + 2):
                    w_issue_dma(wstate["dma"])
                    wstate["dma"] += 1
            wstate["use"] += 1
            return wbf[i % 3], T("wbf", i % 3)

        def proj_fm(wb, wtr, KC, col_chunks, rhs_fn, rhs_trk_fn, evac_fn, ntb=4, bgroup=("proj", (0, 1, 2, 3))):
            for ci, (co, m) in enumerate(col_chunks):
                for tb in range(ntb):
                    bk, btr = bank(*bgroup)
                    for kc in range(KC):
                        OP("pe", lambda e, bk=bk, kc=kc, co=co, m=m, tb=tb: e.matmul(
                            bk[0:m, 0:512], lhsT=wb[:, kc, co:co + m], rhs=rhs_fn(kc, tb), start=(kc == 0), stop=(kc == KC - 1)),
                           r=[wtr, rhs_trk_fn(kc, tb)], w=[btr])
                    evac_fn(ci, tb, bk, btr)

        def rmsnorm_tokmajor(src_ap_fn, ntiles, dstT, dst_trk_fn, nw_off, tag, sb, xs, F):
            junk = sb("junk_" + tag, [128, D], BF16)
            xn = [sb("xn%d_" % i + tag, [128, D], BF16) for i in range(2)]
            st = sb("st_" + tag, [128, 4])
            for t in range(ntiles):
                xt = xs[t % 2]
                xtr = F("xs", t % 2)
                OP("sp", lambda e, xt=xt, t=t: e.dma_start(out=xt[:], in_=src_ap_fn(t)), w=[xtr], dma="x%d" % (t % 2))
                OP("act", lambda e, xt=xt: e.activation(out=junk[:], in_=xt[:], func=AF.Square, scale=1.0 / 32.0, accum_out=st[:, 0:1]),
                   r=[xtr], w=[F("junk", tag), F("st", tag)])
                OP("act", lambda e: e.activation(out=st[:, 1:2], in_=st[:, 0:1], func=AF.Sqrt, bias=EPS, scale=1.0),
                   r=[F("st", tag)], w=[F("st", tag)])
                OP("dve", lambda e: e.reciprocal(out=st[:, 2:3], in_=st[:, 1:2]), r=[F("st", tag)], w=[F("st", tag)])
                xnt = xn[t % 2]
                xntr = F("xn", tag, t % 2)
                OP("dve", lambda e, xt=xt, xnt=xnt: e.tensor_scalar_mul(out=xnt[:], in0=xt[:], scalar1=st[:, 2:3]),
                   r=[xtr, F("st", tag)], w=[xntr])
                for c in range(8):
                    OP("pe", lambda e, c=c, xnt=xnt: e.transpose(out=pTb[:, c * 128:(c + 1) * 128], in_=xnt[:, c * 128:(c + 1) * 128], identity=cb(C_IDENT)),
                       r=[xntr, T("cbf")], w=[T("pTb", 0)])
                OP("dve", lambda e, t=t: e.tensor_tensor(
                    out=dstT[:, :, t * 128:(t + 1) * 128], in0=pTb[:].rearrange("p (c j) -> p c j", c=8),
                    in1=vec[:, nw_off:nw_off + 8].unsqueeze(2).broadcast_to([128, 8, 128]), op=ALU.mult),
                   r=[T("pTb", 0), T("pTb", 0), T("vec")], w=[dst_trk_fn(t)])

        def sumsq_rn(src_ap, src_trks, nparts_scale, lhsT_const, dst_rn, dst_trk, tmp_sq, tmp_trk, bgroup):
            OP("act", lambda e: e.activation(out=tmp_sq, in_=src_ap, func=AF.Square), r=src_trks, w=[tmp_trk])
            bk, btr = bank(*bgroup)
            OP("pe", lambda e: e.matmul(bk[:, 0:512], lhsT=lhsT_const, rhs=tmp_sq, start=True, stop=True), r=[tmp_trk, T("cbf")], w=[btr])
            OP("act", lambda e: e.activation(out=rtmp[:], in_=bk[:, 0:512], func=AF.Sqrt, bias=EPS, scale=nparts_scale), r=[btr], w=[T("rtmp")])
            OP("dve", lambda e: e.reciprocal(out=dst_rn, in_=rtmp[:]), r=[T("rtmp")], w=[dst_trk])

        def body():
          for s_ in range(NSEQ):
            body_seq(s_)

        def body_seq(s_):
            nonlocal_dummy = None
            seqscope = ExitStack()
            SEQSC.append(seqscope)
            ysb = seqscope.enter_context(nc.sbuf_tensor("ysb_s%d" % s_, [128, 4, S], BF16))
            with ExitStack() as ph:
                def psb(name, shape, dt=F32, ph=ph):
                    return ph.enter_context(nc.sbuf_tensor(name + "_s%d" % s_, shape, dt))
                FR = {}

                def F(*key):
                    if key not in FR:
                        FR[key] = sch.fresh()
                    return FR[key]
                xs = [psb("xsA%d" % i, [128, D]) for i in range(2)]
                rmsnorm_tokmajor(lambda t: x_d[s_, t * 128:(t + 1) * 128, :], NT, hT, lambda t: T("hT", t // 4), V_NW_MIX, "a", psb, xs, F)
            sch.barrier(scr[:, 0:1])
            _chk("A")

            with ExitStack() as ph:
                def psb(name, shape, dt=F32, ph=ph):
                    return ph.enter_context(nc.sbuf_tensor(name + "_s%d" % s_, shape, dt))
                qT = psb("qT", [128, 4, S], BF16)
                kT = psb("kT", [128, 4, S], BF16)
                vtok = psb("vtok", [128, NT, 4, 128], BF16)
                ydn = ysb
                FR = {}

                def F(*key):
                    if key not in FR:
                        FR[key] = sch.fresh()
                    return FR[key]

                with ExitStack() as ph2:
                    def psb2(name, shape, dt=F32, ph2=ph2):
                        return ph2.enter_context(nc.sbuf_tensor(name + "_s%d" % s_, shape, dt))
                    pre = [psb2("pre%d" % i, [128, S + 128]) for i in range(2)]
                    cacc = [psb2("cacc%d" % i, [128, S]) for i in range(2)]
                    sqb = psb2("sqb", [128, 512], BF16)
                    rn = psb2("rn", [128, 512])
                    vTt = psb2("vTt", [128, S], BF16)
                    ctmp = psb2("ctmp", [128, S])
                    for i in range(2):
                        OP("dve", lambda e, i=i: e.memset(pre[i][:, 0:64], 0.0), w=[F("prepad", i)])
                        OP("dve", lambda e, i=i: e.memset(pre[i][:, S + 64:S + 128], 0.0), w=[F("prepad", i)])
                    for c in range(12):
                        if c % 2 == 0:
                            wb, wtr = wnext(("w_in", c * 128))
                        pb = pre[c % 2]
                        ca = cacc[c % 2]
                        if c < int(os.environ.get("KSKIP", "0")):
                            continue

                        def ev(ci, tb, bk, btr, pb=pb, c=c):
                            OP("act", lambda e: e.activation(out=pb[:, 64 + tb * 512: 64 + (tb + 1) * 512], in_=bk[:, 0:512], func=AF.Copy),
                               r=[btr], w=[F("pre", c % 2, tb)])
                        proj_fm(wb, wtr, 8, [((c % 2) * 128, 128)], lambda kc, tb: hT[:, kc, tb * 512:(tb + 1) * 512],
                                lambda kc, tb: T("hT", tb), ev)
                        _chk("c%dproj" % c)
                        ce = "dve"
                        pre_tr = [F("pre", c % 2, tb) for tb in range(4)] + [F("prepad", c % 2)]
                        OP(ce, lambda e, ca=ca, pb=pb, c=c: e.tensor_scalar_mul(out=ca[:], in0=pb[:, 62:62 + S], scalar1=vec[:, V_CONV + c * 5:V_CONV + c * 5 + 1]),
                           r=pre_tr + [T("vec")], w=[F("cacc", c % 2)])
                        for j in range(1, 5):
                            if ce == "dve":
                                OP(ce, lambda e, ca=ca, pb=pb, c=c, j=j: e.scalar_tensor_tensor(
                                    out=ca[:], in0=pb[:, 62 + j:62 + j + S], scalar=vec[:, V_CONV + c * 5 + j:V_CONV + c * 5 + j + 1], in1=ca[:],
                                    op0=ALU.mult, op1=ALU.add), r=pre_tr + [F("cacc", c % 2), T("vec")], w=[F("cacc", c % 2)])
                            else:
                                OP(ce, lambda e, pb=pb, c=c, j=j: e.tensor_scalar_mul(out=ctmp[:], in0=pb[:, j:j + S], scalar1=vec[:, V_CONV + c * 5 + j:V_CONV + c * 5 + j + 1]),
                                   r=pre_tr + [T("vec")], w=[F("ctmp")])
                                OP(ce, lambda e, ca=ca: e.tensor_tensor(out=ca[:], in0=ca[:], in1=ctmp[:], op=ALU.add), r=[F("ctmp"), F("cacc", c % 2)], w=[F("cacc", c % 2)])
                        _chk("c%dconv" % c)
                        h = c % 4
                        if c >= 8:
                            OP("act", lambda e, ca=ca: e.activation(out=vTt[:], in_=ca[:], func=AF.Silu), r=[F("cacc", c % 2)], w=[F("vTt")])
                            src, strk, dst, dtrk = vTt, F("vTt"), vtok, F("vtok")
                        else:
                            OP("act", lambda e, ca=ca: e.activation(out=ca[:], in_=ca[:], func=AF.Silu), r=[F("cacc", c % 2)], w=[F("cacc", c % 2)])
                            dstT = qT if c < 4 else kT
                            dtr = F("qT", h) if c < 4 else F("kT", h)
                            for tb in range(4):
                                sl = slice(tb * 512, (tb + 1) * 512)
                                sumsq_rn(ca[:, sl], [F("cacc", c % 2)], 1.0, cb(C_ONES), rn[:], F("rn"), sqb[:], F("sqb"), ("aux", (4, 5)))
                                OP("dve", lambda e, ca=ca, sl=sl, dstT=dstT, h=h, c=c, rn=rn: e.scalar_tensor_tensor(
                                    out=dstT[:, h, sl], in0=ca[:, sl], scalar=(128.0 ** -0.5 if c < 4 else 1.0), in1=rn[:],
                                    op0=ALU.mult, op1=ALU.mult), r=[F("cacc", c % 2), F("rn")], w=[dtr])
                            src, strk, dst, dtrk = (None, None, None, None)
                        _chk("c%dnorm" % c)
                        if src is not None and not os.environ.get("KNOVT"):
                            for n0 in range(0, NT, 4):
                                hf = 0 if os.environ.get("KHF0") else (n0 // 4) % 2
                                for j in range(4):
                                    n = n0 + j
                                    sap = src[:, n * 128:(n + 1) * 128]
                                    OP("pe", lambda e, sap=sap, hf=hf, j=j: e.transpose(out=pTb[:, hf * 512 + j * 128: hf * 512 + (j + 1) * 128], in_=sap, identity=cb(C_IDENT)),
                                       r=[strk, T("cbf")], w=[T("pTb", 0)])
                                OP("dve", lambda e, hf=hf, n0=n0, dst=dst, h=h: e.tensor_scalar_mul(out=dst[:, n0:n0 + 4, h, :], in0=pTb[:, hf * 512:(hf + 1) * 512].rearrange("p (a b) -> p a b", a=4), scalar1=1.0),
                                   r=[T("pTb", 0)], w=[dtrk])
                sch.barrier(scr[:, 1:2])
                _chk("conv")
                FR.clear()
                oacc = psb("oacc", [128, 4, S])
                with ExitStack() as ph2:
                    def psb2(name, shape, dt=F32, ph2=ph2):
                        return ph2.enter_context(nc.sbuf_tensor(name + "_s%d" % s_, shape, dt))
                    btok = psb2("btok", [128, NT, 8])
                    gtk = psb2("gtk", [128, 2, NT, 4])
                    gcs = psb2("gcs", [128, NT, 8])
                    gts = psb2("gts", [128, NT, 8])
                    egc = psb2("egc", [128, NT, 8])
                    nbe = psb2("nbe", [128, NT, 8])
                    nbt = psb2("nbt", [128, NT, 8])
                    ekd = psb2("ekd", [128, NT, 8])
                    egt = psb2("egt", [128, NT, 8])
                    ph2b = ph2.enter_context(ExitStack())
                    bT = ph2b.enter_context(nc.sbuf_tensor("bT_s%d" % s_, [8, S], F32))
                    gT = ph2b.enter_context(nc.sbuf_tensor("gT_s%d" % s_, [8, S], F32))
                    t1 = ph2b.enter_context(nc.sbuf_tensor("t1_s%d" % s_, [8, S], F32))
                    t2 = ph2b.enter_context(nc.sbuf_tensor("t2_s%d" % s_, [8, S], F32))
                    wb, wtr = wnext(("w_in", 2048))
                    for tb in range(4):
                        sl = slice(tb * 512, (tb + 1) * 512)
                        for which in range(2):
                            bk, btr = bank("proj", (0, 1, 2, 3))
                            for kc in range(8):
                                OP("pe", lambda e, bk=bk, kc=kc, which=which, sl=sl, wb=wb: e.matmul(bk[0:8, 0:512], lhsT=wb[:, kc, which * 8:which * 8 + 8], rhs=hT[:, kc, sl], start=(kc == 0), stop=(kc == 7)),
                                   r=[wtr, T("hT", tb)], w=[btr])
                            if which == 0:
                                OP("act", lambda e, bk=bk, sl=sl: e.activation(out=bT[:, sl], in_=bk[0:8, 0:512], func=AF.Sigmoid), r=[btr], w=[F("bT")])
                            else:
                                OP("act", lambda e, bk=bk, sl=sl: e.activation(out=t1[:, sl], in_=bk[0:8, 0:512], func=AF.Identity, bias=vec[0:8, V_DTB:V_DTB + 1], scale=1.0),
                                   r=[btr, T("vec")], w=[F("t1")])
                    OP("act", lambda e: e.activation(out=t2[:], in_=t1[:], func=AF.Abs), r=[F("t1")], w=[F("t2")])
                    OP("act", lambda e: e.activation(out=t2[:], in_=t2[:], func=AF.Exp, scale=-1.0), r=[F("t2")], w=[F("t2")])
                    OP("act", lambda e: e.activation(out=t2[:], in_=t2[:], func=AF.Ln, bias=1.0, scale=1.0), r=[F("t2")], w=[F("t2")])
                    OP("dve", lambda e: e.tensor_scalar_max(out=t1[:], in0=t1[:], scalar1=0.0), r=[F("t1")], w=[F("t1")])
                    OP("dve", lambda e: e.tensor_tensor(out=t1[:], in0=t1[:], in1=t2[:], op=ALU.add), r=[F("t1"), F("t2")], w=[F("t1")])
                    OP("dve", lambda e: e.tensor_scalar_mul(out=gT[:], in0=t1[:], scalar1=nA[:, 0:1]), r=[F("t1"), T("nA")], w=[F("gT")])
                    bk, btr = bank("aux", (4, 5))
                    for n in range(NT):
                        OP("pe", lambda e, n=n, bk=bk: e.transpose(out=bk[:, n * 8:(n + 1) * 8], in_=bT[0:8, n * 128:(n + 1) * 128], identity=cst[0:8, C_IDENT, 0:8]),
                           r=[F("bT"), T("cst")], w=[btr])
                        OP("pe", lambda e, n=n, bk=bk: e.transpose(out=bk[:, 128 + n * 8:128 + (n + 1) * 8], in_=gT[0:8, n * 128:(n + 1) * 128], identity=cst[0:8, C_IDENT, 0:8]),
                           r=[F("gT"), T("cst")], w=[btr])
                    OP("dve", lambda e, bk=bk: e.tensor_copy(out=btok[:], in_=bk[:, 0:128].rearrange("p (n r) -> p n r", r=8)), r=[btr], w=[F("btok")])
                    for dr in range(2):
                        OP("dve", lambda e, bk=bk, dr=dr: e.tensor_copy(out=gtk[:, dr], in_=bk[:, 128:256].rearrange("p (n r) -> p n r", r=8)[:, :, dr * 4:dr * 4 + 4]),
                           r=[btr], w=[F("gtk")])
                    bk2, btr2 = bank("aux", (4, 5))
                    for dr in range(2):
                        OP("pe", lambda e, dr=dr, bk2=bk2: e.matmul(bk2[:, dr * 64:(dr + 1) * 64], lhsT=cf(C_UF if dr == 0 else C_UB), rhs=gtk[:, dr].rearrange("p n r -> p (n r)"), start=True, stop=True),
                           r=[F("gtk"), T("cst")], w=[btr2])
                        OP("pe", lambda e, dr=dr, bk2=bk2: e.matmul(bk2[:, 128 + dr * 64:128 + (dr + 1) * 64], lhsT=cf(C_ONES), rhs=gtk[:, dr].rearrange("p n r -> p (n r)"), start=True, stop=True),
                           r=[F("gtk"), T("cst")], w=[btr2])
                    for dr in range(2):
                        OP("dve", lambda e, dr=dr, bk2=bk2: e.tensor_copy(out=gcs[:, :, dr * 4:dr * 4 + 4], in_=bk2[:, dr * 64:(dr + 1) * 64].rearrange("p (n r) -> p n r", r=4)), r=[btr2], w=[F("gcs")])
                        OP("dve", lambda e, dr=dr, bk2=bk2: e.tensor_copy(out=gts[:, :, dr * 4:dr * 4 + 4], in_=bk2[:, 128 + dr * 64:128 + (dr + 1) * 64].rearrange("p (n r) -> p n r", r=4)), r=[btr2], w=[F("gts")])
                    OP("act", lambda e: e.activation(out=egc[:], in_=gcs[:], func=AF.Exp), r=[F("gcs")], w=[F("sc")])
                    OP("act", lambda e: e.activation(out=egt[:], in_=gts[:], func=AF.Exp), r=[F("gts")], w=[F("sc")])
                    OP("dve", lambda e: e.tensor_tensor(out=ekd[:], in0=gts[:], in1=gcs[:], op=ALU.subtract), r=[F("gts"), F("gcs")], w=[F("sc")])
                    OP("act", lambda e: e.activation(out=ekd[:], in_=ekd[:], func=AF.Exp), r=[F("sc")], w=[F("sc")])
                    OP("dve", lambda e: e.tensor_scalar_mul(out=nbt[:], in0=btok[:], scalar1=-1.0), r=[F("btok")], w=[F("sc")])
                    OP("dve", lambda e: e.tensor_tensor(out=nbe[:], in0=nbt[:], in1=egc[:], op=ALU.mult), r=[F("sc")], w=[F("sc")])
                    SC = F("sc")
                    _chk("tables")
                    ph2b.close()
                    sch.barrier(scr[:, 3:4])
                    for k_ in ("sc", "btok", "gtk"):
                        FR[(k_,)] = sch.fresh()
                    for h_ in range(4):
                        FR[("kT", h_)] = sch.fresh()
                        FR[("qT", h_)] = sch.fresh()
                    FR[("vtok",)] = sch.fresh()
                    SC = F("sc")

                    GU = psb2("GU", [128, 4, 128])
                    Dm = psb2("Dm", [128, 4, 128])
                    Dsn = psb2("Dsn", [128, 4, 128])
                    P0 = [psb2("Pk%d" % i, [128, 4, 128], BF16) for i in range(2)]
                    M0 = [psb2("Mk%d" % i, [128, 4, 128], BF16) for i in range(2)]
                    Y0 = [psb2("Yk%d" % i, [128, 4, 128], BF16) for i in range(2)]
                    Abf = psb2("Abf", [128, 4, 128], BF16)
                    ATb = psb2("ATb", [128, 4, 128], BF16)
                    Sst = [psb2("Sst%d" % i, [128, 4, 128]) for i in range(2)]
                    Sbf = [psb2("Sbf%d" % i, [128, 4, 128], BF16) for i in range(2)]
                    VBt = psb2("VBt", [128, 4, 128])
                    R1 = psb2("R1", [128, 4, 128])
                    Rb = psb2("Rb", [128, 4, 128], BF16)
                    vn = psb2("vn", [128, 4, 128], BF16)
                    O1 = psb2("O1", [128, 4, 128], BF16)
                    kdec = psb2("kdec", [128, 4, 128], BF16)
                    for dr in range(2):
                        OP("dve", lambda e, dr=dr: e.memset(Sst[dr][:], 0.0), w=[F("S", dr)])
                        OP("dve", lambda e, dr=dr: e.memset(Sbf[dr][:], 0.0), w=[F("Sbf", dr)])

                    def bc4(ap3):
                        return ap3.unsqueeze(2).broadcast_to([128, 4, 128])

                    def c4(ci, bf=False):
                        a = cb(ci) if bf else cf(ci)
                        return a.unsqueeze(1).broadcast_to([128, 4, 128])

                    def flat(t):
                        return t[:].rearrange("p h j -> p (h j)")

                    for step in range(NT):
                        for dr in range(2):
                            n = step if dr == 0 else NT - 1 - step
                            r0 = dr * 4
                            tok = slice(n * 128, (n + 1) * 128)
                            OP("dve", lambda e, dr=dr, n=n: e.tensor_tensor(out=GU[:], in0=c4(C_UF if dr == 0 else C_UB), in1=bc4(gtk[:, dr, n, :]), op=ALU.mult),
                               r=[F("gtk"), T("cst")], w=[F("GU")])
                            bG, tG = bank("pre", (0, 1, 2))
                            OP("pe", lambda e, bG=bG: e.matmul(bG[:, 0:512], lhsT=cf(C_NEG1), rhs=flat(GU), start=True, stop=False), r=[F("GU"), T("cst")], w=[tG])
                            for h in range(4):
                                OP("pe", lambda e, bG=bG, h=h: e.matmul(bG[:, h * 128:(h + 1) * 128], lhsT=GU[:, h, :], rhs=cf(C_ONES), start=False, stop=True), r=[F("GU"), T("cst")], w=[tG])
                            OP("dve", lambda e, bG=bG, dr=dr: e.tensor_tensor(out=Dm[:], in0=bG[:, 0:512].rearrange("p (h j) -> p h j", h=4), in1=c4(C_MASKF if dr == 0 else C_MASKB), op=ALU.add),
                               r=[tG, T("cst")], w=[F("Dm")])
                            OP("act", lambda e: e.activation(out=Dm[:], in_=Dm[:], func=AF.Exp), r=[F("Dm")], w=[F("Dm")])
                            bKK, tKK = bank("pre", (0, 1, 2))
                            bQK, tQK = bank("pre", (0, 1, 2))
                            for h in range(4):
                                OP("pe", lambda e, h=h, bKK=bKK, tok=tok: e.matmul(bKK[:, h * 128:(h + 1) * 128], lhsT=kT[:, h, tok], rhs=kT[:, h, tok], start=True, stop=True), r=[F("kT", h)], w=[tKK])
                                OP("pe", lambda e, h=h, bQK=bQK, tok=tok: e.matmul(bQK[:, h * 128:(h + 1) * 128], lhsT=qT[:, h, tok], rhs=kT[:, h, tok], start=True, stop=True), r=[F("kT", h), F("qT", h)], w=[tQK])
                            OP("dve", lambda e, dr=dr: e.tensor_tensor(out=Dsn[:], in0=Dm[:], in1=c4(C_STRF if dr == 0 else C_STRB), op=ALU.mult), r=[F("Dm"), T("cst")], w=[F("Dsn")])
                            OP("dve", lambda e, n=n, r0=r0: e.tensor_tensor(out=Dsn[:], in0=Dsn[:], in1=bc4(nbt[:, n, r0:r0 + 4]), op=ALU.mult), r=[F("Dsn"), SC], w=[F("Dsn")])
                            OP("dve", lambda e, bKK=bKK: e.tensor_tensor(out=P0[0][:], in0=bKK[:, 0:512].rearrange("p (h j) -> p h j", h=4), in1=Dsn[:], op=ALU.mult), r=[tKK, F("Dsn")], w=[F("P", 0)])
                            OP("dve", lambda e, bQK=bQK: e.tensor_tensor(out=Abf[:], in0=bQK[:, 0:512].rearrange("p (h j) -> p h j", h=4), in1=Dm[:], op=ALU.mult), r=[tQK, F("Dm")], w=[F("Abf")])
                            for h in range(4):
                                OP("pe", lambda e, h=h: e.transpose(out=pTb[:, h * 128:(h + 1) * 128], in_=P0[0][:, h, :], identity=cb(C_IDENT)), r=[F("P", 0), T("cbf")], w=[T("pTb", 0)])
                            for h in range(4):
                                OP("pe", lambda e, h=h: e.transpose(out=pTb[:, 512 + h * 128:512 + (h + 1) * 128], in_=Abf[:, h, :], identity=cb(C_IDENT)), r=[F("Abf"), T("cbf")], w=[T("pTb", 0)])
                            OP("dve", lambda e: e.tensor_scalar_mul(out=flat(M0[0]), in0=pTb[:, 0:512], scalar1=1.0), r=[T("pTb", 0)], w=[F("M", 0)])
                            OP("dve", lambda e: e.tensor_scalar_mul(out=flat(ATb), in0=pTb[:, 512:1024], scalar1=1.0), r=[T("pTb", 0)], w=[F("AT")])
                            OP("dve", lambda e: e.tensor_tensor(out=Y0[0][:], in0=M0[0][:], in1=c4(C_IDENT, True), op=ALU.add), r=[F("M", 0), T("cbf")], w=[F("Y", 0)])
                            cur = 0
                            for lvl in range(7):
                                nxt = 1 - cur
                                Pc, Mc, Yc = P0[cur], M0[cur], Y0[cur]
                                Pn, Mn, Yn = P0[nxt], M0[nxt], Y0[nxt]
                                tP, tM, tY = F("P", cur), F("M", cur), F("Y", cur)
                                if lvl <= 4:
                                    bM, tbM = bank("pre", (0, 1, 2))
                                    for h in range(4):
                                        OP("pe", lambda e, h=h, bM=bM, Pc=Pc, Mc=Mc: e.matmul(bM[:, h * 128:(h + 1) * 128], lhsT=Pc[:, h, :], rhs=Mc[:, h, :], start=True, stop=True), r=[tP, tM], w=[tbM])
                                if lvl <= 5:
                                    bP, tbP = bank("pre", (0, 1, 2))
                                    for h in range(4):
                                        OP("pe", lambda e, h=h, bP=bP, Pc=Pc, Mc=Mc: e.matmul(bP[:, h * 128:(h + 1) * 128], lhsT=Mc[:, h, :], rhs=Pc[:, h, :], start=True, stop=True), r=[tP, tM], w=[tbP])
                                if lvl >= 1:
                                    bY, tbY = bank("pre", (0, 1, 2))
                                    for h in range(4):
                                        OP("pe", lambda e, h=h, bY=bY, Pc=Pc, Yc=Yc: e.matmul(bY[:, h * 128:(h + 1) * 128], lhsT=Pc[:, h, :], rhs=Yc[:, h, :], start=True, stop=True), r=[tP, tY], w=[tbY])
                                if lvl <= 4:
                                    OP("act", lambda e, bM=bM, Mn=Mn: e.activation(out=flat(Mn), in_=bM[:, 0:512], func=AF.Copy), r=[tbM], w=[F("M", nxt)])
                                if lvl <= 5:
                                    OP("act", lambda e, bP=bP, Pn=Pn: e.activation(out=flat(Pn), in_=bP[:, 0:512], func=AF.Copy), r=[tbP], w=[F("P", nxt)])
                                if lvl >= 1:
                                    OP("dve", lambda e, bY=bY, Yn=Yn, Yc=Yc: e.tensor_tensor(out=flat(Yn), in0=bY[:, 0:512], in1=flat(Yc), op=ALU.add), r=[tbY, tY], w=[F("Y", nxt)])
                                else:
                                    OP("act", lambda e, Yn=Yn, Yc=Yc: e.activation(out=Yn[:], in_=Yc[:], func=AF.Copy), r=[tY], w=[F("Y", nxt)])
                                cur = nxt
                            TTb = Y0[cur]
                            tTT = F("Y", cur)
                            OP("dve", lambda e, n=n, r0=r0: e.tensor_tensor(out=VBt[:], in0=vtok[:, n], in1=bc4(btok[:, n, r0:r0 + 4]), op=ALU.mult), r=[F("vtok"), F("btok")], w=[F("VBt")])
                            for h in range(4):
                                OP("pe", lambda e, h=h, tok=tok: e.transpose(out=pTb[:, h * 128:(h + 1) * 128], in_=kT[:, h, tok], identity=cb(C_IDENT)), r=[F("kT", h), T("cbf")], w=[T("pTb", 0)])
                            OP("dve", lambda e, n=n, r0=r0: e.tensor_tensor(out=kdec[:], in0=pTb[:, 0:512].rearrange("p (h j) -> p h j", h=4), in1=bc4(ekd[:, n, r0:r0 + 4]), op=ALU.mult), r=[T("pTb", 0), SC], w=[F("kdec")])
                            bKS, tKS = bank("scan", (3, 4, 5, 6))
                            bW1, tW1 = bank("scan", (3, 4, 5, 6))
                            for h in range(4):
                                OP("pe", lambda e, h=h, bKS=bKS, tok=tok, dr=dr: e.matmul(bKS[:, h * 128:(h + 1) * 128], lhsT=kT[:, h, tok], rhs=Sbf[dr][:, h, :], start=True, stop=True), r=[F("kT", h), F("Sbf", dr)], w=[tKS])
                            for h in range(4):
                                OP("pe", lambda e, h=h, bW1=bW1, tok=tok, dr=dr: e.matmul(bW1[:, h * 128:(h + 1) * 128], lhsT=qT[:, h, tok], rhs=Sbf[dr][:, h, :], start=True, stop=True), r=[F("qT", h), F("Sbf", dr)], w=[tW1])
                            OP("dve", lambda e, bKS=bKS, n=n, r0=r0: e.tensor_tensor(out=R1[:], in0=bKS[:, 0:512].rearrange("p (h j) -> p h j", h=4), in1=bc4(nbe[:, n, r0:r0 + 4]), op=ALU.mult), r=[tKS, SC], w=[F("R1")])
                            OP("dve", lambda e: e.tensor_tensor(out=Rb[:], in0=R1[:], in1=VBt[:], op=ALU.add), r=[F("R1"), F("VBt")], w=[F("Rb")])
                            bV, tV = bank("scan", (3, 4, 5, 6))
                            for h in range(4):
                                OP("pe", lambda e, h=h, bV=bV, TTb=TTb: e.matmul(bV[:, h * 128:(h + 1) * 128], lhsT=TTb[:, h, :], rhs=Rb[:, h, :], start=True, stop=True), r=[tTT, F("Rb")], w=[tV])
                            OP("act", lambda e, bV=bV: e.activation(out=flat(vn), in_=bV[:, 0:512], func=AF.Copy), r=[tV], w=[F("vn")])
                            OP("dve", lambda e, bW1=bW1, n=n, r0=r0: e.tensor_tensor(out=O1[:], in0=bW1[:, 0:512].rearrange("p (h j) -> p h j", h=4), in1=bc4(egc[:, n, r0:r0 + 4]), op=ALU.mult), r=[tW1, SC], w=[F("O1")])
                            bO, tO = bank("scan", (3, 4, 5, 6))
                            for h in range(4):
                                OP("pe", lambda e, h=h, bO=bO: e.matmul(bO[:, h * 128:(h + 1) * 128], lhsT=O1[:, h, :], rhs=cb(C_IDENT), start=True, stop=False), r=[F("O1"), T("cbf")], w=[tO])
                                OP("pe", lambda e, h=h, bO=bO: e.matmul(bO[:, h * 128:(h + 1) * 128], lhsT=vn[:, h, :], rhs=ATb[:, h, :], start=False, stop=True), r=[F("vn"), F("AT")], w=[tO])
                            first = (step < NT // 2)
                            if first:
                                OP("act", lambda e, bO=bO, tok=tok: e.activation(out=oacc[:, :, tok], in_=bO[:, 0:512].rearrange("p (h j) -> p h j", h=4), func=AF.Copy), r=[tO], w=[F("oacc", n)])
                            else:
                                OP("dve", lambda e, bO=bO, tok=tok: e.tensor_tensor(out=oacc[:, :, tok], in0=bO[:, 0:512].rearrange("p (h j) -> p h j", h=4), in1=oacc[:, :, tok], op=ALU.add), r=[tO, F("oacc", n)], w=[F("oacc", n)])
                            bS, tS = bank("scan", (3, 4, 5, 6))
                            for h in range(4):
                                OP("pe", lambda e, h=h, bS=bS: e.matmul(bS[:, h * 128:(h + 1) * 128], lhsT=kdec[:, h, :], rhs=vn[:, h, :], start=True, stop=True), r=[F("kdec"), F("vn")], w=[tS])
                            OP("dve", lambda e, dr=dr, n=n, r0=r0: e.tensor_tensor(out=Sst[dr][:], in0=Sst[dr][:], in1=bc4(egt[:, n, r0:r0 + 4]), op=ALU.mult), r=[F("S", dr), SC], w=[F("S", dr)])
                            OP("dve", lambda e, dr=dr, bS=bS: e.tensor_tensor(out=flat(Sst[dr]), in0=bS[:, 0:512], in1=flat(Sst[dr]), op=ALU.add), r=[tS, F("S", dr)], w=[F("S", dr)])
                            OP("act", lambda e, dr=dr: e.activation(out=Sbf[dr][:], in_=Sst[dr][:], func=AF.Copy), r=[F("S", dr)], w=[F("Sbf", dr)])
                sch.barrier(scr[:, 2:3])
                _chk("loop")
                FR.clear()
                if DEBUG and s_ == 0:
                    OP("sp", lambda e: e.dma_start(out=dbg_d[:, 5], in_=qT[:]), r=[F("qT", 0)], dma="dbg")
                    OP("sp", lambda e: e.dma_start(out=dbg_d[:, 6], in_=kT[:]), r=[F("kT", 0)], dma="dbg")
                    OP("sp", lambda e: e.dma_start(out=dbgf_d, in_=oacc[:]), r=[F("oacc", 0)], dma="dbg")
                with ExitStack() as ph2:
                    def psb2(name, shape, dt=F32, ph2=ph2):
                        return ph2.enter_context(nc.sbuf_tensor(name + "_s%d" % s_, shape, dt))
                    zs = psb2("zs", [128, 4, S], BF16)
                    sqb = psb2("sqb2", [128, 512], BF16)
                    rn = psb2("rn2", [128, 512])
                    yt = psb2("yt", [128, 512])
                    for zt in range(2):
                        wb, wtr = wnext(("w_in", 1536 + zt * 256))

                        def ev(ci, tb, bk, btr, zt=zt):
                            OP("act", lambda e: e.activation(out=zs[:, zt * 2 + ci, tb * 512:(tb + 1) * 512], in_=bk[:, 0:512], func=AF.Silu), r=[btr], w=[F("zs", zt * 2 + ci)])
                        proj_fm(wb, wtr, 8, [(0, 128), (128, 128)], lambda kc, tb: hT[:, kc, tb * 512:(tb + 1) * 512], lambda kc, tb: T("hT", tb), ev)
                    oall = [F("oacc", n) for n in range(NT)]
                    for h in range(4):
                        for tb in range(4):
                            sl = slice(tb * 512, (tb + 1) * 512)
                            sumsq_rn(oacc[:, h, sl], oall, 1.0 / 128.0, cb(C_ONES), rn[:], F("rn"), sqb[:], F("sqb"), ("aux", (4, 5)))
                            OP("dve", lambda e, h=h, sl=sl, rn=rn: e.tensor_tensor(out=yt[:], in0=oacc[:, h, sl], in1=rn[:], op=ALU.mult), r=oall + [F("rn")], w=[F("yt")])
                            OP("dve", lambda e, h=h, sl=sl: e.scalar_tensor_tensor(out=ydn[:, h, sl], in0=yt[:], scalar=vec[:, V_DNW:V_DNW + 1], in1=zs[:, h, sl], op0=ALU.mult, op1=ALU.mult),
                               r=[F("yt"), F("zs", h), T("vec")], w=[F("ydn")])
                    if DEBUG and s_ == 0:
                        OP("sp", lambda e: e.dma_start(out=dbg_d[:, 0], in_=ydn[:]), r=[F("ydn")], dma="dbg")
            sch.barrier(scr[:, 0:1])
            TYS = sch.fresh()
            mrg = seqscope.enter_context(nc.sbuf_tensor("mrg_s%d" % s_, [128, 8, S], BF16))
            for dc_ in range(8):
                for tb_ in range(4):
                    TK[("mrg", dc_, tb_)] = sch.fresh()
            if True:

                def merge(nb, ysT, ystr, first):
                    with ExitStack() as ph3:
                        sg = ph3.enter_context(nc.sbuf_tensor("sg_%d_%d" % (s_, nb), [128, 512], F32))
                        ct = ph3.enter_context(nc.sbuf_tensor("ct_%d_%d" % (s_, nb), [128, 512], BF16))
                        tsg, tct = sch.fresh(), sch.fresh()
                        for dt in range(4):
                            wg, wgtr = wnext(("w_in", 3344 + nb * 1024 + dt * 256))
                            wbr, wbtr = wnext(("w_br", dt * 256))
                            for cc in range(2):
                                dc = dt * 2 + cc
                                for tb in range(4):
                                    sl = slice(tb * 512, (tb + 1) * 512)
                                    bA, tA = bank("proj", (0, 1, 2, 3))
                                    for kc in range(8):
                                        OP("pe", lambda e, bA=bA, kc=kc, cc=cc, sl=sl, wg=wg: e.matmul(bA[:, 0:512], lhsT=wg[:, kc, cc * 128:(cc + 1) * 128], rhs=hT[:, kc, sl], start=(kc == 0), stop=(kc == 7)), r=[wgtr, T("hT", tb)], w=[tA])
                                    bB, tB = bank("proj", (0, 1, 2, 3))
                                    for kc in range(4):
                                        OP("pe", lambda e, bB=bB, kc=kc, cc=cc, sl=sl, wbr=wbr: e.matmul(bB[:, 0:512], lhsT=wbr[:, kc, cc * 128:(cc + 1) * 128], rhs=ysT[:, kc, sl], start=(kc == 0), stop=(kc == 3)), r=[wbtr, ystr], w=[tB])
                                    OP("act", lambda e, bA=bA: e.activation(out=sg[:], in_=bA[:, 0:512], func=AF.Sigmoid), r=[tA], w=[tsg])
                                    if first:
                                        OP("dve", lambda e, bB=bB, dc=dc, sl=sl: e.tensor_tensor(out=mrg[:, dc, sl], in0=bB[:, 0:512], in1=sg[:], op=ALU.mult), r=[tB, tsg], w=[T("mrg", dc, tb)])
                                    else:
                                        OP("dve", lambda e, bB=bB: e.tensor_tensor(out=ct[:], in0=bB[:, 0:512], in1=sg[:], op=ALU.mult), r=[tB, tsg], w=[tct])
                                        OP("dve", lambda e, dc=dc, sl=sl: e.tensor_tensor(out=mrg[:, dc, sl], in0=mrg[:, dc, sl], in1=ct[:], op=ALU.add), r=[tct, T("mrg", dc, tb)], w=[T("mrg", dc, tb)])
                _chk("dn")
                merge(0, ysb, TYS, True)
            sch.barrier(scr[:, 3:4])
            _chk("m0")

            with ExitStack() as ph:
                def psb(name, shape, dt=F32, ph=ph):
                    return ph.enter_context(nc.sbuf_tensor(name + "_s%d" % s_, shape, dt))
                FR = {}

                def F(*key):
                    if key not in FR:
                        FR[key] = sch.fresh()
                    return FR[key]
                AQ = psb("AQ", [128, 4, S], BF16)
                AK = psb("AK", [128, 2, S], BF16)
                VX = psb("VX", [128, 4, NT, 128], BF16)
                ropes = psb("ropes", [128, 2, S])
                yat = ysb
                sqb = psb("sqb3", [128, 512], BF16)
                rn = psb("rn3", [128, 512])
                aqn = psb("aqn", [128, 512], BF16)
                r1 = psb("r1", [128, 512])
                r2 = psb("r2", [128, 512])
                PT = [psb("PT%d" % i, [128, 512], BF16) for i in range(4)]
                UA = psb("UA", [128, 512])
                UB = psb("UB", [128, 512])
                rd = psb("rd", [128, 512])
                OP("sp", lambda e: e.dma_start(out=ropes[:], in_=rope_d), w=[F("ropes")], dma="misc")
                OP("dve", lambda e: e.memset(VX[:], 1.0), w=[F("VX")])

                def qk_evac(dst_ap_fn, dst_trk, nwcol):
                    def ev(ci, tb, bk, btr):
                        sl = slice(tb * 512, (tb + 1) * 512)
                        sumsq_rn(bk[:, 0:512], [btr], 1.0 / 64.0, cb(C_BD64), rn[:], F("rn"), sqb[:], F("sqb"), ("aux", (4, 5)))
                        OP("dve", lambda e, rn=rn: e.scalar_tensor_tensor(out=aqn[:], in0=bk[:, 0:512], scalar=vec[:, nwcol:nwcol + 1], in1=rn[:], op0=ALU.mult, op1=ALU.mult),
                           r=[btr, F("rn"), T("vec")], w=[F("aqn")])
                        bR, tR = bank("aux", (4, 5))
                        OP("pe", lambda e: e.matmul(bR[:, 0:512], lhsT=cb(C_PERM), rhs=aqn[:], start=True, stop=True), r=[F("aqn"), T("cbf")], w=[tR])
                        OP("dve", lambda e: e.tensor_tensor(out=r1[:], in0=aqn[:], in1=ropes[:, 0, sl], op=ALU.mult), r=[F("aqn"), F("ropes")], w=[F("r1")])
                        OP("dve", lambda e: e.tensor_tensor(out=r2[:], in0=bR[:, 0:512], in1=ropes[:, 1, sl], op=ALU.mult), r=[tR, F("ropes")], w=[F("r2")])
                        OP("dve", lambda e: e.tensor_tensor(out=dst_ap_fn(ci, sl), in0=r1[:], in1=r2[:], op=ALU.add), r=[F("r1"), F("r2")], w=[dst_trk])
                    return ev
                for qt in range(2):
                    wb, wtr = wnext(("w_in", 2064 + qt * 256))
                    proj_fm(wb, wtr, 8, [(0, 128), (128, 128)], lambda kc, tb: hT[:, kc, tb * 512:(tb + 1) * 512], lambda kc, tb: T("hT", tb),
                            qk_evac(lambda ci, sl, qt=qt: AQ[:, qt * 2 + ci, sl], F("AQ"), V_QNW))
                wb, wtr = wnext(("kdup", 2576))
                proj_fm(wb, wtr, 8, [(0, 128), (128, 128)], lambda kc, tb: hT[:, kc, tb * 512:(tb + 1) * 512], lambda kc, tb: T("hT", tb),
                        qk_evac(lambda ci, sl: AK[:, ci, sl], F("AK"), V_KNW))
                wb, wtr = wnext(("w_in", 2704))
                for n0 in range(0, NT, 4):
                    bk, btr = bank("proj", (0, 1, 2, 3))
                    for j in range(4):
                        n = n0 + j
                        for kc in range(8):
                            OP("pe", lambda e, bk=bk, j=j, n=n, kc=kc, wb=wb: e.matmul(bk[:, j * 128:(j + 1) * 128], lhsT=hT[:, kc, n * 128:(n + 1) * 128], rhs=wb[:, kc, 0:128], start=(kc == 0), stop=(kc == 7)),
                               r=[wtr, T("hT", n // 4)], w=[btr])
                    for g in range(2):
                        src = lambda bk=bk, g=g: bk[:, 0:512].rearrange("p (a b) -> p a b", a=4)[:, :, g * 64:(g + 1) * 64]
                        OP("act", lambda e, g=g, n0=n0, src=src: e.activation(out=VX[:, g * 2 + 0, n0:n0 + 4, 0:64], in_=src(), func=AF.Copy), r=[btr], w=[F("VX")])
                        OP("dve", lambda e, g=g, n0=n0, src=src: e.tensor_copy(out=VX[:, g * 2 + 1, n0:n0 + 4, 64:128], in_=src()), r=[btr], w=[F("VX")])
                for c in range(4):
                    g = c // 2
                    for qb in range(4):
                        sl = slice(qb * 512, (qb + 1) * 512)
                        bOA, tOA = banks[4], T("bank", 4)
                        bOB, tOB = banks[5], T("bank", 5)
                        for n in range(NT):
                            tok = slice(n * 128, (n + 1) * 128)
                            b1, t1_ = bank("sc", (0, 1, 2, 3))
                            b2, t2_ = bank("sc", (0, 1, 2, 3))
                            OP("pe", lambda e, b1=b1, g=g, tok=tok, c=c, sl=sl: e.matmul(b1[:, 0:512], lhsT=AK[0:64, g, tok], rhs=AQ[0:64, c, sl], start=True, stop=True), r=[F("AK"), F("AQ")], w=[t1_])
                            OP("pe", lambda e, b2=b2, g=g, tok=tok, c=c, sl=sl: e.matmul(b2[:, 0:512], lhsT=AK[64:128, g, tok], rhs=AQ[64:128, c, sl], start=True, stop=True), r=[F("AK"), F("AQ")], w=[t2_])
                            p1, p2 = PT[(n % 2) * 2], PT[(n % 2) * 2 + 1]
                            tp1, tp2 = F("PT", (n % 2) * 2), F("PT", (n % 2) * 2 + 1)
                            OP("act", lambda e, b1=b1, p1=p1: e.activation(out=p1[:], in_=b1[:, 0:512], func=AF.Exp, scale=0.125), r=[t1_], w=[tp1])
                            OP("act", lambda e, b2=b2, p2=p2: e.activation(out=p2[:], in_=b2[:, 0:512], func=AF.Exp, scale=0.125), r=[t2_], w=[tp2])
                            OP("pe", lambda e, g=g, n=n, p1=p1: e.matmul(bOA[:, 0:512], lhsT=VX[:, g * 2 + 0, n, :], rhs=p1[:], start=(n == 0), stop=(n == NT - 1)), r=[F("VX"), tp1], w=[tOA])
                            OP("pe", lambda e, g=g, n=n, p2=p2: e.matmul(bOB[:, 0:512], lhsT=VX[:, g * 2 + 1, n, :], rhs=p2[:], start=(n == 0), stop=(n == NT - 1)), r=[F("VX"), tp2], w=[tOB])
                        OP("act", lambda e: e.activation(out=UA[:], in_=bOA[:, 0:512], func=AF.Copy), r=[tOA], w=[F("UA")])
                        OP("dve", lambda e: e.tensor_copy(out=UB[:], in_=bOB[:, 0:512]), r=[tOB], w=[F("UB")])
                        bD, tD = banks[6], T("bank", 6)
                        OP("pe", lambda e: e.matmul(bD[:, 0:512], lhsT=cf(C_SWLO), rhs=UA[:], start=True, stop=False), r=[F("UA"), T("cst")], w=[tD])
                        OP("pe", lambda e: e.matmul(bD[:, 0:512], lhsT=cf(C_SWHI), rhs=UB[:], start=False, stop=True), r=[F("UB"), T("cst")], w=[tD])
                        OP("dve", lambda e, rd=rd: e.reciprocal(out=rd[:], in_=bD[:, 0:512]), r=[tD], w=[F("rd")])
                        OP("dve", lambda e, c=c, sl=sl, rd=rd: e.tensor_tensor(out=yat[0:64, c, sl], in0=UA[0:64, :], in1=rd[0:64, :], op=ALU.mult), r=[F("UA"), F("rd")], w=[F("yat")])
                        OP("dve", lambda e, c=c, sl=sl, rd=rd: e.tensor_tensor(out=yat[64:128, c, sl], in0=UB[64:128, :], in1=rd[64:128, :], op=ALU.mult), r=[F("UB"), F("rd")], w=[F("yat")])
                if DEBUG and s_ == 0:
                    OP("sp", lambda e: e.dma_start(out=dbg_d[:, 1], in_=yat[:]), r=[F("yat")], dma="dbg")
                    OP("sp", lambda e: e.dma_start(out=dbg_d[:, 3], in_=AQ[:]), r=[F("AQ")], dma="dbg")
                    OP("sp", lambda e: e.dma_start(out=dbg_d[:, 4, 0:2], in_=AK[:]), r=[F("AK")], dma="dbg")
                _chk("gqa")
                merge(1, yat, F("yat"), False)
            sch.barrier(scr[:, 0:1])

            with ExitStack() as ph:
                def psb(name, shape, dt=F32, ph=ph):
                    return ph.enter_context(nc.sbuf_tensor(name + "_s%d" % s_, shape, dt))
                FR = {}

                def F(*key):
                    if key not in FR:
                        FR[key] = sch.fresh()
                    return FR[key]
                memT = psb("memT", [128, 8, 256], BF16)
                mkT = psb("mkT", [128, 4, 256], BF16)
                mv = psb("mv", [128, 2, 512], BF16)
                xqT = psb("xqT", [128, 4, S], BF16)
                yx = ysb
                PTx = [psb("PTx%d" % i, [128, 512], BF16) for i in range(2)]
                rd = psb("rdx", [128, 512])
                xs = [psb("xsM%d" % i, [128, D]) for i in range(2)]
                rmsnorm_tokmajor(lambda t: mem_d[s_, t * 128:(t + 1) * 128, :], 2, memT, lambda t: F("memT"), V_NW_MEM, "m", psb, xs, F)
                for kt in range(2):
                    wb, wtr = wnext(("w_kv", kt * 256))
                    for ci in range(2):
                        hh = kt * 2 + ci
                        bk, btr = bank("proj", (0, 1, 2, 3))
                        for kc in range(8):
                            OP("pe", lambda e, bk=bk, kc=kc, ci=ci, wb=wb: e.matmul(bk[:, 0:256], lhsT=wb[:, kc, ci * 128:(ci + 1) * 128], rhs=memT[:, kc, :], start=(kc == 0), stop=(kc == 7)), r=[wtr, F("memT")], w=[btr])
                        OP("act", lambda e, bk=bk, hh=hh: e.activation(out=mkT[:, hh, :], in_=bk[:, 0:256], func=AF.Copy), r=[btr], w=[F("mkT")])
                for vt in range(2):
                    wb, wtr = wnext(("w_kv", 512 + vt * 256))
                    for mt in range(2):
                        bk, btr = bank("proj", (0, 1, 2, 3))
                        for kc in range(8):
                            OP("pe", lambda e, bk=bk, kc=kc, mt=mt, wb=wb: e.matmul(bk[:, 0:256], lhsT=memT[:, kc, mt * 128:(mt + 1) * 128], rhs=wb[:, kc, 0:256], start=(kc == 0), stop=(kc == 7)), r=[wtr, F("memT")], w=[btr])
                        OP("act", lambda e, bk=bk, mt=mt, vt=vt: e.activation(out=mv[:, mt, vt * 256:(vt + 1) * 256], in_=bk[:, 0:256], func=AF.Copy), r=[btr], w=[F("mv")])
                for qt in range(2):
                    wb, wtr = wnext(("w_in", 2832 + qt * 256))

                    def ev(ci, tb, bk, btr, qt=qt):
                        OP("act", lambda e: e.activation(out=xqT[:, qt * 2 + ci, tb * 512:(tb + 1) * 512], in_=bk[:, 0:512], func=AF.Copy), r=[btr], w=[F("xqT")])
                    proj_fm(wb, wtr, 8, [(0, 128), (128, 128)], lambda kc, tb: hT[:, kc, tb * 512:(tb + 1) * 512], lambda kc, tb: T("hT", tb), ev)
                for h in range(4):
                    for qb in range(4):
                        sl = slice(qb * 512, (qb + 1) * 512)
                        bO, tO = banks[4], T("bank", 4)
                        bDn, tDn = banks[5], T("bank", 5)
                        for mt in range(2):
                            b1, t1_ = bank("sc", (0, 1, 2, 3))
                            OP("pe", lambda e, b1=b1, h=h, mt=mt, sl=sl: e.matmul(b1[:, 0:512], lhsT=mkT[:, h, mt * 128:(mt + 1) * 128], rhs=xqT[:, h, sl], start=True, stop=True), r=[F("mkT"), F("xqT")], w=[t1_])
                            OP("act", lambda e, b1=b1, mt=mt: e.activation(out=PTx[mt][:], in_=b1[:, 0:512], func=AF.Exp, scale=128.0 ** -0.5), r=[t1_], w=[F("PTx", mt)])
                            OP("pe", lambda e, h=h, mt=mt, bO=bO: e.matmul(bO[:, 0:512], lhsT=mv[:, mt, h * 128:(h + 1) * 128], rhs=PTx[mt][:], start=(mt == 0), stop=(mt == 1)), r=[F("mv"), F("PTx", mt)], w=[tO])
                            OP("pe", lambda e, mt=mt: e.matmul(bDn[:, 0:512], lhsT=cb(C_ONES), rhs=PTx[mt][:], start=(mt == 0), stop=(mt == 1)), r=[T("cbf"), F("PTx", mt)], w=[tDn])
                        OP("dve", lambda e, rd=rd: e.reciprocal(out=rd[:], in_=bDn[:, 0:512]), r=[tDn], w=[F("rd")])
                        OP("dve", lambda e, h=h, sl=sl, bO=bO, rd=rd: e.tensor_tensor(out=yx[:, h, sl], in0=bO[:, 0:512], in1=rd[:], op=ALU.mult), r=[tO, F("rd")], w=[F("yx")])
                if DEBUG and s_ == 0:
                    OP("sp", lambda e: e.dma_start(out=dbg_d[:, 2], in_=yx[:]), r=[F("yx")], dma="dbg")
                _chk("xat")
                merge(2, yx, F("yx"), False)
            sch.barrier(scr[:, 1:2])

            with ExitStack() as ph:
                def psb(name, shape, dt=F32, ph=ph):
                    return ph.enter_context(nc.sbuf_tensor(name + "_s%d" % s_, shape, dt))
                FR = {}

                def F(*key):
                    if key not in FR:
                        FR[key] = sch.fresh()
                    return FR[key]
                xT = psb("xT", [128, 8, 1024])
                xs = [psb("xsC%d" % i, [128, D]) for i in range(2)]
                hfT = hT[:, :, 0:1024]
                uT = hT[:, :, 1024:2048]
                sq8 = psb("sq8", [128, 8, 512], BF16)
                rn = psb("rn4", [128, 512])
                rl = [psb("rl%d" % i, [128, 512]) for i in range(2)]
                yo = psb("yo", [128, 8, 128])
                ot = [psb("ot%d" % i, [128, D]) for i in range(2)]
                for hs in range(2):
                    for t8 in range(8):
                        t = hs * 8 + t8
                        xt = xs[t % 2]
                        xtr = F("xs", t % 2)
                        OP("sp", lambda e, xt=xt, t=t: e.dma_start(out=xt[:], in_=x_d[s_, t * 128:(t + 1) * 128, :]), w=[xtr], dma="x%d" % (t % 2))
                        for half in range(2):
                            bk, btr = bank("aux", (4, 5, 6))
                            for j in range(4):
                                c = half * 4 + j
                                OP("pe", lambda e, bk=bk, j=j, c=c, xt=xt: e.transpose(out=bk[:, j * 128:(j + 1) * 128], in_=xt[:, c * 128:(c + 1) * 128], identity=cf(C_IDENT)), r=[xtr, T("cst")], w=[btr])
                            OP("act" if half == 0 else "dve", lambda e, bk=bk, half=half, t8=t8: (e.activation(out=xT[:, half * 4:half * 4 + 4, t8 * 128:(t8 + 1) * 128], in_=bk[:, 0:512].rearrange("p (a b) -> p a b", a=4), func=AF.Copy) if half == 0 else e.tensor_copy(out=xT[:, half * 4:half * 4 + 4, t8 * 128:(t8 + 1) * 128], in_=bk[:, 0:512].rearrange("p (a b) -> p a b", a=4))),
                               r=[btr], w=[F("xT", hs, t8 // 4)])
                    for dt in range(4):
                        wb, wtr = wnext(("w_out", dt * 256))
                        for cc in range(2):
                            dc = dt * 2 + cc
                            for tbh in range(2):
                                tb = hs * 2 + tbh
                                bk, btr = bank("proj", (0, 1, 2, 3))
                                for kc in range(8):
                                    OP("pe", lambda e, bk=bk, kc=kc, cc=cc, tb=tb, wb=wb: e.matmul(bk[:, 0:512], lhsT=wb[:, kc, cc * 128:(cc + 1) * 128], rhs=mrg[:, kc, tb * 512:(tb + 1) * 512], start=(kc == 0), stop=(kc == 7)), r=[wtr, T("mrg", kc, tb)], w=[btr])
                                OP("dve", lambda e, bk=bk, dc=dc, tbh=tbh: e.tensor_tensor(out=xT[:, dc, tbh * 512:(tbh + 1) * 512], in0=bk[:, 0:512], in1=xT[:, dc, tbh * 512:(tbh + 1) * 512], op=ALU.add), r=[btr, F("xT", hs, tbh)], w=[F("xT", hs, tbh)])

                    def fm_norm(tbh, nwoff, dst_fn, dst_trk, eng2):
                        sl = slice(tbh * 512, (tbh + 1) * 512)
                        OP("act", lambda e: e.activation(out=sq8[:], in_=xT[:, :, sl], func=AF.Square), r=[F("xT", hs, tbh)], w=[F("sq8")])
                        bk, btr = bank("aux", (4, 5, 6))
                        for c in range(8):
                            OP("pe", lambda e, bk=bk, c=c: e.matmul(bk[:, 0:512], lhsT=cb(C_ONES), rhs=sq8[:, c, :], start=(c == 0), stop=(c == 7)), r=[F("sq8"), T("cbf")], w=[btr])
                        OP("act", lambda e, bk=bk: e.activation(out=rtmp[:], in_=bk[:, 0:512], func=AF.Sqrt, bias=EPS, scale=1.0 / 1024.0), r=[btr], w=[T("rtmp")])
                        OP("dve", lambda e, rn=rn: e.reciprocal(out=rn[:], in_=rtmp[:]), r=[T("rtmp")], w=[F("rn")])
                        for c in range(8):
                            OP("dve", lambda e, c=c, rn=rn: e.scalar_tensor_tensor(out=dst_fn(c, sl), in0=xT[:, c, sl], scalar=vec[:, nwoff + c:nwoff + c + 1], in1=rn[:], op0=ALU.mult, op1=ALU.mult),
                               r=[F("xT", hs, tbh), F("rn"), T("vec")], w=[dst_trk])
                    for tbh in range(2):
                        fm_norm(tbh, V_NW_FFN, lambda c, sl: hfT[:, c, sl], F("hfT", tbh), "dve")
                    for fg in range(4):
                        for ut in range(4):
                            wb, wtr = wnext(("w_up", fg * 1024 + ut * 256))
                            for cc in range(2):
                                fc = ut * 2 + cc
                                for tbh in range(2):
                                    sl = slice(tbh * 512, (tbh + 1) * 512)
                                    bk, btr = bank("proj", (0, 1, 2, 3))
                                    for kc in range(8):
                                        OP("pe", lambda e, bk=bk, kc=kc, cc=cc, sl=sl, wb=wb: e.matmul(bk[:, 0:512], lhsT=wb[:, kc, cc * 128:(cc + 1) * 128], rhs=hfT[:, kc, sl], start=(kc == 0), stop=(kc == 7)), r=[wtr, F("hfT", tbh)], w=[btr])
                                    rr = rl[(fc * 2 + tbh) % 2]
                                    trr = F("rl", (fc * 2 + tbh) % 2)
                                    OP("act", lambda e, bk=bk, rr=rr: e.activation(out=rr[:], in_=bk[:, 0:512], func=AF.Relu), r=[btr], w=[trr])
                                    OP("dve", lambda e, rr=rr, fc=fc, sl=sl: e.tensor_tensor(out=uT[:, fc, sl], in0=rr[:], in1=rr[:], op=ALU.mult), r=[trr], w=[F("uT", fc, tbh)])
                        for dt in range(4):
                            wb, wtr = wnext(("w_dn", dt * 256))
                            for cc in range(2):
                                dc = dt * 2 + cc
                                for tbh in range(2):
                                    sl = slice(tbh * 512, (tbh + 1) * 512)
                                    bk, btr = bank("proj", (0, 1, 2, 3))
                                    for kc in range(8):
                                        OP("pe", lambda e, bk=bk, kc=kc, cc=cc, sl=sl, wb=wb: e.matmul(bk[:, 0:512], lhsT=wb[:, kc, cc * 128:(cc + 1) * 128], rhs=uT[:, kc, sl], start=(kc == 0), stop=(kc == 7)), r=[wtr, F("uT", kc, tbh)], w=[btr])
                                    OP("dve", lambda e, bk=bk, dc=dc, sl=sl: e.tensor_tensor(out=xT[:, dc, sl], in0=bk[:, 0:512], in1=xT[:, dc, sl], op=ALU.add), r=[btr, F("xT", hs, tbh)], w=[F("xT", hs, tbh)])
                    for tbh in range(2):
                        sl = slice(tbh * 512, (tbh + 1) * 512)
                        OP("act", lambda e, sl=sl: e.activation(out=sq8[:], in_=xT[:, :, sl], func=AF.Square), r=[F("xT", hs, tbh)], w=[F("sq8")])
                        bk, btr = bank("aux", (4, 5, 6))
                        for c in range(8):
                            OP("pe", lambda e, bk=bk, c=c: e.matmul(bk[:, 0:512], lhsT=cb(C_ONES), rhs=sq8[:, c, :], start=(c == 0), stop=(c == 7)), r=[F("sq8"), T("cbf")], w=[btr])
                        OP("act", lambda e, bk=bk: e.activation(out=rtmp[:], in_=bk[:, 0:512], func=AF.Sqrt, bias=EPS, scale=1.0 / 1024.0), r=[btr], w=[T("rtmp")])
                        OP("dve", lambda e, rn=rn: e.reciprocal(out=rn[:], in_=rtmp[:]), r=[T("rtmp")], w=[F("rn")])
                        for j in range(4):
                            t = hs * 8 + tbh * 4 + j
                            tsl = slice(tbh * 512 + j * 128, tbh * 512 + (j + 1) * 128)
                            for c in range(8):
                                OP("dve", lambda e, c=c, tsl=tsl, j=j, rn=rn: e.scalar_tensor_tensor(out=yo[:, c, :], in0=xT[:, c, tsl], scalar=vec[:, V_NW_FIN + c:V_NW_FIN + c + 1], in1=rn[:, j * 128:(j + 1) * 128], op0=ALU.mult, op1=ALU.mult),
                                   r=[F("xT", hs, tbh), F("rn"), T("vec")], w=[F("yo")])
                            o_t = ot[t % 2]
                            to_t = F("ot", t % 2)
                            for half in range(2):
                                bk2, btr2 = bank("aux", (4, 5, 6))
                                for jj in range(4):
                                    c = half * 4 + jj
                                    OP("pe", lambda e, bk2=bk2, jj=jj, c=c: e.transpose(out=bk2[:, jj * 128:(jj + 1) * 128], in_=yo[:, c, :], identity=cf(C_IDENT)), r=[F("yo"), T("cst")], w=[btr2])
                                OP("act", lambda e, bk2=bk2, half=half, o_t=o_t: e.activation(out=o_t[:, half * 512:(half + 1) * 512], in_=bk2[:, 0:512], func=AF.Copy), r=[btr2], w=[to_t])
                            OP("sp", lambda e, o_t=o_t, t=t: e.dma_start(out=out_d[s_, t * 128:(t + 1) * 128, :], in_=o_t[:]), r=[to_t], dma="out%d" % (t % 2))
            sch.barrier(scr[:, 2:3])
            for tb_ in range(4):
                TK[("hT", tb_)] = sch.fresh()
            seqscope.close()

        SEQSC = []
        try:
            body()
        except _Stop:
            for sc_ in reversed(SEQSC):
                sc_.close()
        assert STOP or wstate["use"] == len(wplan), (wstate, len(wplan))
        sch.finalize()
        print("kernel: ops", len(sch.ops), {e: sum(1 for o in sch.ops if o.eng == e) for e in Sched.ENGS}, flush=True)
        sch.emit(nc, final_waits=["out0", "out1", "dbg"])
    return nc


_CACHE = {}


def kernel(**inputs):
    inp = {k: np.asarray(v) for k, v in inputs.items()}
    if "nc" not in _CACHE:
        _CACHE["nc"] = build_program()
        _CACHE["consts"] = make_consts()
    nc = _CACHE["nc"]
    cst, rope = _CACHE["consts"]
    vec = make_vec(inp)
    shared = {
        "w_in": np.ascontiguousarray(inp["w_in"][0]),
        "w_mem_kv": np.ascontiguousarray(inp["w_mem_kv"][0]),
        "w_branch": np.ascontiguousarray(inp["w_branch"][0].reshape(1536, D)),
        "w_out": np.ascontiguousarray(inp["w_out"][0]),
        "w_up": np.ascontiguousarray(inp["w_up"][0]),
        "w_down": np.ascontiguousarray(inp["w_down"][0]),
        "cst": cst, "rope": rope, "vec": vec,
    }
    in_maps = []
    ncores = int(os.environ.get("KCORES", "8"))
    for c in range(ncores):
        m = dict(shared)
        m["x"] = np.ascontiguousarray(inp["x"][c * NSEQ:(c + 1) * NSEQ])
        m["mem"] = np.ascontiguousarray(inp["mem"][c * NSEQ:(c + 1) * NSEQ])
        in_maps.append(m)
    res = run_bass_kernel_spmd(nc, in_maps, core_ids=list(range(ncores)))
    _CACHE["last"] = res
    out = np.concatenate([np.asarray(r["out"]) for r in res.results], axis=0)
    if ncores < 8:
        out = np.concatenate([out, np.zeros((16 - out.shape[0], S, D), np.float32)], axis=0)
    return out.astype(np.float32)
```

```python
import os
import numpy as np
from contextlib import ExitStack
from collections import defaultdict
import concourse.bass as bass
import concourse.mybir as mybir
from concourse.bass_utils import run_bass_kernel_spmd

F32 = mybir.dt.float32
BF16 = mybir.dt.bfloat16
AF = mybir.ActivationFunctionType
ALU = mybir.AluOpType

NSEQ = 2
S = 2048
D = 1024
NT = 16
EPS = 1e-6
DEBUG = bool(os.environ.get("KDEBUG", ""))
STOP = os.environ.get("KSTOP", "")
STRICT = bool(os.environ.get("KSTRICT"))


class _Stop(Exception):
    pass


_ST = {"stopped": False}


def _chk(tag):
    if STOP == tag:
        _ST["stopped"] = True


class Trk:
    __slots__ = ("last_w", "readers")

    def __init__(self):
        self.last_w = None
        self.readers = {}


class Op:
    __slots__ = ("eng", "fn", "deps", "signal", "dma_sem", "dma_val", "idx", "sigval")

    def __init__(self, eng, fn):
        self.eng = eng
        self.fn = fn
        self.deps = {}
        self.signal = False
        self.dma_sem = None
        self.dma_val = 0
        self.idx = -1
        self.sigval = 0


class Sched:
    ENGS = ("pe", "act", "dve", "pool", "sp")

    def __init__(self, nc):
        self.nc = nc
        self.ops = []
        self.dma_sems = {}
        self.eng_sems = {}
        self.last_on = {}

    def op(self, eng, fn, reads=(), writes=(), dma=None):
        o = Op(eng, fn)
        o.idx = len(self.ops)
        for t in reads:
            if t.last_w is not None:
                o.deps[t.last_w] = True
        for t in writes:
            if t.last_w is not None:
                o.deps.setdefault(t.last_w, False)
            for r in t.readers.values():
                o.deps.setdefault(r, False)
        for t in reads:
            t.readers[eng if dma is None else ("dma", dma)] = o.idx
        for t in writes:
            t.last_w = o.idx
            t.readers = {}
        o.deps.pop(o.idx, None)
        if dma is not None:
            ent = self.dma_sems[dma]
            ent[1] += 16
            o.dma_sem = dma
            o.dma_val = ent[1]
            self.last_on[("dma", dma)] = o.idx
        else:
            self.last_on[eng] = o.idx
        self.ops.append(o)
        return o

    def barrier(self, scratch_ap):
        if _ST["stopped"]:
            return None
        o = Op("dve", lambda e: e.memset(scratch_ap, 0.0))
        o.idx = len(self.ops)
        for k, v in self.last_on.items():
            o.deps[v] = True
        self.last_on["dve"] = o.idx
        self.ops.append(o)
        self.bar = o.idx
        return o.idx

    def fresh(self):
        t = Trk()
        t.last_w = getattr(self, "bar", None)
        return t

    def new_dma_sem(self, name, handle):
        self.dma_sems[name] = [handle, 0]

    def _skip(self, p, o):
        return p.dma_sem is None and o.dma_sem is None and p.eng == o.eng

    def finalize(self):
        ops = self.ops
        for o in ops:
            for d, raw in o.deps.items():
                p = ops[d]
                if p.dma_sem is None:
                    if self._skip(p, o) and (o.eng == "pe" or (not raw and not STRICT)):
                        continue
                    p.signal = True
        cnt = {e: 0 for e in self.ENGS}
        for o in ops:
            if o.dma_sem is None and o.signal:
                cnt[o.eng] += 1
                o.sigval = cnt[o.eng]

    def emit(self, nc, final_waits=(), max_pe=int(os.environ.get("KMAXPE", "4000"))):
        ops = self.ops
        sems = self.eng_sems
        dma_sems = self.dma_sems
        segments = []
        cur = []
        npe = 0
        for o in ops:
            cur.append(o)
            if o.eng == "pe":
                npe += 1
            if npe >= max_pe or len(cur) >= 4 * max_pe:
                segments.append(cur)
                cur = []
                npe = 0
        if cur:
            segments.append(cur)
        waited = {e: {} for e in self.ENGS}
        self._skipfn = self._skip

        def run(engname, eng, seg_ops, last):
            wd = waited[engname]
            for o in seg_ops:
                need = {}
                for d, raw in o.deps.items():
                    p = ops[d]
                    if p.dma_sem is not None:
                        key = ("d", p.dma_sem)
                        val = p.dma_val
                    else:
                        if self._skip(p, o) and (engname == "pe" or (not raw and not STRICT)):
                            continue
                        key = ("e", p.eng)
                        val = p.sigval
                    if need.get(key, 0) < val:
                        need[key] = val
                for key, val in need.items():
                    if wd.get(key, 0) >= val:
                        continue
                    wd[key] = val
                    h = dma_sems[key[1]][0] if key[0] == "d" else sems[key[1]]
                    eng.wait_ge(h, val)
                if o.fn is None:
                    continue
                ins = o.fn(eng)
                if o.dma_sem is not None:
                    ins.then_inc(dma_sems[o.dma_sem][0], 16)
                elif o.signal:
                    ins.then_inc(sems[engname], 1)
            if engname == "sp" and last:
                for name in final_waits:
                    h, c = dma_sems[name]
                    if c > 0:
                        eng.wait_ge(h, c)

        for si, seg in enumerate(segments):
            last = (si == len(segments) - 1)
            per_eng = {e: [o for o in seg if o.eng == e] for e in self.ENGS}
            with nc.Block() as block:
                if per_eng["pe"]:
                    block.tensor(lambda e, l=per_eng["pe"]: run("pe", e, l, last))
                if per_eng["act"]:
                    block.scalar(lambda e, l=per_eng["act"]: run("act", e, l, last))
                if per_eng["dve"]:
                    block.vector(lambda e, l=per_eng["dve"]: run("dve", e, l, last))
                if per_eng["pool"]:
                    block.gpsimd(lambda e, l=per_eng["pool"]: run("pool", e, l, last))
                if per_eng["sp"] or last:
                    block.sync(lambda e, l=per_eng["sp"]: run("sp", e, l, last))
        print("kernel: blocks", len(segments), flush=True)


C_IDENT, C_UF, C_UB, C_ONES, C_MASKF, C_MASKB, C_STRF, C_STRB, C_PERM, C_BD64, C_SWLO, C_SWHI, C_NEG1 = range(13)
NCST = 13


def make_consts():
    i = np.arange(128)
    c = np.zeros((NCST, 128, 128), np.float32)
    c[C_IDENT] = np.eye(128)
    c[C_UF] = (i[:, None] <= i[None, :])
    c[C_UB] = (i[:, None] >= i[None, :])
    c[C_ONES] = 1.0
    c[C_MASKF] = np.where(i[None, :] <= i[:, None], 0.0, -1e30)
    c[C_MASKB] = np.where(i[None, :] >= i[:, None], 0.0, -1e30)
    c[C_STRF] = (i[None, :] < i[:, None])
    c[C_STRB] = (i[None, :] > i[:, None])
    d = i % 64
    partner = np.where((d % 32) < 16, i + 16, i - 16)
    pm = np.zeros((128, 128), np.float32)
    pm[partner, i] = 1.0
    c[C_PERM] = pm
    c[C_BD64] = ((i[:, None] // 64) == (i[None, :] // 64))
    c[C_SWLO] = (i[:, None] == i[None, :] + 64)
    c[C_SWHI] = (i[:, None] + 64 == i[None, :])
    c[C_NEG1] = -1.0
    cst = np.ascontiguousarray(c.transpose(1, 0, 2).reshape(128, NCST * 128))
    t = np.arange(S)
    inv = (10000.0 ** (-np.arange(16, dtype=np.float32) / 16)).astype(np.float32)
    row = (t // 64).astype(np.float32)
    col = (t % 64).astype(np.float32)
    ang = np.stack([row[:, None] * inv[None, :], col[:, None] * inv[None, :]], 0)
    cos = np.cos(ang).astype(np.float32)
    sin = np.sin(ang).astype(np.float32)
    rope = np.zeros((128, 2, S), np.float32)
    for p in range(128):
        dd = p % 64
        ax = dd // 32
        half = (dd % 32) // 16
        pr = dd % 16
        rope[p, 0] = cos[ax, :, pr]
        rope[p, 1] = (-sin[ax, :, pr]) if half == 0 else sin[ax, :, pr]
    return cst, rope


V_NW_MIX, V_NW_MEM, V_NW_FFN, V_NW_FIN, V_CONV, V_DNW, V_QNW, V_KNW, V_ALOG, V_DTB = 0, 8, 16, 24, 32, 92, 93, 94, 95, 96
NVEC = 100


def make_vec(inp):
    v = np.zeros((128, NVEC), np.float32)
    v[:, V_NW_MIX:V_NW_MIX + 8] = inp["mix_norm_w"][0].reshape(8, 128).T
    v[:, V_NW_MEM:V_NW_MEM + 8] = inp["mem_norm_w"][0].reshape(8, 128).T
    v[:, V_NW_FFN:V_NW_FFN + 8] = inp["ffn_norm_w"][0].reshape(8, 128).T
    v[:, V_NW_FIN:V_NW_FIN + 8] = inp["final_norm_w"].reshape(8, 128).T
    cw = inp["dn_conv_w"][0]
    v[:, V_CONV:V_CONV + 60] = cw.reshape(5, 12, 128).transpose(2, 1, 0).reshape(128, 60)
    v[:, V_DNW] = inp["dn_norm_w"][0]
    v[:, V_QNW] = np.tile(inp["q_norm_w"][0], 2)
    v[:, V_KNW] = np.tile(inp["k_norm_w"][0], 2)
    v[0:8, V_ALOG] = inp["dn_a_log"][0].reshape(8)
    v[0:8, V_DTB] = inp["dn_dt_bias"][0].reshape(8)
    return v


def build_program():
    _ST["stopped"] = False
    nc = bass.Bass("TRN2", target_bir_lowering=False)

    def dram(name, shape, kind="ExternalInput"):
        return nc.dram_tensor(name, shape, F32, kind=kind).ap()

    x_d = dram("x", [NSEQ, S, D])
    mem_d = dram("mem", [NSEQ, 256, D])
    w_in_d = dram("w_in", [D, 6416])
    w_kv_d = dram("w_mem_kv", [D, 1024])
    w_br_d = dram("w_branch", [1536, D])
    w_out_d = dram("w_out", [D, D])
    w_up_d = dram("w_up", [D, 4096])
    w_dn_d = dram("w_down", [4096, D])
    cst_d = dram("cst", [128, NCST * 128])
    rope_d = dram("rope", [128, 2, S])
    vec_d = dram("vec", [128, NVEC])
    out_d = dram("out", [NSEQ, S, D], kind="ExternalOutput")
    dbg_d = nc.dram_tensor("dbg", [128, 8, 4, S], BF16, kind="ExternalOutput").ap() if DEBUG else None
    dbgf_d = nc.dram_tensor("dbgf", [128, 4, S], F32, kind="ExternalOutput").ap() if DEBUG else None
    wmap = {"w_in": w_in_d, "w_kv": w_kv_d, "w_br": w_br_d, "w_out": w_out_d, "w_up": w_up_d, "w_dn": w_dn_d}

    es = ExitStack()
    with es:
        def sb(name, shape, dt=F32):
            return es.enter_context(nc.sbuf_tensor(name, shape, dt))

        sch = Sched(nc)
        for e in Sched.ENGS:
            sch.eng_sems[e] = es.enter_context(nc.semaphore("sem_" + e))

        def dsem(name):
            sch.new_dma_sem(name, es.enter_context(nc.semaphore("ds_" + name)))

        TK = defaultdict(Trk)

        def T(*key):
            return TK[key]

        def OP(eng, fn, r=(), w=(), dma=None):
            if _ST["stopped"]:
                return None
            return sch.op(eng, fn, reads=r, writes=w, dma=dma)

        cst = sb("cst_sb", [128, NCST, 128])
        cbf = sb("cbf", [128, NCST, 128], BF16)
        vec = sb("vec_sb", [128, NVEC])
        nA = sb("nA", [8, 1])
        scr = sb("scr", [128, 4])
        rtmp = sb("rtmp", [128, 512])
        wst = [sb("wst%d" % i, [128, 8, 256]) for i in range(2)]
        wbf = [sb("wbf%d" % i, [128, 8, 256], BF16) for i in range(3)]
        hT = sb("hT", [128, 8, S], BF16)
        banks = [es.enter_context(nc.psum_tensor("bank%d" % i, [128, 512], F32)) for i in range(7)]
        pTb = es.enter_context(nc.psum_tensor("pTb", [128, 1024], BF16))
        for n_ in ["cst", "vec", "w0", "w1", "x0", "x1", "out0", "out1", "misc", "dbg"]:
            dsem(n_)

        def cf(i):
            return cst[:, i, :]

        def cb(i):
            return cbf[:, i, :]

        OP("sp", lambda e: e.dma_start(out=cst[:].rearrange("p a b -> p (a b)"), in_=cst_d), w=[T("cst")], dma="cst")
        OP("sp", lambda e: e.dma_start(out=vec[:], in_=vec_d), w=[T("vec")], dma="vec")
        OP("dve", lambda e: e.tensor_copy(out=cbf[:], in_=cst[:]), r=[T("cst")], w=[T("cbf")])
        OP("act", lambda e: e.activation(out=nA[:], in_=vec[0:8, V_ALOG:V_ALOG + 1], func=AF.Exp), r=[T("vec")], w=[T("nA")])
        OP("dve", lambda e: e.tensor_scalar_mul(out=nA[:], in0=nA[:], scalar1=-1.0), r=[T("nA")], w=[T("nA")])
        CONSTS = [T("cst"), T("cbf"), T("vec"), T("nA")]

        bank_rr = defaultdict(int)

        def bank(group, ids):
            i = ids[bank_rr[group] % len(ids)]
            bank_rr[group] += 1
            return banks[i], T("bank", i)

        wplan = []
        for s_ in range(NSEQ):
            for c0 in range(0, 1536, 256):
                wplan.append(("w_in", 0, 8, c0, 256))
            wplan.append(("w_in", 0, 8, 2048, 256))
            for c0 in range(1536, 2048, 256):
                wplan.append(("w_in", 0, 8, c0, 256))
            for dt in range(4):
                wplan.append(("w_in", 0, 8, 3344 + 0 * 1024 + dt * 256, 256))
                wplan.append(("w_br", 0 * 512, 4, dt * 256, 256))
            for c0 in range(2064, 2576, 256):
                wplan.append(("w_in", 0, 8, c0, 256))
            wplan.append(("kdup", 0, 8, 2576, 256))
            wplan.append(("w_in", 0, 8, 2704, 128))
            for dt in range(4):
                wplan.append(("w_in", 0, 8, 3344 + 1 * 1024 + dt * 256, 256))
                wplan.append(("w_br", 1 * 512, 4, dt * 256, 256))
            for c0 in range(0, 1024, 256):
                wplan.append(("w_kv", 0, 8, c0, 256))
            for c0 in range(2832, 3344, 256):
                wplan.append(("w_in", 0, 8, c0, 256))
            for dt in range(4):
                wplan.append(("w_in", 0, 8, 3344 + 2 * 1024 + dt * 256, 256))
                wplan.append(("w_br", 2 * 512, 4, dt * 256, 256))
            for hs in range(2):
                for dt in range(4):
                    wplan.append(("w_out", 0, 8, dt * 256, 256))
                for fg in range(4):
                    for ut in range(4):
                        wplan.append(("w_up", 0, 8, fg * 1024 + ut * 256, 256))
                    for dt in range(4):
                        wplan.append(("w_dn", fg * 1024, 8, dt * 256, 256))
        wstate = {"dma": 0, "cast": 0, "use": 0}

        def w_issue_dma(i):
            name, k0, KC, c0, ncols = wplan[i]
            st = wst[i % 2]
            sem = "w%d" % (i % 2)
            tr = T("wst", i % 2)
            if name == "kdup":
                for j in range(4):
                    src = w_in_d[0:D, 2576 + (j // 2) * 64: 2576 + (j // 2) * 64 + 64].rearrange("(kc p) n -> p kc n", p=128)
                    OP("sp", lambda e, st=st, src=src, j=j: e.dma_start(out=st[:, 0:8, j * 64:(j + 1) * 64], in_=src),
                       w=[tr], dma=sem)
            else:
                src = wmap[name][k0:k0 + KC * 128, c0:c0 + ncols].rearrange("(kc p) n -> p kc n", p=128)
                OP("sp", lambda e, st=st, src=src, KC=KC, ncols=ncols: e.dma_start(out=st[:, 0:KC, 0:ncols], in_=src),
                   w=[tr], dma=sem)

        def w_issue_cast(i):
            name, k0, KC, c0, ncols = wplan[i]
            st = wst[i % 2]
            wb = wbf[i % 3]
            if i % 2 == 0:
                OP("act", lambda e, st=st, wb=wb, KC=KC, ncols=ncols: e.activation(out=wb[:, 0:KC, 0:ncols], in_=st[:, 0:KC, 0:ncols], func=AF.Copy),
                   r=[T("wst", i % 2)], w=[T("wbf", i % 3)])
            else:
                OP("dve", lambda e, st=st, wb=wb, KC=KC, ncols=ncols: e.tensor_copy(out=wb[:, 0:KC, 0:ncols], in_=st[:, 0:KC, 0:ncols]),
                   r=[T("wst", i % 2)], w=[T("wbf", i % 3)])

        def wnext(expect):
            i = wstate["use"]
            if _ST["stopped"]:
                wstate["use"] += 1
                return wbf[i % 3], T("wbf", i % 3)
            assert wplan[i][0] == expect[0] and wplan[i][3] == expect[1], (wplan[i], expect)
            while wstate["dma"] < min(len(wplan), i + 2):
                w_issue_dma(wstate["dma"])
                wstate["dma"] += 1
            while wstate["cast"] < min(len(wplan), i + 2):
                if wstate["dma"] <= wstate["cast"]:
                    w_issue_dma(wstate["dma"])
                    wstate["dma"] += 1
                w_issue_cast(wstate["cast"])
                wstate["cast"] += 1
                while wstate["dma"] < min(len(wplan), wstate["cast"] + 2):
                    w_issue_dma(wstate["dma"])
                    wstate["dma"] += 1
            wstate["use"] += 1
            return wbf[i % 3], T("wbf", i % 3)

        def proj_fm(wb, wtr, KC, col_chunks, rhs_fn, rhs_trk_fn, evac_fn, ntb=4, bgroup=("proj", (0, 1, 2, 3))):
            for ci, (co, m) in enumerate(col_chunks):
                for tb in range(ntb):
                    bk, btr = bank(*bgroup)
                    for kc in range(KC):
                        OP("pe", lambda e, bk=bk, kc=kc, co=co, m=m, tb=tb: e.matmul(
                            bk[0:m, 0:512], lhsT=wb[:, kc, co:co + m], rhs=rhs_fn(kc, tb), start=(kc == 0), stop=(kc == KC - 1)),
                           r=[wtr, rhs_trk_fn(kc, tb)], w=[btr])
                    evac_fn(ci, tb, bk, btr)

        def rmsnorm_tokmajor(src_ap_fn, ntiles, dstT, dst_trk_fn, nw_off, tag, sb, xs, F):
            junk = sb("junk_" + tag, [128, D], BF16)
            xn = [sb("xn%d_" % i + tag, [128, D], BF16) for i in range(2)]
            st = sb("st_" + tag, [128, 4])
            for t in range(ntiles):
                xt = xs[t % 2]
                xtr = F("xs", t % 2)
                OP("sp", lambda e, xt=xt, t=t: e.dma_start(out=xt[:], in_=src_ap_fn(t)), w=[xtr], dma="x%d" % (t % 2))
                OP("act", lambda e, xt=xt: e.activation(out=junk[:], in_=xt[:], func=AF.Square, scale=1.0 / 32.0, accum_out=st[:, 0:1]),
                   r=[xtr], w=[F("junk", tag), F("st", tag)])
                OP("act", lambda e: e.activation(out=st[:, 1:2], in_=st[:, 0:1], func=AF.Sqrt, bias=EPS, scale=1.0),
                   r=[F("st", tag)], w=[F("st", tag)])
                OP("dve", lambda e: e.reciprocal(out=st[:, 2:3], in_=st[:, 1:2]), r=[F("st", tag)], w=[F("st", tag)])
                xnt = xn[t % 2]
                xntr = F("xn", tag, t % 2)
                OP("dve", lambda e, xt=xt, xnt=xnt: e.tensor_scalar_mul(out=xnt[:], in0=xt[:], scalar1=st[:, 2:3]),
                   r=[xtr, F("st", tag)], w=[xntr])
                for c in range(8):
                    OP("pe", lambda e, c=c, xnt=xnt: e.transpose(out=pTb[:, c * 128:(c + 1) * 128], in_=xnt[:, c * 128:(c + 1) * 128], identity=cb(C_IDENT)),
                       r=[xntr, T("cbf")], w=[T("pTb", 0)])
                OP("dve", lambda e, t=t: e.tensor_tensor(
                    out=dstT[:, :, t * 128:(t + 1) * 128], in0=pTb[:].rearrange("p (c j) -> p c j", c=8),
                    in1=vec[:, nw_off:nw_off + 8].unsqueeze(2).broadcast_to([128, 8, 128]), op=ALU.mult),
                   r=[T("pTb", 0), T("pTb", 0), T("vec")], w=[dst_trk_fn(t)])

        def sumsq_rn(src_ap, src_trks, nparts_scale, lhsT_const, dst_rn, dst_trk, tmp_sq, tmp_trk, bgroup):
            OP("act", lambda e: e.activation(out=tmp_sq, in_=src_ap, func=AF.Square), r=src_trks, w=[tmp_trk])
            bk, btr = bank(*bgroup)
            OP("pe", lambda e: e.matmul(bk[:, 0:512], lhsT=lhsT_const, rhs=tmp_sq, start=True, stop=True), r=[tmp_trk, T("cbf")], w=[btr])
            OP("act", lambda e: e.activation(out=rtmp[:], in_=bk[:, 0:512], func=AF.Sqrt, bias=EPS, scale=nparts_scale), r=[btr], w=[T("rtmp")])
            OP("dve", lambda e: e.reciprocal(out=dst_rn, in_=rtmp[:]), r=[T("rtmp")], w=[dst_trk])

        def body():
          for s_ in range(NSEQ):
            body_seq(s_)

        def body_seq(s_):
            nonlocal_dummy = None
            seqscope = ExitStack()
            SEQSC.append(seqscope)
            ysb = seqscope.enter_context(nc.sbuf_tensor("ysb_s%d" % s_, [128, 4, S], BF16))
            with ExitStack() as ph:
                def psb(name, shape, dt=F32, ph=ph):
                    return ph.enter_context(nc.sbuf_tensor(name + "_s%d" % s_, shape, dt))
                FR = {}

                def F(*key):
                    if key not in FR:
                        FR[key] = sch.fresh()
                    return FR[key]
                xs = [psb("xsA%d" % i, [128, D]) for i in range(2)]
                rmsnorm_tokmajor(lambda t: x_d[s_, t * 128:(t + 1) * 128, :], NT, hT, lambda t: T("hT", t // 4), V_NW_MIX, "a", psb, xs, F)
            sch.barrier(scr[:, 0:1])
            _chk("A")

            with ExitStack() as ph:
                def psb(name, shape, dt=F32, ph=ph):
                    return ph.enter_context(nc.sbuf_tensor(name + "_s%d" % s_, shape, dt))
                qT = psb("qT", [128, 4, S], BF16)
                kT = psb("kT", [128, 4, S], BF16)
                vtok = psb("vtok", [128, NT, 4, 128], BF16)
                ydn = ysb
                FR = {}

                def F(*key):
                    if key not in FR:
                        FR[key] = sch.fresh()
                    return FR[key]

                with ExitStack() as ph2:
                    def psb2(name, shape, dt=F32, ph2=ph2):
                        return ph2.enter_context(nc.sbuf_tensor(name + "_s%d" % s_, shape, dt))
                    pre = [psb2("pre%d" % i, [128, S + 128]) for i in range(2)]
                    cacc = [psb2("cacc%d" % i, [128, S]) for i in range(2)]
                    sqb = psb2("sqb", [128, 512], BF16)
                    rn = psb2("rn", [128, 512])
                    vTt = psb2("vTt", [128, S], BF16)
                    ctmp = psb2("ctmp", [128, S])
                    for i in range(2):
                        OP("dve", lambda e, i=i: e.memset(pre[i][:, 0:64], 0.0), w=[F("prepad", i)])
                        OP("dve", lambda e, i=i: e.memset(pre[i][:, S + 64:S + 128], 0.0), w=[F("prepad", i)])
                    for c in range(12):
                        if c % 2 == 0:
                            wb, wtr = wnext(("w_in", c * 128))
                        pb = pre[c % 2]
                        ca = cacc[c % 2]
                        if c < int(os.environ.get("KSKIP", "0")):
                            continue

                        def ev(ci, tb, bk, btr, pb=pb, c=c):
                            OP("act", lambda e: e.activation(out=pb[:, 64 + tb * 512: 64 + (tb + 1) * 512], in_=bk[:, 0:512], func=AF.Copy),
                               r=[btr], w=[F("pre", c % 2, tb)])
                        proj_fm(wb, wtr, 8, [((c % 2) * 128, 128)], lambda kc, tb: hT[:, kc, tb * 512:(tb + 1) * 512],
                                lambda kc, tb: T("hT", tb), ev)
                        _chk("c%dproj" % c)
                        ce = "dve"
                        pre_tr = [F("pre", c % 2, tb) for tb in range(4)] + [F("prepad", c % 2)]
                        OP(ce, lambda e, ca=ca, pb=pb, c=c: e.tensor_scalar_mul(out=ca[:], in0=pb[:, 62:62 + S], scalar1=vec[:, V_CONV + c * 5:V_CONV + c * 5 + 1]),
                           r=pre_tr + [T("vec")], w=[F("cacc", c % 2)])
                        for j in range(1, 5):
                            if ce == "dve":
                                OP(ce, lambda e, ca=ca, pb=pb, c=c, j=j: e.scalar_tensor_tensor(
                                    out=ca[:], in0=pb[:, 62 + j:62 + j + S], scalar=vec[:, V_CONV + c * 5 + j:V_CONV + c * 5 + j + 1], in1=ca[:],
                                    op0=ALU.mult, op1=ALU.add), r=pre_tr + [F("cacc", c % 2), T("vec")], w=[F("cacc", c % 2)])
                            else:
                                OP(ce, lambda e, pb=pb, c=c, j=j: e.tensor_scalar_mul(out=ctmp[:], in0=pb[:, j:j + S], scalar1=vec[:, V_CONV + c * 5 + j:V_CONV + c * 5 + j + 1]),
                                   r=pre_tr + [T("vec")], w=[F("ctmp")])
                                OP(ce, lambda e, ca=ca: e.tensor_tensor(out=ca[:], in0=ca[:], in1=ctmp[:], op=ALU.add), r=[F("ctmp"), F("cacc", c % 2)], w=[F("cacc", c % 2)])
                        _chk("c%dconv" % c)
                        h = c % 4
                        if c >= 8:
                            OP("act", lambda e, ca=ca: e.activation(out=vTt[:], in_=ca[:], func=AF.Silu), r=[F("cacc", c % 2)], w=[F("vTt")])
                            src, strk, dst, dtrk = vTt, F("vTt"), vtok, F("vtok")
                        else:
                            OP("act", lambda e, ca=ca: e.activation(out=ca[:], in_=ca[:], func=AF.Silu), r=[F("cacc", c % 2)], w=[F("cacc", c % 2)])
                            dstT = qT if c < 4 else kT
                            dtr = F("qT", h) if c < 4 else F("kT", h)
                            for tb in range(4):
                                sl = slice(tb * 512, (tb + 1) * 512)
                                sumsq_rn(ca[:, sl], [F("cacc", c % 2)], 1.0, cb(C_ONES), rn[:], F("rn"), sqb[:], F("sqb"), ("aux", (4, 5)))
                                OP("dve", lambda e, ca=ca, sl=sl, dstT=dstT, h=h, c=c, rn=rn: e.scalar_tensor_tensor(
                                    out=dstT[:, h, sl], in0=ca[:, sl], scalar=(128.0 ** -0.5 if c < 4 else 1.0), in1=rn[:],
                                    op0=ALU.mult, op1=ALU.mult), r=[F("cacc", c % 2), F("rn")], w=[dtr])
                            src, strk, dst, dtrk = (None, None, None, None)
                        _chk("c%dnorm" % c)
                        if src is not None and not os.environ.get("KNOVT"):
                            for n0 in range(0, NT, 4):
                                hf = 0 if os.environ.get("KHF0") else (n0 // 4) % 2
                                for j in range(4):
                                    n = n0 + j
                                    sap = src[:, n * 128:(n + 1) * 128]
                                    OP("pe", lambda e, sap=sap, hf=hf, j=j: e.transpose(out=pTb[:, hf * 512 + j * 128: hf * 512 + (j + 1) * 128], in_=sap, identity=cb(C_IDENT)),
                                       r=[strk, T("cbf")], w=[T("pTb", 0)])
                                OP("dve", lambda e, hf=hf, n0=n0, dst=dst, h=h: e.tensor_scalar_mul(out=dst[:, n0:n0 + 4, h, :], in0=pTb[:, hf * 512:(hf + 1) * 512].rearrange("p (a b) -> p a b", a=4), scalar1=1.0),
                                   r=[T("pTb", 0)], w=[dtrk])
                sch.barrier(scr[:, 1:2])
                _chk("conv")
                FR.clear()
                oacc = psb("oacc", [128, 4, S])
                with ExitStack() as ph2:
                    def psb2(name, shape, dt=F32, ph2=ph2):
                        return ph2.enter_context(nc.sbuf_tensor(name + "_s%d" % s_, shape, dt))
                    btok = psb2("btok", [128, NT, 8])
                    gtk = psb2("gtk", [128, 2, NT, 4])
                    gcs = psb2("gcs", [128, NT, 8])
                    gts = psb2("gts", [128, NT, 8])
                    egc = psb2("egc", [128, NT, 8])
                    nbe = psb2("nbe", [128, NT, 8])
                    nbt = psb2("nbt", [128, NT, 8])
                    ekd = psb2("ekd", [128, NT, 8])
                    egt = psb2("egt", [128, NT, 8])
                    ph2b = ph2.enter_context(ExitStack())
                    bT = ph2b.enter_context(nc.sbuf_tensor("bT_s%d" % s_, [8, S], F32))
                    gT = ph2b.enter_context(nc.sbuf_tensor("gT_s%d" % s_, [8, S], F32))
                    t1 = ph2b.enter_context(nc.sbuf_tensor("t1_s%d" % s_, [8, S], F32))
                    t2 = ph2b.enter_context(nc.sbuf_tensor("t2_s%d" % s_, [8, S], F32))
                    wb, wtr = wnext(("w_in", 2048))
                    for tb in range(4):
                        sl = slice(tb * 512, (tb + 1) * 512)
                        for which in range(2):
                            bk, btr = bank("proj", (0, 1, 2, 3))
                            for kc in range(8):
                                OP("pe", lambda e, bk=bk, kc=kc, which=which, sl=sl, wb=wb: e.matmul(bk[0:8, 0:512], lhsT=wb[:, kc, which * 8:which * 8 + 8], rhs=hT[:, kc, sl], start=(kc == 0), stop=(kc == 7)),
                                   r=[wtr, T("hT", tb)], w=[btr])
                            if which == 0:
                                OP("act", lambda e, bk=bk, sl=sl: e.activation(out=bT[:, sl], in_=bk[0:8, 0:512], func=AF.Sigmoid), r=[btr], w=[F("bT")])
                            else:
                                OP("act", lambda e, bk=bk, sl=sl: e.activation(out=t1[:, sl], in_=bk[0:8, 0:512], func=AF.Identity, bias=vec[0:8, V_DTB:V_DTB + 1], scale=1.0),
                                   r=[btr, T("vec")], w=[F("t1")])
                    OP("act", lambda e: e.activation(out=t2[:], in_=t1[:], func=AF.Abs), r=[F("t1")], w=[F("t2")])
                    OP("act", lambda e: e.activation(out=t2[:], in_=t2[:], func=AF.Exp, scale=-1.0), r=[F("t2")], w=[F("t2")])
                    OP("act", lambda e: e.activation(out=t2[:], in_=t2[:], func=AF.Ln, bias=1.0, scale=1.0), r=[F("t2")], w=[F("t2")])
                    OP("dve", lambda e: e.tensor_scalar_max(out=t1[:], in0=t1[:], scalar1=0.0), r=[F("t1")], w=[F("t1")])
                    OP("dve", lambda e: e.tensor_tensor(out=t1[:], in0=t1[:], in1=t2[:], op=ALU.add), r=[F("t1"), F("t2")], w=[F("t1")])
                    OP("dve", lambda e: e.tensor_scalar_mul(out=gT[:], in0=t1[:], scalar1=nA[:, 0:1]), r=[F("t1"), T("nA")], w=[F("gT")])
                    bk, btr = bank("aux", (4, 5))
                    for n in range(NT):
                        OP("pe", lambda e, n=n, bk=bk: e.transpose(out=bk[:, n * 8:(n + 1) * 8], in_=bT[0:8, n * 128:(n + 1) * 128], identity=cst[0:8, C_IDENT, 0:8]),
                           r=[F("bT"), T("cst")], w=[btr])
                        OP("pe", lambda e, n=n, bk=bk: e.transpose(out=bk[:, 128 + n * 8:128 + (n + 1) * 8], in_=gT[0:8, n * 128:(n + 1) * 128], identity=cst[0:8, C_IDENT, 0:8]),
                           r=[F("gT"), T("cst")], w=[btr])
                    OP("dve", lambda e, bk=bk: e.tensor_copy(out=btok[:], in_=bk[:, 0:128].rearrange("p (n r) -> p n r", r=8)), r=[btr], w=[F("btok")])
                    for dr in range(2):
                        OP("dve", lambda e, bk=bk, dr=dr: e.tensor_copy(out=gtk[:, dr], in_=bk[:, 128:256].rearrange("p (n r) -> p n r", r=8)[:, :, dr * 4:dr * 4 + 4]),
                           r=[btr], w=[F("gtk")])
                    bk2, btr2 = bank("aux", (4, 5))
                    for dr in range(2):
                        OP("pe", lambda e, dr=dr, bk2=bk2: e.matmul(bk2[:, dr * 64:(dr + 1) * 64], lhsT=cf(C_UF if dr == 0 else C_UB), rhs=gtk[:, dr].rearrange("p n r -> p (n r)"), start=True, stop=True),
                           r=[F("gtk"), T("cst")], w=[btr2])
                        OP("pe", lambda e, dr=dr, bk2=bk2: e.matmul(bk2[:, 128 + dr * 64:128 + (dr + 1) * 64], lhsT=cf(C_ONES), rhs=gtk[:, dr].rearrange("p n r -> p (n r)"), start=True, stop=True),
                           r=[F("gtk"), T("cst")], w=[btr2])
                    for dr in range(2):
                        OP("dve", lambda e, dr=dr, bk2=bk2: e.tensor_copy(out=gcs[:, :, dr * 4:dr * 4 + 4], in_=bk2[:, dr * 64:(dr + 1) * 64].rearrange("p (n r) -> p n r", r=4)), r=[btr2], w=[F("gcs")])
                        OP("dve", lambda e, dr=dr, bk2=bk2: e.tensor_copy(out=gts[:, :, dr * 4:dr * 4 + 4], in_=bk2[:, 128 + dr * 64:128 + (dr + 1) * 64].rearrange("p (n r) -> p n r", r=4)), r=[btr2], w=[F("gts")])
                    OP("act", lambda e: e.activation(out=egc[:], in_=gcs[:], func=AF.Exp), r=[F("gcs")], w=[F("sc")])
                    OP("act", lambda e: e.activation(out=egt[:], in_=gts[:], func=AF.Exp), r=[F("gts")], w=[F("sc")])
                    OP("dve", lambda e: e.tensor_tensor(out=ekd[:], in0=gts[:], in1=gcs[:], op=ALU.subtract), r=[F("gts"), F("gcs")], w=[F("sc")])
                    OP("act", lambda e: e.activation(out=ekd[:], in_=ekd[:], func=AF.Exp), r=[F("sc")], w=[F("sc")])
                    OP("dve", lambda e: e.tensor_scalar_mul(out=nbt[:], in0=btok[:], scalar1=-1.0), r=[F("btok")], w=[F("sc")])
                    OP("dve", lambda e: e.tensor_tensor(out=nbe[:], in0=nbt[:], in1=egc[:], op=ALU.mult), r=[F("sc")], w=[F("sc")])
                    SC = F("sc")
                    _chk("tables")
                    ph2b.close()
                    sch.barrier(scr[:, 3:4])
                    for k_ in ("sc", "btok", "gtk"):
                        FR[(k_,)] = sch.fresh()
                    for h_ in range(4):
                        FR[("kT", h_)] = sch.fresh()
                        FR[("qT", h_)] = sch.fresh()
                    FR[("vtok",)] = sch.fresh()
                    SC = F("sc")

                    GU = psb2("GU", [128, 4, 128])
                    Dm = psb2("Dm", [128, 4, 128])
                    Dsn = psb2("Dsn", [128, 4, 128])
                    P0 = [psb2("Pk%d" % i, [128, 4, 128], BF16) for i in range(2)]
                    M0 = [psb2("Mk%d" % i, [128, 4, 128], BF16) for i in range(2)]
                    Y0 = [psb2("Yk%d" % i, [128, 4, 128], BF16) for i in range(2)]
                    Abf = psb2("Abf", [128, 4, 128], BF16)
                    ATb = psb2("ATb", [128, 4, 128], BF16)
                    Sst = [psb2("Sst%d" % i, [128, 4, 128]) for i in range(2)]
                    Sbf = [psb2("Sbf%d" % i, [128, 4, 128], BF16) for i in range(2)]
                    VBt = psb2("VBt", [128, 4, 128])
                    R1 = psb2("R1", [128, 4, 128])
                    Rb = psb2("Rb", [128, 4, 128], BF16)
                    vn = psb2("vn", [128, 4, 128], BF16)
                    O1 = psb2("O1", [128, 4, 128], BF16)
                    kdec = psb2("kdec", [128, 4, 128], BF16)
                    for dr in range(2):
                        OP("dve", lambda e, dr=dr: e.memset(Sst[dr][:], 0.0), w=[F("S", dr)])
                        OP("dve", lambda e, dr=dr: e.memset(Sbf[dr][:], 0.0), w=[F("Sbf", dr)])

                    def bc4(ap3):
                        return ap3.unsqueeze(2).broadcast_to([128, 4, 128])

                    def c4(ci, bf=False):
                        a = cb(ci) if bf else cf(ci)
                        return a.unsqueeze(1).broadcast_to([128, 4, 128])

                    def flat(t):
                        return t[:].rearrange("p h j -> p (h j)")

                    for step in range(NT):
                        for dr in range(2):
                            n = step if dr == 0 else NT - 1 - step
                            r0 = dr * 4
                            tok = slice(n * 128, (n + 1) * 128)
                            OP("dve", lambda e, dr=dr, n=n: e.tensor_tensor(out=GU[:], in0=c4(C_UF if dr == 0 else C_UB), in1=bc4(gtk[:, dr, n, :]), op=ALU.mult),
                               r=[F("gtk"), T("cst")], w=[F("GU")])
                            bG, tG = bank("pre", (0, 1, 2))
                            OP("pe", lambda e, bG=bG: e.matmul(bG[:, 0:512], lhsT=cf(C_NEG1), rhs=flat(GU), start=True, stop=False), r=[F("GU"), T("cst")], w=[tG])
                            for h in range(4):
                                OP("pe", lambda e, bG=bG, h=h: e.matmul(bG[:, h * 128:(h + 1) * 128], lhsT=GU[:, h, :], rhs=cf(C_ONES), start=False, stop=True), r=[F("GU"), T("cst")], w=[tG])
                            OP("dve", lambda e, bG=bG, dr=dr: e.tensor_tensor(out=Dm[:], in0=bG[:, 0:512].rearrange("p (h j) -> p h j", h=4), in1=c4(C_MASKF if dr == 0 else C_MASKB), op=ALU.add),
                               r=[tG, T("cst")], w=[F("Dm")])
                            OP("act", lambda e: e.activation(out=Dm[:], in_=Dm[:], func=AF.Exp), r=[F("Dm")], w=[F("Dm")])
                            bKK, tKK = bank("pre", (0, 1, 2))
                            bQK, tQK = bank("pre", (0, 1, 2))
                            for h in range(4):
                                OP("pe", lambda e, h=h, bKK=bKK, tok=tok: e.matmul(bKK[:, h * 128:(h + 1) * 128], lhsT=kT[:, h, tok], rhs=kT[:, h, tok], start=True, stop=True), r=[F("kT", h)], w=[tKK])
                                OP("pe", lambda e, h=h, bQK=bQK, tok=tok: e.matmul(bQK[:, h * 128:(h + 1) * 128], lhsT=qT[:, h, tok], rhs=kT[:, h, tok], start=True, stop=True), r=[F("kT", h), F("qT", h)], w=[tQK])
                            OP("dve", lambda e, dr=dr: e.tensor_tensor(out=Dsn[:], in0=Dm[:], in1=c4(C_STRF if dr == 0 else C_STRB), op=ALU.mult), r=[F("Dm"), T("cst")], w=[F("Dsn")])
                            OP("dve", lambda e, n=n, r0=r0: e.tensor_tensor(out=Dsn[:], in0=Dsn[:], in1=bc4(nbt[:, n, r0:r0 + 4]), op=ALU.mult), r=[F("Dsn"), SC], w=[F("Dsn")])
                            OP("dve", lambda e, bKK=bKK: e.tensor_tensor(out=P0[0][:], in0=bKK[:, 0:512].rearrange("p (h j) -> p h j", h=4), in1=Dsn[:], op=ALU.mult), r=[tKK, F("Dsn")], w=[F("P", 0)])
                            OP("dve", lambda e, bQK=bQK: e.tensor_tensor(out=Abf[:], in0=bQK[:, 0:512].rearrange("p (h j) -> p h j", h=4), in1=Dm[:], op=ALU.mult), r=[tQK, F("Dm")], w=[F("Abf")])
                            for h in range(4):
                                OP("pe", lambda e, h=h: e.transpose(out=pTb[:, h * 128:(h + 1) * 128], in_=P0[0][:, h, :], identity=cb(C_IDENT)), r=[F("P", 0), T("cbf")], w=[T("pTb", 0)])
                            for h in range(4):
                                OP("pe", lambda e, h=h: e.transpose(out=pTb[:, 512 + h * 128:512 + (h + 1) * 128], in_=Abf[:, h, :], identity=cb(C_IDENT)), r=[F("Abf"), T("cbf")], w=[T("pTb", 0)])
                            OP("dve", lambda e: e.tensor_scalar_mul(out=flat(M0[0]), in0=pTb[:, 0:512], scalar1=1.0), r=[T("pTb", 0)], w=[F("M", 0)])
                            OP("dve", lambda e: e.tensor_scalar_mul(out=flat(ATb), in0=pTb[:, 512:1024], scalar1=1.0), r=[T("pTb", 0)], w=[F("AT")])
                            OP("dve", lambda e: e.tensor_tensor(out=Y0[0][:], in0=M0[0][:], in1=c4(C_IDENT, True), op=ALU.add), r=[F("M", 0), T("cbf")], w=[F("Y", 0)])
                            cur = 0
                            for lvl in range(7):
                                nxt = 1 - cur
                                Pc, Mc, Yc = P0[cur], M0[cur], Y0[cur]
                                Pn, Mn, Yn = P0[nxt], M0[nxt], Y0[nxt]
                                tP, tM, tY = F("P", cur), F("M", cur), F("Y", cur)
                                if lvl <= 4:
                                    bM, tbM = bank("pre", (0, 1, 2))
                                    for h in range(4):
                                        OP("pe", lambda e, h=h, bM=bM, Pc=Pc, Mc=Mc: e.matmul(bM[:, h * 128:(h + 1) * 128], lhsT=Pc[:, h, :], rhs=Mc[:, h, :], start=True, stop=True), r=[tP, tM], w=[tbM])
                                if lvl <= 5:
                                    bP, tbP = bank("pre", (0, 1, 2))
                                    for h in range(4):
                                        OP("pe", lambda e, h=h, bP=bP, Pc=Pc, Mc=Mc: e.matmul(bP[:, h * 128:(h + 1) * 128], lhsT=Mc[:, h, :], rhs=Pc[:, h, :], start=True, stop=True), r=[tP, tM], w=[tbP])
                                if lvl >= 1:
                                    bY, tbY = bank("pre", (0, 1, 2))
                                    for h in range(4):
                                        OP("pe", lambda e, h=h, bY=bY, Pc=Pc, Yc=Yc: e.matmul(bY[:, h * 128:(h + 1) * 128], lhsT=Pc[:, h, :], rhs=Yc[:, h, :], start=True, stop=True), r=[tP, tY], w=[tbY])
                                if lvl <= 4:
                                    OP("act", lambda e, bM=bM, Mn=Mn: e.activation(out=flat(Mn), in_=bM[:, 0:512], func=AF.Copy), r=[tbM], w=[F("M", nxt)])
                                if lvl <= 5:
                                    OP("act", lambda e, bP=bP, Pn=Pn: e.activation(out=flat(Pn), in_=bP[:, 0:512], func=AF.Copy), r=[tbP], w=[F("P", nxt)])
                                if lvl >= 1:
                                    OP("dve", lambda e, bY=bY, Yn=Yn, Yc=Yc: e.tensor_tensor(out=flat(Yn), in0=bY[:, 0:512], in1=flat(Yc), op=ALU.add), r=[tbY, tY], w=[F("Y", nxt)])
                                else:
                                    OP("act", lambda e, Yn=Yn, Yc=Yc: e.activation(out=Yn[:], in_=Yc[:], func=AF.Copy), r=[tY], w=[F("Y", nxt)])
                                cur = nxt
                            TTb = Y0[cur]
                            tTT = F("Y", cur)
                            OP("dve", lambda e, n=n, r0=r0: e.tensor_tensor(out=VBt[:], in0=vtok[:, n], in1=bc4(btok[:, n, r0:r0 + 4]), op=ALU.mult), r=[F("vtok"), F("btok")], w=[F("VBt")])
                            for h in range(4):
                                OP("pe", lambda e, h=h, tok=tok: e.transpose(out=pTb[:, h * 128:(h + 1) * 128], in_=kT[:, h, tok], identity=cb(C_IDENT)), r=[F("kT", h), T("cbf")], w=[T("pTb", 0)])
                            OP("dve", lambda e, n=n, r0=r0: e.tensor_tensor(out=kdec[:], in0=pTb[:, 0:512].rearrange("p (h j) -> p h j", h=4), in1=bc4(ekd[:, n, r0:r0 + 4]), op=ALU.mult), r=[T("pTb", 0), SC], w=[F("kdec")])
                            bKS, tKS = bank("scan", (3, 4, 5, 6))
                            bW1, tW1 = bank("scan", (3, 4, 5, 6))
                            for h in range(4):
                                OP("pe", lambda e, h=h, bKS=bKS, tok=tok, dr=dr: e.matmul(bKS[:, h * 128:(h + 1) * 128], lhsT=kT[:, h, tok], rhs=Sbf[dr][:, h, :], start=True, stop=True), r=[F("kT", h), F("Sbf", dr)], w=[tKS])
                            for h in range(4):
                                OP("pe", lambda e, h=h, bW1=bW1, tok=tok, dr=dr: e.matmul(bW1[:, h * 128:(h + 1) * 128], lhsT=qT[:, h, tok], rhs=Sbf[dr][:, h, :], start=True, stop=True), r=[F("qT", h), F("Sbf", dr)], w=[tW1])
                            OP("dve", lambda e, bKS=bKS, n=n, r0=r0: e.tensor_tensor(out=R1[:], in0=bKS[:, 0:512].rearrange("p (h j) -> p h j", h=4), in1=bc4(nbe[:, n, r0:r0 + 4]), op=ALU.mult), r=[tKS, SC], w=[F("R1")])
                            OP("dve", lambda e: e.tensor_tensor(out=Rb[:], in0=R1[:], in1=VBt[:], op=ALU.add), r=[F("R1"), F("VBt")], w=[F("Rb")])
                            bV, tV = bank("scan", (3, 4, 5, 6))
                            for h in range(4):
                                OP("pe", lambda e, h=h, bV=bV, TTb=TTb: e.matmul(bV[:, h * 128:(h + 1) * 128], lhsT=TTb[:, h, :], rhs=Rb[:, h, :], start=True, stop=True), r=[tTT, F("Rb")], w=[tV])
                            OP("act", lambda e, bV=bV: e.activation(out=flat(vn), in_=bV[:, 0:512], func=AF.Copy), r=[tV], w=[F("vn")])
                            OP("dve", lambda e, bW1=bW1, n=n, r0=r0: e.tensor_tensor(out=O1[:], in0=bW1[:, 0:512].rearrange("p (h j) -> p h j", h=4), in1=bc4(egc[:, n, r0:r0 + 4]), op=ALU.mult), r=[tW1, SC], w=[F("O1")])
                            bO, tO = bank("scan", (3, 4, 5, 6))
                            for h in range(4):
                                OP("pe", lambda e, h=h, bO=bO: e.matmul(bO[:, h * 128:(h + 1) * 128], lhsT=O1[:, h, :], rhs=cb(C_IDENT), start=True, stop=False), r=[F("O1"), T("cbf")], w=[tO])
                                OP("pe", lambda e, h=h, bO=bO: e.matmul(bO[:, h * 128:(h + 1) * 128], lhsT=vn[:, h, :], rhs=ATb[:, h, :], start=False, stop=True), r=[F("vn"), F("AT")], w=[tO])
                            first = (step < NT // 2)
                            if first:
                                OP("act", lambda e, bO=bO, tok=tok: e.activation(out=oacc[:, :, tok], in_=bO[:, 0:512].rearrange("p (h j) -> p h j", h=4), func=AF.Copy), r=[tO], w=[F("oacc", n)])
                            else:
                                OP("dve", lambda e, bO=bO, tok=tok: e.tensor_tensor(out=oacc[:, :, tok], in0=bO[:, 0:512].rearrange("p (h j) -> p h j", h=4), in1=oacc[:, :, tok], op=ALU.add), r=[tO, F("oacc", n)], w=[F("oacc", n)])
                            bS, tS = bank("scan", (3, 4, 5, 6))
                            for h in range(4):
                                OP("pe", lambda e, h=h, bS=bS: e.matmul(bS[:, h * 128:(h + 1) * 128], lhsT=kdec[:, h, :], rhs=vn[:, h, :], start=True, stop=True), r=[F("kdec"), F("vn")], w=[tS])
                            OP("dve", lambda e, dr=dr, n=n, r0=r0: e.tensor_tensor(out=Sst[dr][:], in0=Sst[dr][:], in1=bc4(egt[:, n, r0:r0 + 4]), op=ALU.mult), r=[F("S", dr), SC], w=[F("S", dr)])
                            OP("dve", lambda e, dr=dr, bS=bS: e.tensor_tensor(out=flat(Sst[dr]), in0=bS[:, 0:512], in1=flat(Sst[dr]), op=ALU.add), r=[tS, F("S", dr)], w=[F("S", dr)])
                            OP("act", lambda e, dr=dr: e.activation(out=Sbf[dr][:], in_=Sst[dr][:], func=AF.Copy), r=[F("S", dr)], w=[F("Sbf", dr)])
                sch.barrier(scr[:, 2:3])
                _chk("loop")
                FR.clear()
                if DEBUG and s_ == 0:
                    OP("sp", lambda e: e.dma_start(out=dbg_d[:, 5], in_=qT[:]), r=[F("qT", 0)], dma="dbg")
                    OP("sp", lambda e: e.dma_start(out=dbg_d[:, 6], in_=kT[:]), r=[F("kT", 0)], dma="dbg")
                    OP("sp", lambda e: e.dma_start(out=dbgf_d, in_=oacc[:]), r=[F("oacc", 0)], dma="dbg")
                with ExitStack() as ph2:
                    def psb2(name, shape, dt=F32, ph2=ph2):
                        return ph2.enter_context(nc.sbuf_tensor(name + "_s%d" % s_, shape, dt))
                    zs = psb2("zs", [128, 4, S], BF16)
                    sqb = psb2("sqb2", [128, 512], BF16)
                    rn = psb2("rn2", [128, 512])
                    yt = psb2("yt", [128, 512])
                    for zt in range(2):
                        wb, wtr = wnext(("w_in", 1536 + zt * 256))

                        def ev(ci, tb, bk, btr, zt=zt):
                            OP("act", lambda e: e.activation(out=zs[:, zt * 2 + ci, tb * 512:(tb + 1) * 512], in_=bk[:, 0:512], func=AF.Silu), r=[btr], w=[F("zs", zt * 2 + ci)])
                        proj_fm(wb, wtr, 8, [(0, 128), (128, 128)], lambda kc, tb: hT[:, kc, tb * 512:(tb + 1) * 512], lambda kc, tb: T("hT", tb), ev)
                    oall = [F("oacc", n) for n in range(NT)]
                    for h in range(4):
                        for tb in range(4):
                            sl = slice(tb * 512, (tb + 1) * 512)
                            sumsq_rn(oacc[:, h, sl], oall, 1.0 / 128.0, cb(C_ONES), rn[:], F("rn"), sqb[:], F("sqb"), ("aux", (4, 5)))
                            OP("dve", lambda e, h=h, sl=sl, rn=rn: e.tensor_tensor(out=yt[:], in0=oacc[:, h, sl], in1=rn[:], op=ALU.mult), r=oall + [F("rn")], w=[F("yt")])
                            OP("dve", lambda e, h=h, sl=sl: e.scalar_tensor_tensor(out=ydn[:, h, sl], in0=yt[:], scalar=vec[:, V_DNW:V_DNW + 1], in1=zs[:, h, sl], op0=ALU.mult, op1=ALU.mult),
                               r=[F("yt"), F("zs", h), T("vec")], w=[F("ydn")])
                    if DEBUG and s_ == 0:
                        OP("sp", lambda e: e.dma_start(out=dbg_d[:, 0], in_=ydn[:]), r=[F("ydn")], dma="dbg")
            sch.barrier(scr[:, 0:1])
            TYS = sch.fresh()
            mrg = seqscope.enter_context(nc.sbuf_tensor("mrg_s%d" % s_, [128, 8, S], BF16))
            for dc_ in range(8):
                for tb_ in range(4):
                    TK[("mrg", dc_, tb_)] = sch.fresh()
            if True:

                def merge(nb, ysT, ystr, first):
                    with ExitStack() as ph3:
                        sg = ph3.enter_context(nc.sbuf_tensor("sg_%d_%d" % (s_, nb), [128, 512], F32))
                        ct = ph3.enter_context(nc.sbuf_tensor("ct_%d_%d" % (s_, nb), [128, 512], BF16))
                        tsg, tct = sch.fresh(), sch.fresh()
                        for dt in range(4):
                            wg, wgtr = wnext(("w_in", 3344 + nb * 1024 + dt * 256))
                            wbr, wbtr = wnext(("w_br", dt * 256))
                            for cc in range(2):
                                dc = dt * 2 + cc
                                for tb in range(4):
                                    sl = slice(tb * 512, (tb + 1) * 512)
                                    bA, tA = bank("proj", (0, 1, 2, 3))
                                    for kc in range(8):
                                        OP("pe", lambda e, bA=bA, kc=kc, cc=cc, sl=sl, wg=wg: e.matmul(bA[:, 0:512], lhsT=wg[:, kc, cc * 128:(cc + 1) * 128], rhs=hT[:, kc, sl], start=(kc == 0), stop=(kc == 7)), r=[wgtr, T("hT", tb)], w=[tA])
                                    bB, tB = bank("proj", (0, 1, 2, 3))
                                    for kc in range(4):
                                        OP("pe", lambda e, bB=bB, kc=kc, cc=cc, sl=sl, wbr=wbr: e.matmul(bB[:, 0:512], lhsT=wbr[:, kc, cc * 128:(cc + 1) * 128], rhs=ysT[:, kc, sl], start=(kc == 0), stop=(kc == 3)), r=[wbtr, ystr], w=[tB])
                                    OP("act", lambda e, bA=bA: e.activation(out=sg[:], in_=bA[:, 0:512], func=AF.Sigmoid), r=[tA], w=[tsg])
                                    if first:
                                        OP("dve", lambda e, bB=bB, dc=dc, sl=sl: e.tensor_tensor(out=mrg[:, dc, sl], in0=bB[:, 0:512], in1=sg[:], op=ALU.mult), r=[tB, tsg], w=[T("mrg", dc, tb)])
                                    else:
                                        OP("dve", lambda e, bB=bB: e.tensor_tensor(out=ct[:], in0=bB[:, 0:512], in1=sg[:], op=ALU.mult), r=[tB, tsg], w=[tct])
                                        OP("dve", lambda e, dc=dc, sl=sl: e.tensor_tensor(out=mrg[:, dc, sl], in0=mrg[:, dc, sl], in1=ct[:], op=ALU.add), r=[tct, T("mrg", dc, tb)], w=[T("mrg", dc, tb)])
                _chk("dn")
                merge(0, ysb, TYS, True)
            sch.barrier(scr[:, 3:4])
            _chk("m0")

            with ExitStack() as ph:
                def psb(name, shape, dt=F32, ph=ph):
                    return ph.enter_context(nc.sbuf_tensor(name + "_s%d" % s_, shape, dt))
                FR = {}

                def F(*key):
                    if key not in FR:
                        FR[key] = sch.fresh()
                    return FR[key]
                AQ = psb("AQ", [128, 4, S], BF16)
                AK = psb("AK", [128, 2, S], BF16)
                VX = psb("VX", [128, 4, NT, 128], BF16)
                ropes = psb("ropes", [128, 2, S])
                yat = ysb
                sqb = psb("sqb3", [128, 512], BF16)
                rn = psb("rn3", [128, 512])
                aqn = psb("aqn", [128, 512], BF16)
                r1 = psb("r1", [128, 512])
                r2 = psb("r2", [128, 512])
                PT = [psb("PT%d" % i, [128, 512], BF16) for i in range(4)]
                UA = psb("UA", [128, 512])
                UB = psb("UB", [128, 512])
                rd = psb("rd", [128, 512])
                OP("sp", lambda e: e.dma_start(out=ropes[:], in_=rope_d), w=[F("ropes")], dma="misc")
                OP("dve", lambda e: e.memset(VX[:], 1.0), w=[F("VX")])

                def qk_evac(dst_ap_fn, dst_trk, nwcol):
                    def ev(ci, tb, bk, btr):
                        sl = slice(tb * 512, (tb + 1) * 512)
                        sumsq_rn(bk[:, 0:512], [btr], 1.0 / 64.0, cb(C_BD64), rn[:], F("rn"), sqb[:], F("sqb"), ("aux", (4, 5)))
                        OP("dve", lambda e, rn=rn: e.scalar_tensor_tensor(out=aqn[:], in0=bk[:, 0:512], scalar=vec[:, nwcol:nwcol + 1], in1=rn[:], op0=ALU.mult, op1=ALU.mult),
                           r=[btr, F("rn"), T("vec")], w=[F("aqn")])
                        bR, tR = bank("aux", (4, 5))
                        OP("pe", lambda e: e.matmul(bR[:, 0:512], lhsT=cb(C_PERM), rhs=aqn[:], start=True, stop=True), r=[F("aqn"), T("cbf")], w=[tR])
                        OP("dve", lambda e: e.tensor_tensor(out=r1[:], in0=aqn[:], in1=ropes[:, 0, sl], op=ALU.mult), r=[F("aqn"), F("ropes")], w=[F("r1")])
                        OP("dve", lambda e: e.tensor_tensor(out=r2[:], in0=bR[:, 0:512], in1=ropes[:, 1, sl], op=ALU.mult), r=[tR, F("ropes")], w=[F("r2")])
                        OP("dve", lambda e: e.tensor_tensor(out=dst_ap_fn(ci, sl), in0=r1[:], in1=r2[:], op=ALU.add), r=[F("r1"), F("r2")], w=[dst_trk])
                    return ev
                for qt in range(2):
                    wb, wtr = wnext(("w_in", 2064 + qt * 256))
                    proj_fm(wb, wtr, 8, [(0, 128), (128, 128)], lambda kc, tb: hT[:, kc, tb * 512:(tb + 1) * 512], lambda kc, tb: T("hT", tb),
                            qk_evac(lambda ci, sl, qt=qt: AQ[:, qt * 2 + ci, sl], F("AQ"), V_QNW))
                wb, wtr = wnext(("kdup", 2576))
                proj_fm(wb, wtr, 8, [(0, 128), (128, 128)], lambda kc, tb: hT[:, kc, tb * 512:(tb + 1) * 512], lambda kc, tb: T("hT", tb),
                        qk_evac(lambda ci, sl: AK[:, ci, sl], F("AK"), V_KNW))
                wb, wtr = wnext(("w_in", 2704))
                for n0 in range(0, NT, 4):
                    bk, btr = bank("proj", (0, 1, 2, 3))
                    for j in range(4):
                        n = n0 + j
                        for kc in range(8):
                            OP("pe", lambda e, bk=bk, j=j, n=n, kc=kc, wb=wb: e.matmul(bk[:, j * 128:(j + 1) * 128], lhsT=hT[:, kc, n * 128:(n + 1) * 128], rhs=wb[:, kc, 0:128], start=(kc == 0), stop=(kc == 7)),
                               r=[wtr, T("hT", n // 4)], w=[btr])
                    for g in range(2):
                        src = lambda bk=bk, g=g: bk[:, 0:512].rearrange("p (a b) -> p a b", a=4)[:, :, g * 64:(g + 1) * 64]
                        OP("act", lambda e, g=g, n0=n0, src=src: e.activation(out=VX[:, g * 2 + 0, n0:n0 + 4, 0:64], in_=src(), func=AF.Copy), r=[btr], w=[F("VX")])
                        OP("dve", lambda e, g=g, n0=n0, src=src: e.tensor_copy(out=VX[:, g * 2 + 1, n0:n0 + 4, 64:128], in_=src()), r=[btr], w=[F("VX")])
                for c in range(4):
                    g = c // 2
                    for qb in range(4):
                        sl = slice(qb * 512, (qb + 1) * 512)
                        bOA, tOA = banks[4], T("bank", 4)
                        bOB, tOB = banks[5], T("bank", 5)
                        for n in range(NT):
                            tok = slice(n * 128, (n + 1) * 128)
                            b1, t1_ = bank("sc", (0, 1, 2, 3))
                            b2, t2_ = bank("sc", (0, 1, 2, 3))
                            OP("pe", lambda e, b1=b1, g=g, tok=tok, c=c, sl=sl: e.matmul(b1[:, 0:512], lhsT=AK[0:64, g, tok], rhs=AQ[0:64, c, sl], start=True, stop=True), r=[F("AK"), F("AQ")], w=[t1_])
                            OP("pe", lambda e, b2=b2, g=g, tok=tok, c=c, sl=sl: e.matmul(b2[:, 0:512], lhsT=AK[64:128, g, tok], rhs=AQ[64:128, c, sl], start=True, stop=True), r=[F("AK"), F("AQ")], w=[t2_])
                            p1, p2 = PT[(n % 2) * 2], PT[(n % 2) * 2 + 1]
                            tp1, tp2 = F("PT", (n % 2) * 2), F("PT", (n % 2) * 2 + 1)
                            OP("act", lambda e, b1=b1, p1=p1: e.activation(out=p1[:], in_=b1[:, 0:512], func=AF.Exp, scale=0.125), r=[t1_], w=[tp1])
                            OP("act", lambda e, b2=b2, p2=p2: e.activation(out=p2[:], in_=b2[:, 0:512], func=AF.Exp, scale=0.125), r=[t2_], w=[tp2])
                            OP("pe", lambda e, g=g, n=n, p1=p1: e.matmul(bOA[:, 0:512], lhsT=VX[:, g * 2 + 0, n, :], rhs=p1[:], start=(n == 0), stop=(n == NT - 1)), r=[F("VX"), tp1], w=[tOA])
                            OP("pe", lambda e, g=g, n=n, p2=p2: e.matmul(bOB[:, 0:512], lhsT=VX[:, g * 2 + 1, n, :], rhs=p2[:], start=(n == 0), stop=(n == NT - 1)), r=[F("VX"), tp2], w=[tOB])
                        OP("act", lambda e: e.activation(out=UA[:], in_=bOA[:, 0:512], func=AF.Copy), r=[tOA], w=[F("UA")])
                        OP("dve", lambda e: e.tensor_copy(out=UB[:], in_=bOB[:, 0:512]), r=[tOB], w=[F("UB")])
                        bD, tD = banks[6], T("bank", 6)
                        OP("pe", lambda e: e.matmul(bD[:, 0:512], lhsT=cf(C_SWLO), rhs=UA[:], start=True, stop=False), r=[F("UA"), T("cst")], w=[tD])
                        OP("pe", lambda e: e.matmul(bD[:, 0:512], lhsT=cf(C_SWHI), rhs=UB[:], start=False, stop=True), r=[F("UB"), T("cst")], w=[tD])
                        OP("dve", lambda e, rd=rd: e.reciprocal(out=rd[:], in_=bD[:, 0:512]), r=[tD], w=[F("rd")])
                        OP("dve", lambda e, c=c, sl=sl, rd=rd: e.tensor_tensor(out=yat[0:64, c, sl], in0=UA[0:64, :], in1=rd[0:64, :], op=ALU.mult), r=[F("UA"), F("rd")], w=[F("yat")])
                        OP("dve", lambda e, c=c, sl=sl, rd=rd: e.tensor_tensor(out=yat[64:128, c, sl], in0=UB[64:128, :], in1=rd[64:128, :], op=ALU.mult), r=[F("UB"), F("rd")], w=[F("yat")])
                if DEBUG and s_ == 0:
                    OP("sp", lambda e: e.dma_start(out=dbg_d[:, 1], in_=yat[:]), r=[F("yat")], dma="dbg")
                    OP("sp", lambda e: e.dma_start(out=dbg_d[:, 3], in_=AQ[:]), r=[F("AQ")], dma="dbg")
                    OP("sp", lambda e: e.dma_start(out=dbg_d[:, 4, 0:2], in_=AK[:]), r=[F("AK")], dma="dbg")
                _chk("gqa")
                merge(1, yat, F("yat"), False)
            sch.barrier(scr[:, 0:1])

            with ExitStack() as ph:
                def psb(name, shape, dt=F32, ph=ph):
                    return ph.enter_context(nc.sbuf_tensor(name + "_s%d" % s_, shape, dt))
                FR = {}

                def F(*key):
                    if key not in FR:
                        FR[key] = sch.fresh()
                    return FR[key]
                memT = psb("memT", [128, 8, 256], BF16)
                mkT = psb("mkT", [128, 4, 256], BF16)
                mv = psb("mv", [128, 2, 512], BF16)
                xqT = psb("xqT", [128, 4, S], BF16)
                yx = ysb
                PTx = [psb("PTx%d" % i, [128, 512], BF16) for i in range(2)]
                rd = psb("rdx", [128, 512])
                xs = [psb("xsM%d" % i, [128, D]) for i in range(2)]
                rmsnorm_tokmajor(lambda t: mem_d[s_, t * 128:(t + 1) * 128, :], 2, memT, lambda t: F("memT"), V_NW_MEM, "m", psb, xs, F)
                for kt in range(2):
                    wb, wtr = wnext(("w_kv", kt * 256))
                    for ci in range(2):
                        hh = kt * 2 + ci
                        bk, btr = bank("proj", (0, 1, 2, 3))
                        for kc in range(8):
                            OP("pe", lambda e, bk=bk, kc=kc, ci=ci, wb=wb: e.matmul(bk[:, 0:256], lhsT=wb[:, kc, ci * 128:(ci + 1) * 128], rhs=memT[:, kc, :], start=(kc == 0), stop=(kc == 7)), r=[wtr, F("memT")], w=[btr])
                        OP("act", lambda e, bk=bk, hh=hh: e.activation(out=mkT[:, hh, :], in_=bk[:, 0:256], func=AF.Copy), r=[btr], w=[F("mkT")])
                for vt in range(2):
                    wb, wtr = wnext(("w_kv", 512 + vt * 256))
                    for mt in range(2):
                        bk, btr = bank("proj", (0, 1, 2, 3))
                        for kc in range(8):
                            OP("pe", lambda e, bk=bk, kc=kc, mt=mt, wb=wb: e.matmul(bk[:, 0:256], lhsT=memT[:, kc, mt * 128:(mt + 1) * 128], rhs=wb[:, kc, 0:256], start=(kc == 0), stop=(kc == 7)), r=[wtr, F("memT")], w=[btr])
                        OP("act", lambda e, bk=bk, mt=mt, vt=vt: e.activation(out=mv[:, mt, vt * 256:(vt + 1) * 256], in_=bk[:, 0:256], func=AF.Copy), r=[btr], w=[F("mv")])
                for qt in range(2):
                    wb, wtr = wnext(("w_in", 2832 + qt * 256))

                    def ev(ci, tb, bk, btr, qt=qt):
                        OP("act", lambda e: e.activation(out=xqT[:, qt * 2 + ci, tb * 512:(tb + 1) * 512], in_=bk[:, 0:512], func=AF.Copy), r=[btr], w=[F("xqT")])
                    proj_fm(wb, wtr, 8, [(0, 128), (128, 128)], lambda kc, tb: hT[:, kc, tb * 512:(tb + 1) * 512], lambda kc, tb: T("hT", tb), ev)
                for h in range(4):
                    for qb in range(4):
                        sl = slice(qb * 512, (qb + 1) * 512)
                        bO, tO = banks[4], T("bank", 4)
                        bDn, tDn = banks[5], T("bank", 5)
                        for mt in range(2):
                            b1, t1_ = bank("sc", (0, 1, 2, 3))
                            OP("pe", lambda e, b1=b1, h=h, mt=mt, sl=sl: e.matmul(b1[:, 0:512], lhsT=mkT[:, h, mt * 128:(mt + 1) * 128], rhs=xqT[:, h, sl], start=True, stop=True), r=[F("mkT"), F("xqT")], w=[t1_])
                            OP("act", lambda e, b1=b1, mt=mt: e.activation(out=PTx[mt][:], in_=b1[:, 0:512], func=AF.Exp, scale=128.0 ** -0.5), r=[t1_], w=[F("PTx", mt)])
                            OP("pe", lambda e, h=h, mt=mt, bO=bO: e.matmul(bO[:, 0:512], lhsT=mv[:, mt, h * 128:(h + 1) * 128], rhs=PTx[mt][:], start=(mt == 0), stop=(mt == 1)), r=[F("mv"), F("PTx", mt)], w=[tO])
                            OP("pe", lambda e, mt=mt: e.matmul(bDn[:, 0:512], lhsT=cb(C_ONES), rhs=PTx[mt][:], start=(mt == 0), stop=(mt == 1)), r=[T("cbf"), F("PTx", mt)], w=[tDn])
                        OP("dve", lambda e, rd=rd: e.reciprocal(out=rd[:], in_=bDn[:, 0:512]), r=[tDn], w=[F("rd")])
                        OP("dve", lambda e, h=h, sl=sl, bO=bO, rd=rd: e.tensor_tensor(out=yx[:, h, sl], in0=bO[:, 0:512], in1=rd[:], op=ALU.mult), r=[tO, F("rd")], w=[F("yx")])
                if DEBUG and s_ == 0:
                    OP("sp", lambda e: e.dma_start(out=dbg_d[:, 2], in_=yx[:]), r=[F("yx")], dma="dbg")
                _chk("xat")
                merge(2, yx, F("yx"), False)
            sch.barrier(scr[:, 1:2])

            with ExitStack() as ph:
                def psb(name, shape, dt=F32, ph=ph):
                    return ph.enter_context(nc.sbuf_tensor(name + "_s%d" % s_, shape, dt))
                FR = {}

                def F(*key):
                    if key not in FR:
                        FR[key] = sch.fresh()
                    return FR[key]
                xT = psb("xT", [128, 8, 1024])
                xs = [psb("xsC%d" % i, [128, D]) for i in range(2)]
                hfT = hT[:, :, 0:1024]
                uT = hT[:, :, 1024:2048]
                sq8 = psb("sq8", [128, 8, 512], BF16)
                rn = psb("rn4", [128, 512])
                rl = [psb("rl%d" % i, [128, 512]) for i in range(2)]
                yo = psb("yo", [128, 8, 128])
                ot = [psb("ot%d" % i, [128, D]) for i in range(2)]
                for hs in range(2):
                    for t8 in range(8):
                        t = hs * 8 + t8
                        xt = xs[t % 2]
                        xtr = F("xs", t % 2)
                        OP("sp", lambda e, xt=xt, t=t: e.dma_start(out=xt[:], in_=x_d[s_, t * 128:(t + 1) * 128, :]), w=[xtr], dma="x%d" % (t % 2))
                        for half in range(2):
                            bk, btr = bank("aux", (4, 5, 6))
                            for j in range(4):
                                c = half * 4 + j
                                OP("pe", lambda e, bk=bk, j=j, c=c, xt=xt: e.transpose(out=bk[:, j * 128:(j + 1) * 128], in_=xt[:, c * 128:(c + 1) * 128], identity=cf(C_IDENT)), r=[xtr, T("cst")], w=[btr])
                            OP("act" if half == 0 else "dve", lambda e, bk=bk, half=half, t8=t8: (e.activation(out=xT[:, half * 4:half * 4 + 4, t8 * 128:(t8 + 1) * 128], in_=bk[:, 0:512].rearrange("p (a b) -> p a b", a=4), func=AF.Copy) if half == 0 else e.tensor_copy(out=xT[:, half * 4:half * 4 + 4, t8 * 128:(t8 + 1) * 128], in_=bk[:, 0:512].rearrange("p (a b) -> p a b", a=4))),
                               r=[btr], w=[F("xT", hs, t8 // 4)])
                    for dt in range(4):
                        wb, wtr = wnext(("w_out", dt * 256))
                        for cc in range(2):
                            dc = dt * 2 + cc
                            for tbh in range(2):
                                tb = hs * 2 + tbh
                                bk, btr = bank("proj", (0, 1, 2, 3))
                                for kc in range(8):
                                    OP("pe", lambda e, bk=bk, kc=kc, cc=cc, tb=tb, wb=wb: e.matmul(bk[:, 0:512], lhsT=wb[:, kc, cc * 128:(cc + 1) * 128], rhs=mrg[:, kc, tb * 512:(tb + 1) * 512], start=(kc == 0), stop=(kc == 7)), r=[wtr, T("mrg", kc, tb)], w=[btr])
                                OP("dve", lambda e, bk=bk, dc=dc, tbh=tbh: e.tensor_tensor(out=xT[:, dc, tbh * 512:(tbh + 1) * 512], in0=bk[:, 0:512], in1=xT[:, dc, tbh * 512:(tbh + 1) * 512], op=ALU.add), r=[btr, F("xT", hs, tbh)], w=[F("xT", hs, tbh)])

                    def fm_norm(tbh, nwoff, dst_fn, dst_trk, eng2):
                        sl = slice(tbh * 512, (tbh + 1) * 512)
                        OP("act", lambda e: e.activation(out=sq8[:], in_=xT[:, :, sl], func=AF.Square), r=[F("xT", hs, tbh)], w=[F("sq8")])
                        bk, btr = bank("aux", (4, 5, 6))
                        for c in range(8):
                            OP("pe", lambda e, bk=bk, c=c: e.matmul(bk[:, 0:512], lhsT=cb(C_ONES), rhs=sq8[:, c, :], start=(c == 0), stop=(c == 7)), r=[F("sq8"), T("cbf")], w=[btr])
                        OP("act", lambda e, bk=bk: e.activation(out=rtmp[:], in_=bk[:, 0:512], func=AF.Sqrt, bias=EPS, scale=1.0 / 1024.0), r=[btr], w=[T("rtmp")])
                        OP("dve", lambda e, rn=rn: e.reciprocal(out=rn[:], in_=rtmp[:]), r=[T("rtmp")], w=[F("rn")])
                        for c in range(8):
                            OP("dve", lambda e, c=c, rn=rn: e.scalar_tensor_tensor(out=dst_fn(c, sl), in0=xT[:, c, sl], scalar=vec[:, nwoff + c:nwoff + c + 1], in1=rn[:], op0=ALU.mult, op1=ALU.mult),
                               r=[F("xT", hs, tbh), F("rn"), T("vec")], w=[dst_trk])
                    for tbh in range(2):
                        fm_norm(tbh, V_NW_FFN, lambda c, sl: hfT[:, c, sl], F("hfT", tbh), "dve")
                    for fg in range(4):
                        for ut in range(4):
                            wb, wtr = wnext(("w_up", fg * 1024 + ut * 256))
                            for cc in range(2):
                                fc = ut * 2 + cc
                                for tbh in range(2):
                                    sl = slice(tbh * 512, (tbh + 1) * 512)
                                    bk, btr = bank("proj", (0, 1, 2, 3))
                                    for kc in range(8):
                                        OP("pe", lambda e, bk=bk, kc=kc, cc=cc, sl=sl, wb=wb: e.matmul(bk[:, 0:512], lhsT=wb[:, kc, cc * 128:(cc + 1) * 128], rhs=hfT[:, kc, sl], start=(kc == 0), stop=(kc == 7)), r=[wtr, F("hfT", tbh)], w=[btr])
                                    rr = rl[(fc * 2 + tbh) % 2]
                                    trr = F("rl", (fc * 2 + tbh) % 2)
                                    OP("act", lambda e, bk=bk, rr=rr: e.activation(out=rr[:], in_=bk[:, 0:512], func=AF.Relu), r=[btr], w=[trr])
                                    OP("dve", lambda e, rr=rr, fc=fc, sl=sl: e.tensor_tensor(out=uT[:, fc, sl], in0=rr[:], in1=rr[:], op=ALU.mult), r=[trr], w=[F("uT", fc, tbh)])
                        for dt in range(4):
                            wb, wtr = wnext(("w_dn", dt * 256))
                            for cc in range(2):
                                dc = dt * 2 + cc
                                for tbh in range(2):
                                    sl = slice(tbh * 512, (tbh + 1) * 512)
                                    bk, btr = bank("proj", (0, 1, 2, 3))
                                    for kc in range(8):
                                        OP("pe", lambda e, bk=bk, kc=kc, cc=cc, sl=sl, wb=wb: e.matmul(bk[:, 0:512], lhsT=wb[:, kc, cc * 128:(cc + 1) * 128], rhs=uT[:, kc, sl], start=(kc == 0), stop=(kc == 7)), r=[wtr, F("uT", kc, tbh)], w=[btr])
                                    OP("dve", lambda e, bk=bk, dc=dc, sl=sl: e.tensor_tensor(out=xT[:, dc, sl], in0=bk[:, 0:512], in1=xT[:, dc, sl], op=ALU.add), r=[btr, F("xT", hs, tbh)], w=[F("xT", hs, tbh)])
                    for tbh in range(2):
                        sl = slice(tbh * 512, (tbh + 1) * 512)
                        OP("act", lambda e, sl=sl: e.activation(out=sq8[:], in_=xT[:, :, sl], func=AF.Square), r=[F("xT", hs, tbh)], w=[F("sq8")])
                        bk, btr = bank("aux", (4, 5, 6))
                        for c in range(8):
                            OP("pe", lambda e, bk=bk, c=c: e.matmul(bk[:, 0:512], lhsT=cb(C_ONES), rhs=sq8[:, c, :], start=(c == 0), stop=(c == 7)), r=[F("sq8"), T("cbf")], w=[btr])
                        OP("act", lambda e, bk=bk: e.activation(out=rtmp[:], in_=bk[:, 0:512], func=AF.Sqrt, bias=EPS, scale=1.0 / 1024.0), r=[btr], w=[T("rtmp")])
                        OP("dve", lambda e, rn=rn: e.reciprocal(out=rn[:], in_=rtmp[:]), r=[T("rtmp")], w=[F("rn")])
                        for j in range(4):
                            t = hs * 8 + tbh * 4 + j
                            tsl = slice(tbh * 512 + j * 128, tbh * 512 + (j + 1) * 128)
                            for c in range(8):
                                OP("dve", lambda e, c=c, tsl=tsl, j=j, rn=rn: e.scalar_tensor_tensor(out=yo[:, c, :], in0=xT[:, c, tsl], scalar=vec[:, V_NW_FIN + c:V_NW_FIN + c + 1], in1=rn[:, j * 128:(j + 1) * 128], op0=ALU.mult, op1=ALU.mult),
                                   r=[F("xT", hs, tbh), F("rn"), T("vec")], w=[F("yo")])
                            o_t = ot[t % 2]
                            to_t = F("ot", t % 2)
                            for half in range(2):
                                bk2, btr2 = bank("aux", (4, 5, 6))
                                for jj in range(4):
                                    c = half * 4 + jj
                                    OP("pe", lambda e, bk2=bk2, jj=jj, c=c: e.transpose(out=bk2[:, jj * 128:(jj + 1) * 128], in_=yo[:, c, :], identity=cf(C_IDENT)), r=[F("yo"), T("cst")], w=[btr2])
                                OP("act", lambda e, bk2=bk2, half=half, o_t=o_t: e.activation(out=o_t[:, half * 512:(half + 1) * 512], in_=bk2[:, 0:512], func=AF.Copy), r=[btr2], w=[to_t])
                            OP("sp", lambda e, o_t=o_t, t=t: e.dma_start(out=out_d[s_, t * 128:(t + 1) * 128, :], in_=o_t[:]), r=[to_t], dma="out%d" % (t % 2))
            sch.barrier(scr[:, 2:3])
            for tb_ in range(4):
                TK[("hT", tb_)] = sch.fresh()
            seqscope.close()

        SEQSC = []
        try:
            body()
        except _Stop:
            for sc_ in reversed(SEQSC):
                sc_.close()
        assert STOP or wstate["use"] == len(wplan), (wstate, len(wplan))
        sch.finalize()
        print("kernel: ops", len(sch.ops), {e: sum(1 for o in sch.ops if o.eng == e) for e in Sched.ENGS}, flush=True)
        sch.emit(nc, final_waits=["out0", "out1", "dbg"])
    return nc


_CACHE = {}


def kernel(**inputs):
    inp = {k: np.asarray(v) for k, v in inputs.items()}
    if "nc" not in _CACHE:
        _CACHE["nc"] = build_program()
        _CACHE["consts"] = make_consts()
    nc = _CACHE["nc"]
    cst, rope = _CACHE["consts"]
    vec = make_vec(inp)
    shared = {
        "w_in": np.ascontiguousarray(inp["w_in"][0]),
        "w_mem_kv": np.ascontiguousarray(inp["w_mem_kv"][0]),
        "w_branch": np.ascontiguousarray(inp["w_branch"][0].reshape(1536, D)),
        "w_out": np.ascontiguousarray(inp["w_out"][0]),
        "w_up": np.ascontiguousarray(inp["w_up"][0]),
        "w_down": np.ascontiguousarray(inp["w_down"][0]),
        "cst": cst, "rope": rope, "vec": vec,
    }
    in_maps = []
    ncores = int(os.environ.get("KCORES", "8"))
    for c in range(ncores):
        m = dict(shared)
        m["x"] = np.ascontiguousarray(inp["x"][c * NSEQ:(c + 1) * NSEQ])
        m["mem"] = np.ascontiguousarray(inp["mem"][c * NSEQ:(c + 1) * NSEQ])
        in_maps.append(m)
    res = run_bass_kernel_spmd(nc, in_maps, core_ids=list(range(ncores)))
    _CACHE["last"] = res
    out = np.concatenate([np.asarray(r["out"]) for r in res.results], axis=0)
    if ncores < 8:
        out = np.concatenate([out, np.zeros((16 - out.shape[0], S, D), np.float32)], axis=0)
    return out.astype(np.float32)
```
